# Optimizing a Trainium2 kernel written in Bass

```python
import jax, jax.numpy as jnp
from jax import lax
import numpy as np

D_MODEL = 2048
BATCH = 8
SEQ = 4096
DEPTH = 4
DEC_BATCH = 4
DEC_SEQ = 8192
PAST_LEN = 128

N_HEADS = 16
N_KV_HEADS = 4
HEAD_DIM = 128
GQA_GROUP = N_HEADS // N_KV_HEADS
QKV_DIM = (N_HEADS + 2 * N_KV_HEADS) * HEAD_DIM
ATTN_DIM = N_HEADS * HEAD_DIM
ROPE_THETA = 10000.0
AXIS_DIM = HEAD_DIM // 2
AXIS_FREQS = AXIS_DIM // 2
GRID_W = 64
Q_BLOCK = 128
CONV_K = 31
FFN_CONV_K = 3
D_FF = 5632
EPS = 1e-6
N_MIXERS = 2
N_ATTN_LAYERS = (DEPTH + 1) // 2
N_CONV_LAYERS = DEPTH // 2

kernel_name = "hybrid_gqa_conformer_convffn_encoder"


def rms_norm(x, gain):
    xf = x.astype(jnp.float32)
    y = xf * lax.rsqrt(jnp.mean(xf * xf, axis=-1, keepdims=True) + EPS)
    return (y * gain.astype(jnp.float32)).astype(x.dtype)


def layer_norm(x, gain, bias):
    xf = x.astype(jnp.float32)
    mu = jnp.mean(xf, axis=-1, keepdims=True)
    var = jnp.mean(jnp.square(xf - mu), axis=-1, keepdims=True)
    y = (xf - mu) * lax.rsqrt(var + EPS)
    return (y * gain.astype(jnp.float32) + bias.astype(jnp.float32)).astype(x.dtype)


def depthwise_conv(x, w, b):
    k = w.shape[0]
    y = lax.conv_general_dilated(
        x, w.astype(x.dtype)[:, None, :], window_strides=(1,),
        padding=((k // 2, k // 2),), dimension_numbers=('NWC', 'WIO', 'NWC'),
        feature_group_count=x.shape[-1])
    return y + b.astype(x.dtype)


def axial_rope_tables(seq_len, dtype):
    rows = seq_len // GRID_W
    row_pos = jnp.repeat(jnp.arange(rows, dtype=jnp.float32), GRID_W)
    col_pos = jnp.tile(jnp.arange(GRID_W, dtype=jnp.float32), rows)
    freqs = ROPE_THETA ** (-jnp.arange(AXIS_FREQS, dtype=jnp.float32) * 2.0 / AXIS_DIM)
    ang_r = row_pos[:, None] * freqs[None, :]
    ang_c = col_pos[:, None] * freqs[None, :]
    return tuple(t.astype(dtype)[None, :, None, :] for t in
                 (jnp.cos(ang_r), jnp.sin(ang_r), jnp.cos(ang_c), jnp.sin(ang_c)))


def apply_axial_rope(x, tables):
    cr, sr, cc, sc = tables
    x1, x2, x3, x4 = jnp.split(x, 4, axis=-1)
    return jnp.concatenate([x1 * cr - x2 * sr, x2 * cr + x1 * sr,
                            x3 * cc - x4 * sc, x4 * cc + x3 * sc], axis=-1)


def gqa_attention_mixer(h, w_qkv, q_gain, k_gain, w_o):
    b, s, _ = h.shape
    qkv = h @ w_qkv.astype(h.dtype)
    q, k, v = jnp.split(qkv, [ATTN_DIM, ATTN_DIM + N_KV_HEADS * HEAD_DIM], axis=-1)
    q = rms_norm(q.reshape(b, s, N_HEADS, HEAD_DIM), q_gain)
    k = rms_norm(k.reshape(b, s, N_KV_HEADS, HEAD_DIM), k_gain)
    v = v.reshape(b, s, N_KV_HEADS, HEAD_DIM)
    tables = axial_rope_tables(s, h.dtype)
    q = apply_axial_rope(q, tables)
    k = apply_axial_rope(k, tables)
    n_blk = s // Q_BLOCK
    qb = q.reshape(b, n_blk, Q_BLOCK, N_KV_HEADS, GQA_GROUP, HEAD_DIM).transpose(1, 0, 3, 4, 2, 5)
    kt = k.transpose(0, 2, 1, 3)
    vt = v.transpose(0, 2, 1, 3)
    scale = HEAD_DIM ** -0.5

    def one_block(q_blk):
        sc = jnp.einsum('bkgqd,bksd->bkgqs', q_blk, kt,
                        preferred_element_type=jnp.float32) * scale
        p = jax.nn.softmax(sc, axis=-1).astype(vt.dtype)
        return jnp.einsum('bkgqs,bksd->bkgqd', p, vt)

    o = lax.map(one_block, qb)
    o = o.transpose(1, 0, 4, 2, 3, 5).reshape(b, s, ATTN_DIM)
    return o @ w_o.astype(h.dtype)


def conformer_conv_mixer(h, w_pw1, b_pw1, w_dw, b_dw, ln_g, ln_b, w_pw2, b_pw2):
    u = h @ w_pw1.astype(h.dtype) + b_pw1.astype(h.dtype)
    a, g = jnp.split(u, 2, axis=-1)
    u = a * jax.nn.sigmoid(g)
    u = depthwise_conv(u, w_dw, b_dw)
    u = jax.nn.silu(layer_norm(u, ln_g, ln_b))
    return u @ w_pw2.astype(h.dtype) + b_pw2.astype(h.dtype)


def conv_gated_mlp(h, w_up, w_dw, b_dw, w_down):
    u = h @ w_up.astype(h.dtype)
    u = depthwise_conv(u, w_dw, b_dw)
    g, v = jnp.split(u, 2, axis=-1)
    return (jax.nn.silu(g) * v) @ w_down.astype(h.dtype)


def run_trunk(x, attn_norm, attn_w_qkv, attn_q_norm, attn_k_norm, attn_w_o,
              conv_norm, conv_w_pw1, conv_b_pw1, conv_w_dw, conv_b_dw,
              conv_ln_g, conv_ln_b, conv_w_pw2, conv_b_pw2,
              ffn_norm, ffn_w_up, ffn_w_dw, ffn_b_dw, ffn_w_down):
    for i in range(DEPTH):
        j = i // N_MIXERS
        if i % N_MIXERS == 0:
            x = x + gqa_attention_mixer(rms_norm(x, attn_norm[j]), attn_w_qkv[j],
                                        attn_q_norm[j], attn_k_norm[j], attn_w_o[j])
        else:
            x = x + conformer_conv_mixer(rms_norm(x, conv_norm[j]), conv_w_pw1[j], conv_b_pw1[j],
                                         conv_w_dw[j], conv_b_dw[j], conv_ln_g[j], conv_ln_b[j],
                                         conv_w_pw2[j], conv_b_pw2[j])
        x = x + conv_gated_mlp(rms_norm(x, ffn_norm[i]), ffn_w_up[i], ffn_w_dw[i],
                               ffn_b_dw[i], ffn_w_down[i])
    return x


def setup_inputs(seed: int = 0) -> dict:
    key = jax.random.key(seed)
    ks = jax.random.split(key, 24)
    f32 = jnp.float32

    def w(k, shape, fan_in):
        return jax.random.normal(k, shape, f32) * (fan_in ** -0.5)

    def gain(k, shape):
        return 1.0 + 0.01 * jax.random.normal(k, shape, f32)

    def bias(k, shape):
        return 0.02 * jax.random.normal(k, shape, f32)

    na, nc, d, f = N_ATTN_LAYERS, N_CONV_LAYERS, D_MODEL, D_FF
    return {
        "x_prompt": jax.random.normal(ks[0], (BATCH, SEQ, D_MODEL), f32),
        "x_sample": jax.random.normal(ks[1], (DEC_BATCH, DEC_SEQ, D_MODEL), f32),
        "attn_norm": gain(ks[2], (na, d)),
        "attn_w_qkv": w(ks[3], (na, d, QKV_DIM), d),
        "attn_q_norm": gain(ks[4], (na, HEAD_DIM)),
        "attn_k_norm": gain(ks[5], (na, HEAD_DIM)),
        "attn_w_o": w(ks[6], (na, ATTN_DIM, d), ATTN_DIM),
        "conv_norm": gain(ks[7], (nc, d)),
        "conv_w_pw1": w(ks[8], (nc, d, 2 * d), d),
        "conv_b_pw1": bias(ks[9], (nc, 2 * d)),
        "conv_w_dw": w(ks[10], (nc, CONV_K, d), CONV_K),
        "conv_b_dw": bias(ks[11], (nc, d)),
        "conv_ln_g": gain(ks[12], (nc, d)),
        "conv_ln_b": bias(ks[13], (nc, d)),
        "conv_w_pw2": w(ks[14], (nc, d, d), d),
        "conv_b_pw2": bias(ks[15], (nc, d)),
        "ffn_norm": gain(ks[16], (DEPTH, d)),
        "ffn_w_up": w(ks[17], (DEPTH, d, 2 * f), d),
        "ffn_w_dw": w(ks[18], (DEPTH, FFN_CONV_K, 2 * f), FFN_CONV_K),
        "ffn_b_dw": bias(ks[19], (DEPTH, 2 * f)),
        "ffn_w_down": w(ks[20], (DEPTH, f, d), f),
    }


def reference(x_prompt, x_sample, attn_norm, attn_w_qkv, attn_q_norm, attn_k_norm, attn_w_o,
              conv_norm, conv_w_pw1, conv_b_pw1, conv_w_dw, conv_b_dw,
              conv_ln_g, conv_ln_b, conv_w_pw2, conv_b_pw2,
              ffn_norm, ffn_w_up, ffn_w_dw, ffn_b_dw, ffn_w_down):
    y_prompt = run_trunk(x_prompt, attn_norm, attn_w_qkv, attn_q_norm, attn_k_norm, attn_w_o,
                         conv_norm, conv_w_pw1, conv_b_pw1, conv_w_dw, conv_b_dw,
                         conv_ln_g, conv_ln_b, conv_w_pw2, conv_b_pw2,
                         ffn_norm, ffn_w_up, ffn_w_dw, ffn_b_dw, ffn_w_down)
    y_sample = run_trunk(x_sample, attn_norm, attn_w_qkv, attn_q_norm, attn_k_norm, attn_w_o,
                         conv_norm, conv_w_pw1, conv_b_pw1, conv_w_dw, conv_b_dw,
                         conv_ln_g, conv_ln_b, conv_w_pw2, conv_b_pw2,
                         ffn_norm, ffn_w_up, ffn_w_dw, ffn_b_dw, ffn_w_down)
    return (y_prompt, y_sample)
```

```python
import numpy as np
from contextlib import ExitStack
import concourse.bass as bass
import concourse.mybir as mybir
from concourse.bass_utils import run_bass_kernel_spmd

F32 = mybir.dt.float32
BF16 = mybir.dt.bfloat16
AF = mybir.ActivationFunctionType
ALU = mybir.AluOpType
AX = mybir.AxisListType

D = 2048
KC = 16
T = 512
NH, NKV, HD = 16, 4, 128
DFF = 5632
JF = DFF // 128
CK = 31
EPS = 1e-6
NEG = -30000.0


class Op:
    __slots__ = ("eng", "fn", "deps", "ddeps", "signal", "idx", "dent", "dval", "bar", "tiny")


class Tok:
    __slots__ = ("w", "r", "x")

    def __init__(self, x=False):
        self.w = []
        self.r = {}
        self.x = x


ENGS = ("pe", "act", "dve", "pool", "sp")


class Sched:
    def __init__(self, nc, stack):
        self.nc = nc
        self.h = {"pe": nc.tensor, "act": nc.scalar, "dve": nc.vector, "pool": nc.gpsimd, "sp": nc.sync}
        self.sem = {e: stack.enter_context(nc.semaphore("s_" + e)) for e in ENGS}
        self.cnt = {e: 0 for e in ENGS}
        self.ops = {e: [] for e in ENGS}
        self.dfree = [[stack.enter_context(nc.semaphore("d%d" % i)), 0] for i in range(72)]
        self.dkeys = {}
        self.dused = []
        self.waited = {e: {} for e in ENGS}
        self.bar = None
        self.bar_pending = set()
        self.nops = 0

    def _add(self, o, p, raw):
        if p is o:
            return
        if p.dent is not None:
            o.ddeps.append((p.dent, p.dent[1]))
            return
        if p.eng == o.eng and o.dent is None:
            if not (raw and p.tiny and p.eng != "pe"):
                return
        p.signal = True
        o.deps.append(p)

    def _track(self, o, r, w, wadd):
        if any(t.x for t in r):
            w = list(w) + [t for t in r if t.x and t not in w]
            r = [t for t in r if not t.x]
        for t in r:
            for p in t.w:
                self._add(o, p, True)
        for t in w:
            for p in t.w:
                self._add(o, p, False)
            for p in t.r.values():
                self._add(o, p, False)
        for t in wadd:
            for p in t.r.values():
                self._add(o, p, False)
        key = o.eng if o.dent is None else id(o.dent)
        for t in r:
            t.r[key] = o
        for t in w:
            t.w = [o]
            t.r = {}
        for t in wadd:
            t.w.append(o)

    def _new(self, eng):
        o = Op()
        o.eng = eng
        o.deps = []
        o.ddeps = []
        o.signal = False
        o.dent = None
        o.tiny = False
        o.bar = None
        if eng in self.bar_pending:
            o.bar = self.bar
            self.bar_pending.discard(eng)
        self.ops[eng].append(o)
        self.nops += 1
        return o

    def op(self, eng, fn, r=(), w=(), wadd=(), tiny=False):
        o = self._new(eng)
        o.fn = fn
        o.tiny = tiny
        self._track(o, r, w, wadd)
        return o

    def dma(self, q, key, out, in_, r=(), w=(), wadd=(), slow=False):
        o = self._new(q)
        ent = self.dkeys.get(key)
        if ent is None:
            ent = self.dfree.pop()
            self.dkeys[key] = ent
            self.dused.append(ent)
        o.dent = ent
        self._track(o, r, w, wadd)
        ent[1] += 16
        o.dval = ent[1]
        if slow:
            o.fn = lambda e: e.dma_start(out=out, in_=in_, allow_slow_non_contiguous=True)
        else:
            o.fn = lambda e: e.dma_start(out=out, in_=in_)
        return o

    def flush(self, final=False):
        nc = self.nc
        last = []
        for e in ENGS:
            for o in reversed(self.ops[e]):
                if o.dent is None:
                    o.signal = True
                    last.append(o)
                    break
        for e in ENGS:
            c = self.cnt[e]
            for o in self.ops[e]:
                if o.dent is None and o.signal:
                    c += 1
                    o.idx = c
            self.cnt[e] = c
        bar_d = [(ent, ent[1]) for ent in self.dused]
        if final:
            o = self._new("sp")
            o.fn = None
            o.bar = (last, bar_d)
        with nc.Block() as block:
            for e in ENGS:
                if self.ops[e]:
                    getattr(block, {"pe": "tensor", "act": "scalar", "dve": "vector", "pool": "gpsimd",
                                    "sp": "sync"}[e])(self._emitter(e))
        for e in ENGS:
            self.ops[e] = []
        self.bar = (last, bar_d)
        self.bar_pending = set(ENGS)
        self.dfree.extend(self.dused)
        self.dused = []
        self.dkeys = {}

    def _emitter(self, e):
        ops = self.ops[e]
        sem = self.sem
        waited = self.waited[e]
        mysem = sem[e]

        import os
        dbg = os.environ.get("SCHED_DBG")
        log = []

        def wait(h, s, v):
            k = id(s)
            if waited.get(k, 0) >= v:
                return
            waited[k] = v
            if dbg:
                log.append("   %s wait %s >= %d" % (e, getattr(s, "name", str(s)), v))
            h.wait_ge(s, v)

        def run(h):
            for o in ops:
                if o.bar is not None:
                    for p in o.bar[0]:
                        if p.eng != e:
                            wait(h, sem[p.eng], p.idx)
                    for ent, v in o.bar[1]:
                        wait(h, ent[0], v)
                for p in o.deps:
                    wait(h, sem[p.eng], p.idx)
                for ent, v in o.ddeps:
                    wait(h, ent[0], v)
                if o.fn is None:
                    continue
                ins = o.fn(h)
                if dbg and len(log) < 400:
                    log.append("%s op#%d %s%s" % (e, len(log), "DMA->%s=%d" % (getattr(o.dent[0], "name", "?"), o.dval) if o.dent is not None else "", " sig%d" % o.idx if (o.dent is None and o.signal) else ""))
                    if len(log) >= 400:
                        print("\n".join(log))
                if o.dent is not None:
                    ins.then_inc(o.dent[0], 16)
                elif o.signal:
                    ins.then_inc(mysem, 1)
        return run


def _pcols(v):
    v = np.asarray(v, np.float32).reshape(-1, 128)
    return np.ascontiguousarray(v.T)


class PV:
    def __init__(self):
        self.off = {}
        self.cols = []
        self.n = 0

    def add(self, name, arr2d):
        self.off[name] = self.n
        self.cols.append(np.asarray(arr2d, np.float32))
        self.n += arr2d.shape[1]

    def pack(self):
        return np.ascontiguousarray(np.concatenate(self.cols, axis=1))


def lay_blocks(W, cols_of_block):
    K = W.shape[0]
    kcn = K // 128
    out = []
    for cols in cols_of_block:
        blk = W[:, cols].reshape(kcn, 128, len(cols)).transpose(1, 0, 2)
        out.append(blk)
    return np.stack(out, axis=1)


def lay_qkv(W):
    qk = lay_blocks(W, [np.arange(c * 128, (c + 1) * 128) for c in range(20)]).reshape(128, -1)
    v = lay_blocks(W, [np.arange(2560, 3072)]).reshape(128, -1)
    return np.ascontiguousarray(np.concatenate([qk, v], axis=1))


def lay_rowmajor(W):
    return np.ascontiguousarray(lay_blocks(W, [np.arange(W.shape[1])]).reshape(128, -1))


def lay_pw1(W):
    blocks = []
    for j in range(16):
        for ag in range(2):
            blocks.append(np.arange(ag * 2048 + j * 128, ag * 2048 + (j + 1) * 128))
    return np.ascontiguousarray(lay_blocks(W, blocks).reshape(128, -1))


def lay_up(W):
    blocks = []
    for j in range(JF):
        for gv in range(2):
            blocks.append(np.arange(gv * DFF + j * 128, gv * DFF + (j + 1) * 128))
    return np.ascontiguousarray(lay_blocks(W, blocks).reshape(128, -1))


def lay_dn(W):
    return np.ascontiguousarray(lay_blocks(W, [np.arange(m * 128, (m + 1) * 128) for m in range(16)]).reshape(128, -1))


class Cfg:
    def __init__(self, NT=16, layers=None):
        self.NT = NT
        self.NTOK = NT * T
        self.NKC = self.NTOK // 128
        if layers is None:
            layers = []
            for i in range(4):
                layers.append(("attn" if i % 2 == 0 else "conv", i // 2))
                layers.append(("ffn", i))
        self.layers = layers
        self.n_attn = 1 + max([i for k, i in layers if k == "attn"], default=-1)
        self.n_conv = 1 + max([i for k, i in layers if k == "conv"], default=-1)
        self.n_ffn = 1 + max([i for k, i in layers if k == "ffn"], default=-1)
        self.conv_pool_every = 0
        self.debug = False
        self.attn_sum_pe = False
        self.ffn_norm_j = 12


def pv_layout(cfg):
    off = {}
    n = 0

    def add(name, w):
        nonlocal n
        off[name] = n
        n += w
    add("eps", 1)
    for a in range(cfg.n_attn):
        add("an%d" % a, 16)
        add("qg%d" % a, 1)
        add("kg%d" % a, 1)
    for c in range(cfg.n_conv):
        add("cn%d" % c, 16)
        add("bpw1%d" % c, 32)
        add("wdw%d" % c, CK * 16)
        add("bdw%d" % c, 16)
        add("lng%d" % c, 16)
        add("lnb%d" % c, 16)
        add("bpw2%d" % c, 16)
    for i in range(cfg.n_ffn):
        add("fn%d" % i, 16)
        add("fw%d" % i, 3 * 2 * JF)
        add("fb%d" % i, 2 * JF)
    return off, n


def build_program(cfg):
    NT, NTOK, NKC = cfg.NT, cfg.NTOK, cfg.NKC
    nc = bass.Bass("TRN2", target_bir_lowering=False)
    pvoff, npv = pv_layout(cfg)

    def din(name, shape, dt=F32):
        return nc.dram_tensor(name, list(shape), dt, kind="ExternalInput").ap()

    def dscr(name, shape, dt):
        return nc.dram_tensor(name, list(shape), dt).ap()

    xin = din("xin", [NTOK, D])
    yout = nc.dram_tensor("yout", [NTOK, D], F32, kind="ExternalOutput").ap()
    pv_d = din("pv", [128, npv])
    cm_d = din("cm", [128, 4 + 2 * NT])
    ropec_d = din("ropec", [128, NTOK])
    ropes_d = din("ropes", [128, NTOK])
    ident_d = din("ident", [128, 128])
    perm_d = din("perm", [128, 128])
    gn_d = din("gn", [1, max(1, cfg.n_attn) * 256])
    wsrc, wdst = {}, {}
    WL = {"wqkv": 49152, "wo": 32768, "wpw1": 65536, "wpw2": 32768, "wup": 2 * JF * 2048, "wdn": 16 * JF * 128}
    for a in range(cfg.n_attn):
        for nm in ("wqkv", "wo"):
            wsrc[nm, a] = din("%s%d" % (nm, a), [128, WL[nm]])
            wdst[nm, a] = dscr("b_%s%d" % (nm, a), [128, WL[nm]], BF16)
    for c in range(cfg.n_conv):
        for nm in ("wpw1", "wpw2"):
            wsrc[nm, c] = din("%s%d" % (nm, c), [128, WL[nm]])
            wdst[nm, c] = dscr("b_%s%d" % (nm, c), [128, WL[nm]], BF16)
    for i in range(cfg.n_ffn):
        for nm in ("wup", "wdn"):
            wsrc[nm, i] = din("%s%d" % (nm, i), [128, WL[nm]])
            wdst[nm, i] = dscr("b_%s%d" % (nm, i), [128, WL[nm]], BF16)
    XA = dscr("XA", [D, NTOK], F32)
    XB = dscr("XB", [D, NTOK], F32)
    G = dscr("Gs", [D, NTOK], F32)
    QS = dscr("QS", [NH * HD, NTOK], BF16)
    KS = dscr("KS", [NKV * HD, NTOK], BF16)
    VS = dscr("VS", [128, NKV * NKC * HD], BF16)
    OS = dscr("OS", [NH * HD, NTOK], BF16)

    if cfg.debug:
        dbg_xn = nc.dram_tensor("dbg_xn", [128, KC * 514], BF16, kind="ExternalOutput").ap()
        dbg_h = nc.dram_tensor("dbg_h", [128, JF * T], BF16, kind="ExternalOutput").ap()
        dbg_rs = nc.dram_tensor("dbg_rs", [128, 514], F32, kind="ExternalOutput").ap()
        dbg_wu = nc.dram_tensor("dbg_wu", [128, 2 * KC * 128], BF16, kind="ExternalOutput").ap()
        dbg_tt = nc.dram_tensor("dbg_tt", [128, 3 * T], F32, kind="ExternalOutput").ap()
        dbg_pm = nc.dram_tensor("dbg_pm", [128, 4 * T], F32, kind="ExternalOutput").ap()
    top = ExitStack()
    with top:
        S = Sched(nc, top)

        uid = [0]

        def sb(stack, name, shape, dt):
            uid[0] += 1
            return stack.enter_context(nc.sbuf_tensor("%s_u%d" % (name, uid[0]), list(shape), dt))

        pvt = sb(top, "pvt", [128, npv], F32)
        cmt = sb(top, "cmt", [128, 4 + 2 * NT], F32)
        ident = sb(top, "ident", [128, 128], F32)
        permf = sb(top, "permf", [128, 128], F32)
        permb = sb(top, "permb", [128, 128], BF16)
        ones_d = sb(top, "ones_d", [128, 128], BF16)
        ones_h = sb(top, "ones_h", [128, 128], BF16)
        ones_1 = sb(top, "ones_1", [128, 128], BF16)
        ones_f = sb(top, "ones_f", [128, 128], F32)
        tk_const = Tok()

        def pvc(name, c=0, n=1):
            o = pvoff[name] + c
            return pvt[:, o:o + n]

        eps_ap = pvc("eps")

        def psum_banks(stack):
            uid[0] += 1
            return [stack.enter_context(nc.psum_tensor("ps%d_u%d" % (i, uid[0]), [128, 512], F32)) for i in range(8)]

        with ExitStack() as ph:
            S.dma("sp", "c0", pvt[:], pv_d[:, :], w=[tk_const])
            S.dma("sp", "c1", cmt[:], cm_d[:, :], wadd=[tk_const])
            S.dma("sp", "c2", ident[:], ident_d[:, :], wadd=[tk_const])
            S.dma("sp", "c3", permf[:], perm_d[:, :], wadd=[tk_const])
            S.op("dve", lambda e: e.tensor_copy(out=permb[:], in_=permf[:]), r=[tk_const], wadd=[tk_const])
            S.op("dve", lambda e: e.memset(ones_d[:], 1.0 / D), wadd=[tk_const])
            S.op("dve", lambda e: e.memset(ones_h[:], 1.0 / HD), wadd=[tk_const])
            S.op("dve", lambda e: e.memset(ones_1[:], 1.0), wadd=[tk_const])
            S.op("dve", lambda e: e.memset(ones_f[:], 1.0), wadd=[tk_const])
            CH = 8192
            NB_ = 3
            stg = [sb(ph, "stg%d" % i, [128, CH], F32) for i in range(NB_)]
            obf = [sb(ph, "obf%d" % i, [128, CH], BF16) for i in range(NB_)]
            tks = [Tok() for _ in range(NB_)]
            tko = [Tok() for _ in range(NB_)]
            ci = 0

            def cast_region(src, dst, l0, l1, gname, kcn, cb):
                nonlocal ci
                per = kcn * cb
                pos = l0
                while pos < l1:
                    n = min(CH, l1 - pos)
                    b = ci % NB_
                    ci += 1
                    S.dma("sp", "stg%d" % b, stg[b][:, 0:n], src[:, pos:pos + n], w=[tks[b]])
                    if gname is None:
                        S.op("act", (lambda b=b, n=n: lambda e: e.copy(out=obf[b][:, 0:n], in_=stg[b][:, 0:n]))(),
                             r=[tks[b]], w=[tko[b]])
                    else:
                        rel = pos - l0
                        if n >= per:
                            assert n % per == 0 and rel % per == 0
                            reps, k0, nk = n // per, 0, kcn
                        else:
                            assert per % n == 0 and n % cb == 0
                            reps, k0, nk = 1, (rel % per) // cb, n // cb
                        go = pvoff[gname] + k0

                        def f(e, b=b, n=n, reps=reps, nk=nk, go=go):
                            i0 = stg[b][:, 0:n].rearrange("p (r k c) -> p r k c", r=reps, k=nk)
                            o0 = obf[b][:, 0:n].rearrange("p (r k c) -> p r k c", r=reps, k=nk)
                            g = pvt[:, go:go + nk].unsqueeze(1).unsqueeze(3).broadcast_to([128, reps, nk, cb])
                            return e.tensor_tensor(out=o0, in0=i0, in1=g, op=ALU.mult)
                        S.op("dve", f, r=[tks[b], tk_const], w=[tko[b]])
                    S.dma("pool", "obf%d" % b, dst[:, pos:pos + n], obf[b][:, 0:n], r=[tko[b]])
                    pos += n

            for a in range(cfg.n_attn):
                cast_region(wsrc["wqkv", a], wdst["wqkv", a], 0, 40960, "an%d" % a, 16, 128)
                cast_region(wsrc["wqkv", a], wdst["wqkv", a], 40960, 49152, "an%d" % a, 16, 512)
                cast_region(wsrc["wo", a], wdst["wo", a], 0, WL["wo"], None, 0, 0)
            for c in range(cfg.n_conv):
                cast_region(wsrc["wpw1", c], wdst["wpw1", c], 0, WL["wpw1"], "cn%d" % c, 16, 128)
                cast_region(wsrc["wpw2", c], wdst["wpw2", c], 0, WL["wpw2"], None, 0, 0)
            for i in range(cfg.n_ffn):
                cast_region(wsrc["wup", i], wdst["wup", i], 0, WL["wup"], "fn%d" % i, 16, 128)
                cast_region(wsrc["wdn", i], wdst["wdn", i], 0, WL["wdn"], None, 0, 0)
            S.flush()
            tk_const.w = []

        XT = lambda X: X.rearrange("(k p) t -> p k t", p=128)

        with ExitStack() as ph:
            ps = psum_banks(ph)
            ptk = [Tok(True) for _ in range(8)]
            xtok = [sb(ph, "xtok%d" % i, [128, 4, D], F32) for i in range(2)]
            ttok = [Tok() for _ in range(2)]
            xTt = [sb(ph, "xTt%d" % i, [128, KC, T], F32) for i in range(2)]
            tT = [[Tok() for _ in range(KC)] for _ in range(2)]
            for i in range(NT):
                b = i % 2
                S.dma("sp", "xtok%d" % b, xtok[b][:], xin[i * T:(i + 1) * T, :].rearrange("(s p) d -> p s d", p=128),
                      w=[ttok[b]])
                for k in range(KC):
                    pb = k % 4
                    for s in range(4):
                        S.op("pe", (lambda pb=pb, s=s, k=k, b=b: lambda e: e.transpose(
                            out=ps[pb][:, s * 128:(s + 1) * 128], in_=xtok[b][:, s, k * 128:(k + 1) * 128],
                            identity=ident[:]))(), r=[ttok[b], tk_const], w=[ptk[pb]])
                    if k % 2 == 0:
                        S.op("act", (lambda pb=pb, k=k, b=b: lambda e: e.copy(out=xTt[b][:, k, :], in_=ps[pb][:, :]))(),
                             r=[ptk[pb]], w=[tT[b][k]])
                    else:
                        S.op("dve", (lambda pb=pb, k=k, b=b: lambda e: e.tensor_copy(out=xTt[b][:, k, :], in_=ps[pb][:, :]))(),
                             r=[ptk[pb]], w=[tT[b][k]])
                S.dma("pool", "xTt%d" % b, XT(XA)[:, :, i * T:(i + 1) * T], xTt[b][:], r=tT[b])
            S.flush()

        def norm_tile(ph_bufs, src, i, W, xn_b, tk_xn, emask):
            xt, tk_xt, sq, tk_sq, rs, tk_rs, pst, tk_pst, psh, tk_psh, halo_bufs = ph_bufs
            X = XT(src)
            S.dma("sp", "xt", xt[:, :, 0:T], X[:, :, i * T:(i + 1) * T], w=[tk_xt])
            if W == 514:
                hl, hr, tk_hl, tk_hr, tk_xh = halo_bufs
                if i > 0:
                    S.dma("pool", "hl", hl[:], X[:, :, i * T - 16:i * T], w=[tk_hl])
                    S.op("dve", lambda e: e.tensor_copy(out=xt[:, :, 512:513], in_=hl[:, :, 15:16]), r=[tk_hl], w=[tk_xh])
                else:
                    S.op("dve", lambda e: e.memset(xt[:, :, 512:513], 0.0), w=[tk_xh])
                if i < NT - 1:
                    S.dma("pool", "hr", hr[:], X[:, :, (i + 1) * T:(i + 1) * T + 16], w=[tk_hr])
                    S.op("dve", lambda e: e.tensor_copy(out=xt[:, :, 513:514], in_=hr[:, :, 0:1]), r=[tk_hr], w=[tk_xh], tiny=True)
                else:
                    S.op("dve", lambda e: e.memset(xt[:, :, 513:514], 0.0), w=[tk_xh], tiny=True)
            for k in range(KC):
                b = k % 3
                S.op("act", (lambda k=k, b=b: lambda e: e.activation(out=sq[b][:, 0:W], in_=xt[:, k, 0:W], func=AF.Square))(),
                     r=[tk_xt] + ([halo_bufs[4]] if W == 514 else []), w=[tk_sq[b]])
                S.op("pe", (lambda k=k, b=b: lambda e: e.matmul(pst[:, :], ones_d[:], sq[b][:, 0:T], start=(k == 0), stop=(k == KC - 1)))(),
                     r=[tk_sq[b], tk_const], w=[tk_pst])
                if W == 514:
                    S.op("pe", (lambda k=k, b=b: lambda e: e.matmul(psh[:, :], ones_d[:], sq[b][:, 2:514], start=(k == 0), stop=(k == KC - 1)))(),
                         r=[tk_sq[b], tk_const], w=[tk_psh])
            S.op("act", lambda e: e.activation(out=rs[:, 0:T], in_=pst[:, :], func=AF.Sqrt, bias=eps_ap, scale=1.0),
                 r=[tk_const], w=[tk_rs, tk_pst])
            if W == 514:
                S.op("act", lambda e: e.activation(out=rs[:, 2:514], in_=psh[:, :], func=AF.Sqrt, bias=eps_ap, scale=1.0),
                     r=[tk_psh, tk_const], wadd=[tk_rs])
            S.op("dve", lambda e: e.reciprocal(out=rs[:, 0:T], in_=rs[:, 0:T]), r=[tk_rs], w=[tk_rs], tiny=True)
            if W == 514:
                S.op("dve", lambda e: e.reciprocal(out=rs[:, 512:514], in_=rs[:, 512:514]), r=[tk_rs], w=[tk_rs], tiny=True)
            if W == 514:
                S.op("dve", lambda e: e.tensor_tensor(out=rs[:, 512:514], in0=rs[:, 512:514], in1=cmt[:, 4 + 2 * i:6 + 2 * i], op=ALU.mult),
                     r=[tk_const, tk_rs], w=[tk_rs], tiny=True)
            for q in range(4):
                S.op("dve", (lambda q=q: lambda e: e.tensor_tensor(
                    out=xn_b[:, 4 * q:4 * q + 4, 0:W], in0=xt[:, 4 * q:4 * q + 4, 0:W],
                    in1=rs[:, 0:W].unsqueeze(1).broadcast_to([128, 4, W]), op=ALU.mult))(),
                     r=[tk_xt, tk_rs] + ([halo_bufs[4]] if W == 514 else []), w=[tk_xn[q]])

        def norm_bufs(ph, W, pst, tk_pst, psh, tk_psh):
            xt = sb(ph, "xt", [128, KC, W], F32)
            sq = [sb(ph, "sq%d" % i, [128, W], BF16) for i in range(3)]
            rs = sb(ph, "rs", [128, W], F32)
            hb = None
            if W == 514:
                hb = (sb(ph, "hl", [128, KC, 16], F32), sb(ph, "hr", [128, KC, 16], F32), Tok(), Tok(), Tok())
            return (xt, Tok(), sq, [Tok() for _ in range(3)], rs, Tok(), pst, tk_pst, psh, tk_psh, hb)

        def phase_ffn(li, src, dst):
            with ExitStack() as ph:
                ps = psum_banks(ph)
                ptk = [Tok(True) for _ in range(8)]
                nb = norm_bufs(ph, 514, ps[6], ptk[6], ps[7], ptk[7])
                xn = [sb(ph, "xn%d" % i, [128, KC, 514], BF16) for i in range(2)]
                tk_xn = [[Tok() for _ in range(5)] for _ in range(2)]
                hbuf = sb(ph, "hbuf", [128, JF, T], BF16)
                tk_h = [Tok() for _ in range(JF)]
                wu = [sb(ph, "wu%d" % i, [128, 2, KC, 128], BF16) for i in range(3)]
                tk_wu = [Tok() for _ in range(3)]
                wd = [sb(ph, "wd%d" % i, [128, JF, 128], BF16) for i in range(2)]
                tk_wd = [Tok() for _ in range(2)]
                tt = [[sb(ph, "tt%d%d" % (g, i), [128, T], F32) for i in range(2)] for g in range(2)]
                tk_tt = [[Tok() for _ in range(2)] for _ in range(2)]
                sg = [sb(ph, "sg%d" % i, [128, T], F32) for i in range(2)]
                tk_sg = [Tok() for _ in range(2)]
                xr = [sb(ph, "xr%d" % i, [128, T], F32) for i in range(2)]
                tk_xr = [Tok() for _ in range(2)]
                yo = [sb(ph, "yo%d" % i, [128, T], F32) for i in range(2)]
                tk_yo = [Tok() for _ in range(2)]
                wup = wdst["wup", li].rearrange("p (j g k c) -> p j g k c", j=JF, g=2, k=KC)
                wdn = wdst["wdn", li].rearrange("p (m j c) -> p m j c", m=16, j=JF)
                fw, fb = pvoff["fw%d" % li], pvoff["fb%d" % li]
                norm_tile(nb, src, 0, 514, xn[0], tk_xn[0], True)
                if cfg.debug:
                    S.dma("pool", "dbg3", dbg_rs, nb[4][:], r=[nb[5]] + tk_xn[0])
                jn = 0
                for i in range(NT):
                    cur = i % 2
                    xc = xn[cur]
                    for j in range(JF):
                        b = jn % 3
                        jp = jn % 2
                        jn += 1
                        S.dma("sp", "wu%d" % b, wu[b][:], wup[:, j], w=[tk_wu[b]])
                        for gv in range(2):
                            pm = ps[gv * 2 + jp]
                            hoff = gv * 2
                            hb_ = ps[4 + jp]
                            tkh = ptk[4 + jp]
                            for k in range(KC):
                                S.op("pe", (lambda pm=pm, b=b, gv=gv, k=k, xc=xc: lambda e: e.matmul(
                                    pm[:, :], wu[b][:, gv, k, :], xc[:, k, 0:T], start=(k == 0), stop=(k == KC - 1)))(),
                                    r=[tk_wu[b], tk_xn[cur][k // 4]], w=[ptk[gv * 2 + jp]])
                                S.op("pe", (lambda hoff=hoff, hb_=hb_, b=b, gv=gv, k=k, xc=xc: lambda e: e.matmul(
                                    hb_[:, hoff:hoff + 2], wu[b][:, gv, k, :], xc[:, k, 512:514], start=(k == 0), stop=(k == KC - 1)))(),
                                    r=[tk_wu[b], tk_xn[cur][k // 4]], w=[tkh])
                        if cfg.debug and i == 0 and j in (0, 20):
                            dj = 0 if j == 0 else 1
                            if j == 0:
                                dpm = sb(ph, "dpm", [128, 4 * T], F32)
                                tk_dpm = Tok()
                            S.op("act", (lambda jp=jp, dj=dj: lambda e: e.copy(out=dpm[:, dj * 2 * T:dj * 2 * T + T], in_=ps[jp][:, :]))(),
                                 r=[ptk[jp]], w=[tk_dpm])
                            S.op("act", (lambda jp=jp, dj=dj: lambda e: e.copy(out=dpm[:, dj * 2 * T + T:dj * 2 * T + 2 * T], in_=ps[2 + jp][:, :]))(),
                                 r=[ptk[2 + jp]], w=[tk_dpm])
                            if j == 20:
                                S.dma("pool", "dbg8", dbg_pm, dpm[:], r=[tk_dpm])
                        for gv in range(2):
                            pm = ps[gv * 2 + jp]
                            hoff = gv * 2
                            hb_ = ps[4 + jp]
                            tkh = ptk[4 + jp]
                            col = gv * JF + j
                            t_ = tt[gv][jp]
                            w0 = pvt[:, fw + col:fw + col + 1]
                            w1 = pvt[:, fw + 2 * JF + col:fw + 2 * JF + col + 1]
                            w2 = pvt[:, fw + 4 * JF + col:fw + 4 * JF + col + 1]
                            bb = pvt[:, fb + col:fb + col + 1]
                            tk = tk_tt[gv][jp]
                            S.op("act", (lambda t_=t_, pm=pm, w1=w1, bb=bb: lambda e: e.activation(
                                out=t_[:, :], in_=pm[:, :], func=AF.Identity, bias=bb, scale=w1))(),
                                r=[ptk[gv * 2 + jp], tk_const], w=[tk])
                            S.op("dve", (lambda t_=t_, pm=pm, w0=w0: lambda e: e.scalar_tensor_tensor(
                                out=t_[:, 1:T], in0=pm[:, 0:T - 1], scalar=w0, in1=t_[:, 1:T], op0=ALU.mult, op1=ALU.add))(),
                                r=[ptk[gv * 2 + jp]], w=[tk])
                            S.op("dve", (lambda t_=t_, pm=pm, w2=w2: lambda e: e.scalar_tensor_tensor(
                                out=t_[:, 0:T - 1], in0=pm[:, 1:T], scalar=w2, in1=t_[:, 0:T - 1], op0=ALU.mult, op1=ALU.add))(),
                                r=[ptk[gv * 2 + jp]], w=[tk])
                            S.op("dve", (lambda t_=t_, hoff=hoff, hb_=hb_, w0=w0: lambda e: e.scalar_tensor_tensor(
                                out=t_[:, 0:1], in0=hb_[:, hoff:hoff + 1], scalar=w0, in1=t_[:, 0:1], op0=ALU.mult, op1=ALU.add))(),
                                w=[tk, tkh])
                            S.op("dve", (lambda t_=t_, hoff=hoff, hb_=hb_, w2=w2: lambda e: e.scalar_tensor_tensor(
                                out=t_[:, T - 1:T], in0=hb_[:, hoff + 1:hoff + 2], scalar=w2, in1=t_[:, T - 1:T], op0=ALU.mult, op1=ALU.add))(),
                                w=[tk, tkh], tiny=True)
                        S.op("act", (lambda jp=jp: lambda e: e.activation(out=sg[jp][:, :], in_=tt[0][jp][:, :], func=AF.Silu))(),
                             r=[tk_tt[0][jp]], w=[tk_sg[jp]])
                        S.op("dve", (lambda jp=jp, j=j: lambda e: e.tensor_tensor(
                            out=hbuf[:, j, :], in0=sg[jp][:, :], in1=tt[1][jp][:, :], op=ALU.mult))(),
                            r=[tk_sg[jp], tk_tt[1][jp]], w=[tk_h[j]])
                        if cfg.debug and i == 0 and j == 0:
                            S.dma("pool", "dbg4", dbg_wu.rearrange("p (g k c) -> p g k c", g=2, k=KC), wu[b][:], r=[tk_wu[b]])
                            S.dma("pool", "dbg5", dbg_tt[:, 0:T], tt[0][jp][:], r=[tk_tt[0][jp], tk_h[j]])
                            S.dma("pool", "dbg6", dbg_tt[:, T:2 * T], tt[1][jp][:], r=[tk_tt[1][jp], tk_h[j]])
                            S.dma("pool", "dbg7", dbg_tt[:, 2 * T:3 * T], sg[jp][:], r=[tk_sg[jp], tk_h[j]])
                        if j == cfg.ffn_norm_j and i + 1 < NT:
                            norm_tile(nb, src, i + 1, 514, xn[1 - cur], tk_xn[1 - cur], True)
                    if cfg.debug and i == 0:
                        S.dma("pool", "dbg1", dbg_xn.rearrange("p (k t) -> p k t", k=KC), xn[0][:], r=tk_xn[0])
                        S.dma("pool", "dbg2", dbg_h.rearrange("p (j t) -> p j t", j=JF), hbuf[:], r=tk_h)
                    for m in range(16):
                        b = m % 2
                        S.dma("sp", "wd%d" % b, wd[b][:], wdn[:, m], w=[tk_wd[b]])
                        S.dma("pool", "xr%d" % b, xr[b][:], src[m * 128:(m + 1) * 128, i * T:(i + 1) * T], w=[tk_xr[b]])
                        for jc in range(JF):
                            S.op("pe", (lambda b=b, jc=jc: lambda e: e.matmul(
                                ps[6 + b][:, :], wd[b][:, jc, :], hbuf[:, jc, :], start=(jc == 0), stop=(jc == JF - 1)))(),
                                r=[tk_wd[b], tk_h[jc]], w=[ptk[6 + b]])
                        S.op("dve", (lambda b=b: lambda e: e.tensor_tensor(out=yo[b][:, :], in0=ps[6 + b][:, :], in1=xr[b][:, :], op=ALU.add))(),
                             r=[ptk[6 + b], tk_xr[b]], w=[tk_yo[b]])
                        S.dma("pool", "yo%d" % b, dst[m * 128:(m + 1) * 128, i * T:(i + 1) * T], yo[b][:], r=[tk_yo[b]])
                S.flush()

        def phase_conv(li, src, dst):
            with ExitStack() as ph:
                ps = psum_banks(ph)
                ptk = [Tok(True) for _ in range(8)]
                nb = norm_bufs(ph, T, ps[5], ptk[5], None, None)
                xn = [sb(ph, "xn%d" % i, [128, KC, T], BF16) for i in range(2)]
                tk_xn = [[Tok() for _ in range(5)] for _ in range(2)]
                wp = [sb(ph, "wp%d" % i, [128, 2, KC, 128], BF16) for i in range(3)]
                tk_wp = [Tok() for _ in range(3)]
                sig = [sb(ph, "sig%d" % i, [128, T], F32) for i in range(2)]
                tk_sig = [Tok() for _ in range(2)]
                gl = [sb(ph, "gl%d" % i, [128, T], F32) for i in range(2)]
                tk_gl = [Tok() for _ in range(2)]
                wsrc_ = wdst["wpw1", li].rearrange("p (j g k c) -> p j g k c", j=16, g=2, k=KC)
                bo = pvoff["bpw1%d" % li]
                norm_tile(nb, src, 0, T, xn[0], tk_xn[0], False)
                jn = 0
                for i in range(NT):
                    cur = i % 2
                    xc = xn[cur]
                    for j in range(16):
                        b = jn % 3
                        jp = jn % 2
                        jn += 1
                        S.dma("sp", "wp%d" % b, wp[b][:], wsrc_[:, j], w=[tk_wp[b]])
                        for ag in range(2):
                            for k in range(KC):
                                S.op("pe", (lambda ag=ag, jp=jp, b=b, k=k, xc=xc: lambda e: e.matmul(
                                    ps[ag * 2 + jp][:, :], wp[b][:, ag, k, :], xc[:, k, :], start=(k == 0), stop=(k == KC - 1)))(),
                                    r=[tk_wp[b], tk_xn[cur][k // 4]], w=[ptk[ag * 2 + jp]])
                        S.op("act", (lambda jp=jp, j=j: lambda e: e.activation(
                            out=sig[jp][:, :], in_=ps[2 + jp][:, :], func=AF.Sigmoid, bias=pvt[:, bo + 16 + j:bo + 17 + j], scale=1.0))(),
                            r=[ptk[2 + jp], tk_const], w=[tk_sig[jp]])
                        S.op("dve", (lambda jp=jp, j=j: lambda e: e.scalar_tensor_tensor(
                            out=gl[jp][:, :], in0=ps[jp][:, :], scalar=pvt[:, bo + j:bo + j + 1], in1=sig[jp][:, :], op0=ALU.add, op1=ALU.mult))(),
                            r=[ptk[jp], tk_sig[jp]], w=[tk_gl[jp]])
                        S.dma("pool", "gl%d" % jp, G[j * 128:(j + 1) * 128, i * T:(i + 1) * T], gl[jp][:], r=[tk_gl[jp]])
                        if j == 6 and i + 1 < NT:
                            norm_tile(nb, src, i + 1, T, xn[1 - cur], tk_xn[1 - cur], False)
                S.flush()
            with ExitStack() as ph:
                ps = psum_banks(ph)
                ptk = [Tok(True) for _ in range(8)]
                WG = T + CK - 1
                HW = CK // 2
                ge = [sb(ph, "ge%d" % i, [128, WG], F32) for i in range(4)]
                tk_ge = [Tok() for _ in range(4)]
                cb = sb(ph, "cb", [128, KC, T], F32)
                tk_c = [Tok() for _ in range(KC)]
                cbf = [sb(ph, "cbf%d" % i, [128, T], BF16) for i in range(3)]
                tk_cbf = [Tok() for _ in range(3)]
                csq = [sb(ph, "csq%d" % i, [128, T], BF16) for i in range(3)]
                tk_csq = [Tok() for _ in range(3)]
                mu = sb(ph, "mu", [128, T], F32)
                rstd = sb(ph, "rstd", [128, T], F32)
                tk_st = Tok()
                tmp = [sb(ph, "tmp%d" % i, [128, T], F32) for i in range(2)]
                tk_tmp = [Tok() for _ in range(2)]
                sbf = sb(ph, "sbf", [128, KC, T], BF16)
                tk_s = [Tok() for _ in range(KC)]
                w2 = sb(ph, "w2", [128, KC, D], BF16)
                tk_w2 = Tok()
                xr = [sb(ph, "xr%d" % i, [128, T], F32) for i in range(2)]
                tk_xr = [Tok() for _ in range(2)]
                yo = [sb(ph, "yo%d" % i, [128, T], F32) for i in range(2)]
                tk_yo = [Tok() for _ in range(2)]
                S.dma("sp", "w2", w2[:], wdst["wpw2", li].rearrange("p (k n) -> p k n", k=KC), w=[tk_w2])
                wo_, bdo = pvoff["wdw%d" % li], pvoff["bdw%d" % li]
                lg, lb, b2 = pvoff["lng%d" % li], pvoff["lnb%d" % li], pvoff["bpw2%d" % li]
                gn = 0
                for i in range(NT):
                    lo = i * T - HW
                    hi = (i + 1) * T + HW
                    clo, chi = max(lo, 0), min(hi, NTOK)
                    for k in range(KC):
                        b = gn % 4
                        gn += 1
                        g_ = ge[b]
                        if clo > lo:
                            S.op("dve", (lambda g_=g_: lambda e: e.memset(g_[:, 0:HW], 0.0))(), w=[tk_ge[b]])
                        if chi < hi:
                            S.op("dve", (lambda g_=g_: lambda e: e.memset(g_[:, WG - HW:WG], 0.0))(), w=[tk_ge[b]])
                        S.dma("sp", "ge%d" % b, g_[:, clo - lo:chi - lo], G[k * 128:(k + 1) * 128, clo:chi], w=[tk_ge[b]])
                        S.op("dve", (lambda g_=g_, i=i: lambda e: e.tensor_scalar(
                            out=g_[:, 0:HW], in0=g_[:, 0:HW], scalar1=cmt[:, 4 + 2 * i:5 + 2 * i], scalar2=None, op0=ALU.mult))(),
                            r=[tk_const], w=[tk_ge[b]])
                        S.op("dve", (lambda g_=g_, i=i: lambda e: e.tensor_scalar(
                            out=g_[:, WG - HW:WG], in0=g_[:, WG - HW:WG], scalar1=cmt[:, 5 + 2 * i:6 + 2 * i], scalar2=None, op0=ALU.mult))(),
                            r=[tk_const], w=[tk_ge[b]], tiny=True)
                        S.op("act", (lambda g_=g_, k=k: lambda e: e.activation(
                            out=cb[:, k, :], in_=g_[:, HW:HW + T], func=AF.Identity,
                            bias=pvt[:, bdo + k:bdo + k + 1], scale=pvt[:, wo_ + HW * 16 + k:wo_ + HW * 16 + k + 1]))(),
                            r=[tk_ge[b], tk_const], w=[tk_c[k]])
                        ceng = "pool" if (cfg.conv_pool_every and k % cfg.conv_pool_every == 0) else "dve"
                        for tap in range(CK):
                            if tap == HW:
                                continue
                            S.op(ceng, (lambda g_=g_, k=k, tap=tap: lambda e: e.scalar_tensor_tensor(
                                out=cb[:, k, :], in0=g_[:, tap:tap + T], scalar=pvt[:, wo_ + tap * 16 + k:wo_ + tap * 16 + k + 1],
                                in1=cb[:, k, :], op0=ALU.mult, op1=ALU.add))(),
                                r=[tk_ge[b]], w=[tk_c[k]])
                        bb = k % 3
                        S.op("act", (lambda k=k, bb=bb: lambda e: e.copy(out=cbf[bb][:, :], in_=cb[:, k, :]))(),
                             r=[tk_c[k]], w=[tk_cbf[bb]])
                        S.op("act", (lambda k=k, bb=bb: lambda e: e.activation(out=csq[bb][:, :], in_=cb[:, k, :], func=AF.Square))(),
                             r=[tk_c[k]], w=[tk_csq[bb]])
                        S.op("pe", (lambda k=k, bb=bb: lambda e: e.matmul(ps[0][:, :], ones_d[:], cbf[bb][:, :], start=(k == 0), stop=(k == KC - 1)))(),
                             r=[tk_cbf[bb], tk_const], w=[ptk[0]])
                        S.op("pe", (lambda k=k, bb=bb: lambda e: e.matmul(ps[1][:, :], ones_d[:], csq[bb][:, :], start=(k == 0), stop=(k == KC - 1)))(),
                             r=[tk_csq[bb], tk_const], w=[ptk[1]])
                    S.op("dve", lambda e: e.tensor_copy(out=mu[:, :], in_=ps[0][:, :]), r=[ptk[0]], w=[tk_st])
                    S.op("dve", lambda e: e.tensor_tensor(out=rstd[:, :], in0=mu[:, :], in1=mu[:, :], op=ALU.mult), r=[tk_st], w=[tk_st])
                    S.op("dve", lambda e: e.tensor_tensor(out=rstd[:, :], in0=ps[1][:, :], in1=rstd[:, :], op=ALU.subtract), r=[ptk[1]], w=[tk_st])
                    S.op("dve", lambda e: e.tensor_scalar(out=rstd[:, :], in0=rstd[:, :], scalar1=0.0, scalar2=None, op0=ALU.max), w=[tk_st])
                    S.op("act", lambda e: e.activation(out=rstd[:, :], in_=rstd[:, :], func=AF.Sqrt, bias=eps_ap, scale=1.0),
                         r=[tk_const], w=[tk_st])
                    S.op("dve", lambda e: e.reciprocal(out=rstd[:, :], in_=rstd[:, :]), w=[tk_st], tiny=True)
                    for k in range(KC):
                        tb = k % 2
                        S.op("dve", (lambda k=k, tb=tb: lambda e: e.tensor_tensor(out=tmp[tb][:, :], in0=cb[:, k, :], in1=mu[:, :], op=ALU.subtract))(),
                             r=[tk_c[k], tk_st], w=[tk_tmp[tb]])
                        S.op("dve", (lambda k=k, tb=tb: lambda e: e.tensor_tensor(out=tmp[tb][:, :], in0=tmp[tb][:, :], in1=rstd[:, :], op=ALU.mult))(),
                             r=[tk_st], w=[tk_tmp[tb]])
                        S.op("act", (lambda k=k, tb=tb: lambda e: e.activation(
                            out=sbf[:, k, :], in_=tmp[tb][:, :], func=AF.Silu, bias=pvt[:, lb + k:lb + k + 1], scale=pvt[:, lg + k:lg + k + 1]))(),
                            r=[tk_tmp[tb], tk_const], w=[tk_s[k]])
                    for m in range(16):
                        b = m % 2
                        S.dma("pool", "xr%d" % b, xr[b][:], src[m * 128:(m + 1) * 128, i * T:(i + 1) * T], w=[tk_xr[b]])
                        for k in range(KC):
                            S.op("pe", (lambda b=b, k=k, m=m: lambda e: e.matmul(
                                ps[2 + b][:, :], w2[:, k, m * 128:(m + 1) * 128], sbf[:, k, :], start=(k == 0), stop=(k == KC - 1)))(),
                                r=[tk_w2, tk_s[k]], w=[ptk[2 + b]])
                        S.op("dve", (lambda b=b, m=m: lambda e: e.scalar_tensor_tensor(
                            out=yo[b][:, :], in0=ps[2 + b][:, :], scalar=pvt[:, b2 + m:b2 + m + 1], in1=xr[b][:, :], op0=ALU.add, op1=ALU.add))(),
                            r=[ptk[2 + b], tk_xr[b], tk_const], w=[tk_yo[b]])
                        S.dma("pool", "yo%d" % b, dst[m * 128:(m + 1) * 128, i * T:(i + 1) * T], yo[b][:], r=[tk_yo[b]])
                S.flush()

        def phase_attn(li, src, dst):
            with ExitStack() as ph:
                ps = psum_banks(ph)
                ptk = [Tok(True) for _ in range(8)]
                nb = norm_bufs(ph, T, ps[7], ptk[7], None, None)
                xn = [sb(ph, "xn%d" % i, [128, KC, T], BF16) for i in range(2)]
                tk_xn = [[Tok() for _ in range(5)] for _ in range(2)]
                wq = [sb(ph, "wq%d" % i, [128, KC, 128], BF16) for i in range(3)]
                tk_wq = [Tok() for _ in range(3)]
                wv = sb(ph, "wv", [128, KC, 512], BF16)
                tk_wv = Tok()
                rc = [sb(ph, "rc%d" % i, [128, T], F32) for i in range(2)]
                rsn = [sb(ph, "rsn%d" % i, [128, T], F32) for i in range(2)]
                tk_rt = [Tok() for _ in range(2)]
                hsq = [sb(ph, "hsq%d" % i, [128, T], BF16) for i in range(2)]
                tk_hsq = [Tok() for _ in range(2)]
                hrs = [sb(ph, "hrs%d" % i, [128, T], F32) for i in range(2)]
                tk_hrs = [Tok() for _ in range(2)]
                qn = [sb(ph, "qn%d" % i, [128, T], BF16) for i in range(2)]
                tk_qn = [Tok() for _ in range(2)]
                t1 = [sb(ph, "t1%d" % i, [128, T], F32) for i in range(2)]
                tk_t1 = [Tok() for _ in range(2)]
                t2 = [sb(ph, "t2%d" % i, [128, T], F32) for i in range(2)]
                tk_t2 = [Tok() for _ in range(2)]
                qr = [sb(ph, "qr%d" % i, [128, T], BF16) for i in range(3)]
                tk_qr = [Tok() for _ in range(3)]
                vsb = [sb(ph, "vsb%d" % i, [128, 512], BF16) for i in range(2)]
                tk_vsb = [Tok() for _ in range(2)]
                wsrc_ = wdst["wqkv", li]
                wqk = wsrc_[:, 0:40960].rearrange("p (n k c) -> p n k c", n=20, k=KC)
                S.dma("sp", "wv", wv[:], wsrc_[:, 40960:49152].rearrange("p (k c) -> p k c", k=KC), w=[tk_wv])
                VSv = VS.rearrange("p (g c d) -> p g c d", g=NKV, c=NKC)
                norm_tile(nb, src, 0, T, xn[0], tk_xn[0], False)
                cn = 0
                for i in range(NT):
                    cur = i % 2
                    xc = xn[cur]
                    rb = i % 2
                    S.dma("pool", "rc%d" % rb, rc[rb][:], ropec_d[:, i * T:(i + 1) * T], w=[tk_rt[rb]])
                    S.dma("pool", "rsn%d" % rb, rsn[rb][:], ropes_d[:, i * T:(i + 1) * T], wadd=[tk_rt[rb]])
                    for c in range(20):
                        b = cn % 3
                        cp = cn % 2
                        q3 = cn % 3
                        cn += 1
                        S.dma("sp", "wq%d" % b, wq[b][:], wqk[:, c], w=[tk_wq[b]])
                        for k in range(KC):
                            S.op("pe", (lambda cp=cp, b=b, k=k, xc=xc: lambda e: e.matmul(
                                ps[cp][:, :], wq[b][:, k, :], xc[:, k, :], start=(k == 0), stop=(k == KC - 1)))(),
                                r=[tk_wq[b], tk_xn[cur][k // 4]], w=[ptk[cp]])
                        S.op("act", (lambda cp=cp: lambda e: e.activation(out=hsq[cp][:, :], in_=ps[cp][:, :], func=AF.Square))(),
                             r=[ptk[cp]], w=[tk_hsq[cp]])
                        S.op("pe", (lambda cp=cp: lambda e: e.matmul(ps[2 + cp][:, :], ones_h[:], hsq[cp][:, :], start=True, stop=True))(),
                             r=[tk_hsq[cp], tk_const], w=[ptk[2 + cp]])
                        S.op("act", (lambda cp=cp: lambda e: e.activation(out=hrs[cp][:, :], in_=ps[2 + cp][:, :], func=AF.Sqrt, bias=eps_ap, scale=1.0))(),
                             r=[ptk[2 + cp], tk_const], w=[tk_hrs[cp]])
                        S.op("dve", (lambda cp=cp: lambda e: e.reciprocal(out=hrs[cp][:, :], in_=hrs[cp][:, :]))(), w=[tk_hrs[cp]], tiny=True)
                        gcol = pvc("qg%d" % li) if c < 16 else pvc("kg%d" % li)
                        S.op("dve", (lambda cp=cp, gcol=gcol: lambda e: e.scalar_tensor_tensor(
                            out=qn[cp][:, :], in0=ps[cp][:, :], scalar=gcol, in1=hrs[cp][:, :], op0=ALU.mult, op1=ALU.mult))(),
                            r=[ptk[cp], tk_hrs[cp], tk_const], w=[tk_qn[cp]])
                        S.op("pe", (lambda cp=cp: lambda e: e.matmul(ps[4 + cp][:, :], permb[:], qn[cp][:, :], start=True, stop=True))(),
                             r=[tk_qn[cp], tk_const], w=[ptk[4 + cp]])
                        S.op("dve", (lambda cp=cp, rb=rb: lambda e: e.tensor_tensor(out=t1[cp][:, :], in0=qn[cp][:, :], in1=rc[rb][:, :], op=ALU.mult))(),
                             r=[tk_qn[cp], tk_rt[rb]], w=[tk_t1[cp]])
                        S.op("dve", (lambda cp=cp, rb=rb: lambda e: e.tensor_tensor(out=t2[cp][:, :], in0=ps[4 + cp][:, :], in1=rsn[rb][:, :], op=ALU.mult))(),
                             r=[ptk[4 + cp], tk_rt[rb]], w=[tk_t2[cp]])
                        S.op("dve", (lambda cp=cp, q3=q3: lambda e: e.tensor_tensor(out=qr[q3][:, :], in0=t1[cp][:, :], in1=t2[cp][:, :], op=ALU.add))(),
                             r=[tk_t1[cp], tk_t2[cp]], w=[tk_qr[q3]])
                        if c < 16:
                            S.dma("pool", "qr%d" % q3, QS[c * 128:(c + 1) * 128, i * T:(i + 1) * T], qr[q3][:], r=[tk_qr[q3]])
                        else:
                            S.dma("pool", "qr%d" % q3, KS[(c - 16) * 128:(c - 15) * 128, i * T:(i + 1) * T], qr[q3][:], r=[tk_qr[q3]])
                        if c == 8 and i + 1 < NT:
                            norm_tile(nb, src, i + 1, T, xn[1 - cur], tk_xn[1 - cur], False)
                    for s in range(4):
                        vb = s % 2
                        for k in range(KC):
                            S.op("pe", (lambda s=s, k=k, xc=xc: lambda e: e.matmul(
                                ps[6][:, :], xc[:, k, s * 128:(s + 1) * 128], wv[:, k, :], start=(k == 0), stop=(k == KC - 1)))(),
                                r=[tk_wv, tk_xn[cur][k // 4]], w=[ptk[6]])
                        S.op("act", (lambda vb=vb: lambda e: e.copy(out=vsb[vb][:, :], in_=ps[6][:, :]))(), r=[ptk[6]], w=[tk_vsb[vb]])
                        S.dma("pool", "vsb%d" % vb, VSv[:, :, 4 * i + s, :], vsb[vb][:, :].rearrange("p (g d) -> p g d", g=NKV),
                              r=[tk_vsb[vb]])
                S.flush()
            with ExitStack() as ph:
                ps = psum_banks(ph)
                ptk = [Tok(True) for _ in range(8)]
                kt = [sb(ph, "kt%d" % i, [128, NTOK], BF16) for i in range(2)]
                vt = [sb(ph, "vt%d" % i, [128, NKC, 128], BF16) for i in range(2)]
                tk_kv = [Tok() for _ in range(2)]
                qt = [sb(ph, "qt%d" % i, [128, T], BF16) for i in range(3)]
                tk_qt = [Tok() for _ in range(3)]
                pt = [sb(ph, "pt%d" % i, [128, T], BF16) for i in range(4)]
                tk_pt = [Tok() for _ in range(4)]
                rcp = [sb(ph, "rcp%d" % i, [128, T], F32) for i in range(2)]
                tk_rcp = [Tok() for _ in range(2)]
                ot = [sb(ph, "ot%d" % i, [128, T], BF16) for i in range(2)]
                tk_ot = [Tok() for _ in range(2)]
                accd = [sb(ph, "accd%d" % i, [128, T], F32) for i in range(2)]
                accp = [sb(ph, "accp%d" % i, [128, T], F32) for i in range(2)]
                tk_accd = [Tok() for _ in range(2)]
                tk_accp = [Tok() for _ in range(2)]
                gnb = sb(ph, "gnb", [128, 256], F32)
                bias = sb(ph, "bias", [128, 8], F32)
                tk_b = Tok()
                S.dma("sp", "gnb", gnb[:], gn_d[0:1, li * 256:(li + 1) * 256].partition_broadcast(128), w=[tk_b])
                S.op("dve", lambda e: e.reduce_max(out=bias[:, 4:5], in_=gnb[:, 0:128], axis=AX.X, apply_absolute_value=True), r=[tk_b], wadd=[tk_b], tiny=True)
                S.op("dve", lambda e: e.reduce_max(out=bias[:, 5:6], in_=gnb[:, 128:256], axis=AX.X, apply_absolute_value=True), wadd=[tk_b], tiny=True)
                S.op("dve", lambda e: e.tensor_tensor(out=bias[:, 6:7], in0=bias[:, 4:5], in1=bias[:, 5:6], op=ALU.mult), r=[tk_b], wadd=[tk_b], tiny=True)
                S.op("dve", lambda e: e.tensor_scalar(out=bias[:, 7:8], in0=bias[:, 6:7], scalar1=-float(np.sqrt(HD)), scalar2=None, op0=ALU.mult),
                     r=[tk_b], wadd=[tk_b], tiny=True)
                S.op("dve", lambda e: e.tensor_scalar(out=bias[:, 0:4], in0=cmt[:, 0:4], scalar1=bias[:, 7:8], scalar2=None, op0=ALU.add),
                     r=[tk_b, tk_const], wadd=[tk_b], tiny=True)
                scale = float(HD ** -0.5)
                items = []
                for g in range(NKV):
                    for qg in range(NT):
                        for hh in range(4):
                            for c in range(NKC):
                                items.append((g, qg, hh, c))
                LA = 2
                n_it = len(items)
                hn = 0
                for n in range(n_it + LA):
                    if n < n_it:
                        g, qg, hh, c = items[n]
                        gb = g % 2
                        hidx = (g * NT + qg) * 4 + hh
                        if qg == 0 and hh == 0 and c == 0:
                            S.dma("sp", "kt%d" % gb, kt[gb][:], KS[g * 128:(g + 1) * 128, :], w=[tk_kv[gb]])
                            S.dma("sp", "vt%d" % gb, vt[gb][:], VS.rearrange("p (g c d) -> p g c d", g=NKV, c=NKC)[:, g], wadd=[tk_kv[gb]])
                        if c == 0:
                            heads = [(g, qg, hh, hidx)] if n == 0 else []
                            if n + NKC < n_it:
                                g2, qg2, hh2, _ = items[n + NKC]
                                heads.append((g2, qg2, hh2, hidx + 1))
                            for (g2, qg2, hh2, hx) in heads:
                                qb2 = hx % 3
                                h2 = 4 * g2 + hh2
                                S.dma("sp", "qt%d" % qb2, qt[qb2][:], QS[h2 * 128:(h2 + 1) * 128, qg2 * T:(qg2 + 1) * T], w=[tk_qt[qb2]])
                        qb = hidx % 3
                        sbk = n % 3
                        S.op("pe", (lambda sbk=sbk, gb=gb, c=c, qb=qb: lambda e: e.matmul(
                            ps[sbk][:, :], kt[gb][:, c * 128:(c + 1) * 128], qt[qb][:, :], start=True, stop=True))(),
                            r=[tk_kv[gb], tk_qt[qb]], w=[ptk[sbk]])
                    m = n - LA
                    if m >= 0:
                        g, qg, hh, c = items[m]
                        gb = g % 2
                        hidx = (g * NT + qg) * 4 + hh
                        hp = hidx % 2
                        sbk = m % 3
                        pb = m % 4
                        bi = (c // (NKC // 2)) * 2 + (qg // (NT // 2))
                        S.op("act", (lambda sbk=sbk, pb=pb, bi=bi: lambda e: e.activation(
                            out=pt[pb][:, :], in_=ps[sbk][:, :], func=AF.Exp, bias=bias[:, bi:bi + 1], scale=scale))(),
                            r=[ptk[sbk], tk_b], w=[tk_pt[pb]])
                        S.op("pe", (lambda hp=hp, gb=gb, c=c, pb=pb: lambda e: e.matmul(
                            ps[3 + hp][:, :], vt[gb][:, c, :], pt[pb][:, :], start=(c == 0), stop=(c == NKC - 1)))(),
                            r=[tk_kv[gb], tk_pt[pb]], w=[ptk[3 + hp]])
                        if cfg.attn_sum_pe:
                            S.op("pe", (lambda hp=hp, pb=pb, c=c: lambda e: e.matmul(
                                ps[5 + hp][:, :], ones_1[:], pt[pb][:, :], start=(c == 0), stop=(c == NKC - 1)))(),
                                r=[tk_pt[pb], tk_const], w=[ptk[5 + hp]])
                        else:
                            aeng = "dve" if c % 2 == 0 else "pool"
                            acc = accd[hp] if c % 2 == 0 else accp[hp]
                            tka = tk_accd[hp] if c % 2 == 0 else tk_accp[hp]
                            if c < 2:
                                S.op(aeng, (lambda acc=acc, pb=pb: lambda e: e.tensor_copy(out=acc[:, :], in_=pt[pb][:, :]))(),
                                     r=[tk_pt[pb]], w=[tka])
                            else:
                                S.op(aeng, (lambda acc=acc, pb=pb: lambda e: e.tensor_tensor(out=acc[:, :], in0=acc[:, :], in1=pt[pb][:, :], op=ALU.add))(),
                                     r=[tk_pt[pb], tka], w=[tka])
                            if c == NKC - 1:
                                S.op("pe", (lambda hp=hp: lambda e: e.matmul(ps[5 + hp][:, :], ones_f[:], accd[hp][:, :], start=True, stop=False))(),
                                     r=[tk_accd[hp], tk_const], w=[ptk[5 + hp]])
                                S.op("pe", (lambda hp=hp: lambda e: e.matmul(ps[5 + hp][:, :], ones_f[:], accp[hp][:, :], start=False, stop=True))(),
                                     r=[tk_accp[hp], tk_const], w=[ptk[5 + hp]])
                        if c == NKC - 1:
                            h = 4 * g + hh
                            S.op("dve", (lambda hp=hp: lambda e: e.reciprocal(out=rcp[hp][:, :], in_=ps[5 + hp][:, :]))(),
                                 r=[ptk[5 + hp]], w=[tk_rcp[hp]])
                            S.op("dve", (lambda hp=hp: lambda e: e.tensor_tensor(out=ot[hp][:, :], in0=ps[3 + hp][:, :], in1=rcp[hp][:, :], op=ALU.mult))(),
                                 r=[ptk[3 + hp], tk_rcp[hp]], w=[tk_ot[hp]])
                            S.dma("pool", "ot%d" % hp, OS[h * 128:(h + 1) * 128, qg * T:(qg + 1) * T], ot[hp][:], r=[tk_ot[hp]])
                S.flush()
            with ExitStack() as ph:
                ps = psum_banks(ph)
                ptk = [Tok(True) for _ in range(8)]
                wo = sb(ph, "wo", [128, KC, D], BF16)
                tk_wo = Tok()
                oin = [sb(ph, "oin%d" % i, [128, KC, T], BF16) for i in range(2)]
                tk_oin = [Tok() for _ in range(2)]
                xr = [sb(ph, "xr%d" % i, [128, T], F32) for i in range(3)]
                tk_xr = [Tok() for _ in range(3)]
                yo = [sb(ph, "yo%d" % i, [128, T], F32) for i in range(3)]
                tk_yo = [Tok() for _ in range(3)]
                S.dma("sp", "wo", wo[:], wdst["wo", li].rearrange("p (k n) -> p k n", k=KC), w=[tk_wo])
                mn = 0
                for i in range(NT):
                    ob = i % 2
                    S.dma("sp", "oin%d" % ob, oin[ob][:], OS.rearrange("(h p) t -> p h t", p=128)[:, :, i * T:(i + 1) * T], w=[tk_oin[ob]])
                    for m in range(16):
                        b = mn % 3
                        pb = mn % 4
                        mn += 1
                        S.dma("pool", "xr%d" % b, xr[b][:], src[m * 128:(m + 1) * 128, i * T:(i + 1) * T], w=[tk_xr[b]])
                        for k in range(KC):
                            S.op("pe", (lambda pb=pb, k=k, m=m, ob=ob: lambda e: e.matmul(
                                ps[pb][:, :], wo[:, k, m * 128:(m + 1) * 128], oin[ob][:, k, :], start=(k == 0), stop=(k == KC - 1)))(),
                                r=[tk_wo, tk_oin[ob]], w=[ptk[pb]])
                        S.op("dve", (lambda pb=pb, b=b: lambda e: e.tensor_tensor(out=yo[b][:, :], in0=ps[pb][:, :], in1=xr[b][:, :], op=ALU.add))(),
                             r=[ptk[pb], tk_xr[b]], w=[tk_yo[b]])
                        S.dma("pool", "yo%d" % b, dst[m * 128:(m + 1) * 128, i * T:(i + 1) * T], yo[b][:], r=[tk_yo[b]])
                S.flush()

        cur_, nxt_ = XA, XB
        for kind, li in cfg.layers:
            if kind == "attn":
                phase_attn(li, cur_, nxt_)
            elif kind == "conv":
                phase_conv(li, cur_, nxt_)
            else:
                phase_ffn(li, cur_, nxt_)
            cur_, nxt_ = nxt_, cur_

        with ExitStack() as ph:
            ps = psum_banks(ph)
            ptk = [Tok(True) for _ in range(8)]
            xTt = [sb(ph, "xTt%d" % i, [128, KC, T], F32) for i in range(2)]
            tT = [Tok() for _ in range(2)]
            ytok = [sb(ph, "ytok%d" % i, [128, D], F32) for i in range(2)]
            tky = [[Tok() for _ in range(4)] for _ in range(2)]
            yn = 0
            for i in range(NT):
                b = i % 2
                S.dma("sp", "xTt%d" % b, xTt[b][:], XT(cur_)[:, :, i * T:(i + 1) * T], w=[tT[b]])
                for s in range(4):
                    yb = yn % 2
                    yn += 1
                    for kq in range(4):
                        pb = (s * 4 + kq) % 4
                        for q in range(4):
                            k = kq * 4 + q
                            S.op("pe", (lambda pb=pb, q=q, k=k, s=s, b=b: lambda e: e.transpose(
                                out=ps[pb][:, q * 128:(q + 1) * 128], in_=xTt[b][:, k, s * 128:(s + 1) * 128], identity=ident[:]))(),
                                r=[tT[b], tk_const], w=[ptk[pb]])
                        if kq % 2 == 0:
                            S.op("act", (lambda pb=pb, kq=kq, yb=yb: lambda e: e.copy(out=ytok[yb][:, kq * 512:(kq + 1) * 512], in_=ps[pb][:, :]))(),
                                 r=[ptk[pb]], w=[tky[yb][kq]])
                        else:
                            S.op("dve", (lambda pb=pb, kq=kq, yb=yb: lambda e: e.tensor_copy(out=ytok[yb][:, kq * 512:(kq + 1) * 512], in_=ps[pb][:, :]))(),
                                 r=[ptk[pb]], w=[tky[yb][kq]])
                    S.dma("pool", "ytok%d" % yb, yout[i * T + s * 128:i * T + (s + 1) * 128, :], ytok[yb][:], r=tky[yb])
            S.flush(final=True)
    return nc


def rope_tables(pos):
    pos = np.asarray(pos)
    row = (pos // 64).astype(np.float32)
    col = (pos % 64).astype(np.float32)
    freqs = (np.float32(10000.0) ** (-np.arange(32, dtype=np.float32) * np.float32(2.0) / np.float32(64.0))).astype(np.float32)
    ar = (row[None, :] * freqs[:, None]).astype(np.float32)
    ac = (col[None, :] * freqs[:, None]).astype(np.float32)
    C = np.concatenate([np.cos(ar), np.cos(ar), np.cos(ac), np.cos(ac)], axis=0).astype(np.float32)
    Sn = np.concatenate([-np.sin(ar), np.sin(ar), -np.sin(ac), np.sin(ac)], axis=0).astype(np.float32)
    return np.ascontiguousarray(C), np.ascontiguousarray(Sn)


def perm_matrix():
    P = np.zeros((128, 128), np.float32)
    for d in range(128):
        blk, r = divmod(d, 64)
        src = blk * 64 + (r + 32) % 64
        P[src, d] = 1.0
    return P


def core_consts(cfg, nseq):
    NT, NTOK = cfg.NT, cfg.NTOK
    L = NTOK // nseq
    pos = np.arange(NTOK) % L
    C, Sn = rope_tables(pos)
    cm = np.zeros((128, 4 + 2 * NT), np.float32)
    for kh in range(2):
        for qh in range(2):
            cm[:, kh * 2 + qh] = 0.0 if (nseq == 1 or kh == qh) else NEG
    for i in range(NT):
        cm[:, 4 + 2 * i] = 0.0 if (i * T) % L == 0 else 1.0
        cm[:, 5 + 2 * i] = 0.0 if ((i + 1) * T) % L == 0 else 1.0
    return C, Sn, cm


def shared_inputs(cfg, p):
    pvoff, npv = pv_layout(cfg)
    pv = np.zeros((128, npv), np.float32)

    def put(name, arr):
        pv[:, pvoff[name]:pvoff[name] + arr.shape[1]] = arr
    put("eps", np.full((128, 1), EPS, np.float32))
    m = {}
    gn = np.zeros((1, max(1, cfg.n_attn) * 256), np.float32)
    for a in range(cfg.n_attn):
        put("an%d" % a, _pcols(p["attn_norm"][a]))
        put("qg%d" % a, _pcols(p["attn_q_norm"][a]))
        put("kg%d" % a, _pcols(p["attn_k_norm"][a]))
        gn[0, a * 256:a * 256 + 128] = p["attn_q_norm"][a]
        gn[0, a * 256 + 128:a * 256 + 256] = p["attn_k_norm"][a]
        m["wqkv%d" % a] = lay_qkv(np.asarray(p["attn_w_qkv"][a], np.float32))
        m["wo%d" % a] = lay_rowmajor(np.asarray(p["attn_w_o"][a], np.float32))
    for c in range(cfg.n_conv):
        put("cn%d" % c, _pcols(p["conv_norm"][c]))
        put("bpw1%d" % c, _pcols(p["conv_b_pw1"][c]))
        wdw = np.asarray(p["conv_w_dw"][c], np.float32)
        put("wdw%d" % c, np.concatenate([_pcols(wdw[t]) for t in range(CK)], axis=1))
        put("bdw%d" % c, _pcols(p["conv_b_dw"][c]))
        put("lng%d" % c, _pcols(p["conv_ln_g"][c]))
        put("lnb%d" % c, _pcols(p["conv_ln_b"][c]))
        put("bpw2%d" % c, _pcols(p["conv_b_pw2"][c]))
        m["wpw1%d" % c] = lay_pw1(np.asarray(p["conv_w_pw1"][c], np.float32))
        m["wpw2%d" % c] = lay_rowmajor(np.asarray(p["conv_w_pw2"][c], np.float32))
    for i in range(cfg.n_ffn):
        put("fn%d" % i, _pcols(p["ffn_norm"][i]))
        fw = np.asarray(p["ffn_w_dw"][i], np.float32)
        put("fw%d" % i, np.concatenate([_pcols(fw[t]) for t in range(3)], axis=1))
        put("fb%d" % i, _pcols(p["ffn_b_dw"][i]))
        m["wup%d" % i] = lay_up(np.asarray(p["ffn_w_up"][i], np.float32))
        m["wdn%d" % i] = lay_dn(np.asarray(p["ffn_w_down"][i], np.float32))
    m["pv"] = pv
    m["gn"] = gn
    m["ident"] = np.eye(128, dtype=np.float32)
    m["perm"] = perm_matrix()
    return m


def kernel(**inputs):
    cfg = Cfg()
    p = {k: np.asarray(v) for k, v in inputs.items()}
    xp = np.asarray(p["x_prompt"], np.float32)
    xs = np.asarray(p["x_sample"], np.float32)
    shared = shared_inputs(cfg, p)
    in_maps = []
    for core in range(8):
        if core < 4:
            x = xp[2 * core:2 * core + 2].reshape(cfg.NTOK, D)
            C, Sn, cm = core_consts(cfg, 2)
        else:
            x = xs[core - 4].reshape(cfg.NTOK, D)
            C, Sn, cm = core_consts(cfg, 1)
        mm = dict(shared)
        mm["xin"] = np.ascontiguousarray(x)
        mm["ropec"] = C
        mm["ropes"] = Sn
        mm["cm"] = cm
        in_maps.append(mm)
    nc = build_program(cfg)
    res = run_bass_kernel_spmd(nc, in_maps, core_ids=list(range(8)))
    outs = [np.asarray(r["yout"], np.float32) for r in res.results]
    y_prompt = np.stack([o.reshape(2, 4096, D) for o in outs[:4]], axis=0).reshape(8, 4096, D)
    y_sample = np.stack([o.reshape(1, 8192, D) for o in outs[4:]], axis=0).reshape(4, 8192, D)
    return (y_prompt, y_sample)
```

```python
import numpy as np
from contextlib import ExitStack
import concourse.bass as bass
import concourse.mybir as mybir
from concourse.bass_utils import run_bass_kernel_spmd

F32 = mybir.dt.float32
BF16 = mybir.dt.bfloat16
AF = mybir.ActivationFunctionType
ALU = mybir.AluOpType
AX = mybir.AxisListType

D = 2048
KC = 16
T = 512
NH, NKV, HD = 16, 4, 128
DFF = 5632
JF = DFF // 128
CK = 31
EPS = 1e-6
NEG = -30000.0


class Op:
    __slots__ = ("eng", "fn", "deps", "ddeps", "signal", "idx", "dent", "dval", "bar", "tiny")


class Tok:
    __slots__ = ("w", "r", "x")

    def __init__(self, x=False):
        self.w = []
        self.r = {}
        self.x = x


ENGS = ("pe", "act", "dve", "pool", "sp")


class Sched:
    def __init__(self, nc, stack):
        self.nc = nc
        self.h = {"pe": nc.tensor, "act": nc.scalar, "dve": nc.vector, "pool": nc.gpsimd, "sp": nc.sync}
        self.sem = {e: stack.enter_context(nc.semaphore("s_" + e)) for e in ENGS}
        self.cnt = {e: 0 for e in ENGS}
        self.ops = {e: [] for e in ENGS}
        self.dfree = [[stack.enter_context(nc.semaphore("d%d" % i)), 0] for i in range(72)]
        self.dkeys = {}
        self.dused = []
        self.waited = {e: {} for e in ENGS}
        self.bar = None
        self.bar_pending = set()
        self.nops = 0

    def _add(self, o, p, raw):
        if p is o:
            return
        if p.dent is not None:
            o.ddeps.append((p.dent, p.dent[1]))
            return
        if p.eng == o.eng and o.dent is None:
            if not (raw and p.tiny and p.eng != "pe"):
                return
        p.signal = True
        o.deps.append(p)

    def _track(self, o, r, w, wadd):
        if any(t.x for t in r):
            w = list(w) + [t for t in r if t.x and t not in w]
            r = [t for t in r if not t.x]
        for t in r:
            for p in t.w:
                self._add(o, p, True)
        for t in w:
            for p in t.w:
                self._add(o, p, False)
            for p in t.r.values():
                self._add(o, p, False)
        for t in wadd:
            for p in t.r.values():
                self._add(o, p, False)
        key = o.eng if o.dent is None else id(o.dent)
        for t in r:
            t.r[key] = o
        for t in w:
            t.w = [o]
            t.r = {}
        for t in wadd:
            t.w.append(o)

    def _new(self, eng):
        o = Op()
        o.eng = eng
        o.deps = []
        o.ddeps = []
        o.signal = False
        o.dent = None
        o.tiny = False
        o.bar = None
        if eng in self.bar_pending:
            o.bar = self.bar
            self.bar_pending.discard(eng)
        self.ops[eng].append(o)
        self.nops += 1
        return o

    def op(self, eng, fn, r=(), w=(), wadd=(), tiny=False):
        o = self._new(eng)
        o.fn = fn
        o.tiny = tiny
        self._track(o, r, w, wadd)
        return o

    def dma(self, q, key, out, in_, r=(), w=(), wadd=(), slow=False):
        o = self._new(q)
        ent = self.dkeys.get(key)
        if ent is None:
            ent = self.dfree.pop()
            self.dkeys[key] = ent
            self.dused.append(ent)
        o.dent = ent
        self._track(o, r, w, wadd)
        ent[1] += 16
        o.dval = ent[1]
        if slow:
            o.fn = lambda e: e.dma_start(out=out, in_=in_, allow_slow_non_contiguous=True)
        else:
            o.fn = lambda e: e.dma_start(out=out, in_=in_)
        return o

    def flush(self, final=False):
        nc = self.nc
        last = []
        for e in ENGS:
            for o in reversed(self.ops[e]):
                if o.dent is None:
                    o.signal = True
                    last.append(o)
                    break
        for e in ENGS:
            c = self.cnt[e]
            for o in self.ops[e]:
                if o.dent is None and o.signal:
                    c += 1
                    o.idx = c
            self.cnt[e] = c
        bar_d = [(ent, ent[1]) for ent in self.dused]
        if final:
            o = self._new("sp")
            o.fn = None
            o.bar = (last, bar_d)
        with nc.Block() as block:
            for e in ENGS:
                if self.ops[e]:
                    getattr(block, {"pe": "tensor", "act": "scalar", "dve": "vector", "pool": "gpsimd",
                                    "sp": "sync"}[e])(self._emitter(e))
        for e in ENGS:
            self.ops[e] = []
        self.bar = (last, bar_d)
        self.bar_pending = set(ENGS)
        self.dfree.extend(self.dused)
        self.dused = []
        self.dkeys = {}

    def _emitter(self, e):
        ops = self.ops[e]
        sem = self.sem
        waited = self.waited[e]
        mysem = sem[e]

        import os
        dbg = os.environ.get("SCHED_DBG")
        log = []

        def wait(h, s, v):
            k = id(s)
            if waited.get(k, 0) >= v:
                return
            waited[k] = v
            if dbg:
                log.append("   %s wait %s >= %d" % (e, getattr(s, "name", str(s)), v))
            h.wait_ge(s, v)

        def run(h):
            for o in ops:
                if o.bar is not None:
                    for p in o.bar[0]:
                        if p.eng != e:
                            wait(h, sem[p.eng], p.idx)
                    for ent, v in o.bar[1]:
                        wait(h, ent[0], v)
                for p in o.deps:
                    wait(h, sem[p.eng], p.idx)
                for ent, v in o.ddeps:
                    wait(h, ent[0], v)
                if o.fn is None:
                    continue
                ins = o.fn(h)
                if dbg and len(log) < 400:
                    log.append("%s op#%d %s%s" % (e, len(log), "DMA->%s=%d" % (getattr(o.dent[0], "name", "?"), o.dval) if o.dent is not None else "", " sig%d" % o.idx if (o.dent is None and o.signal) else ""))
                    if len(log) >= 400:
                        print("\n".join(log))
                if o.dent is not None:
                    ins.then_inc(o.dent[0], 16)
                elif o.signal:
                    ins.then_inc(mysem, 1)
        return run


def _pcols(v):
    v = np.asarray(v, np.float32).reshape(-1, 128)
    return np.ascontiguousarray(v.T)


class PV:
    def __init__(self):
        self.off = {}
        self.cols = []
        self.n = 0

    def add(self, name, arr2d):
        self.off[name] = self.n
        self.cols.append(np.asarray(arr2d, np.float32))
        self.n += arr2d.shape[1]

    def pack(self):
        return np.ascontiguousarray(np.concatenate(self.cols, axis=1))


def lay_blocks(W, cols_of_block):
    K = W.shape[0]
    kcn = K // 128
    out = []
    for cols in cols_of_block:
        blk = W[:, cols].reshape(kcn, 128, len(cols)).transpose(1, 0, 2)
        out.append(blk)
    return np.stack(out, axis=1)


def lay_qkv(W):
    qk = lay_blocks(W, [np.arange(c * 128, (c + 1) * 128) for c in range(20)]).reshape(128, -1)
    v = lay_blocks(W, [np.arange(2560, 3072)]).reshape(128, -1)
    return np.ascontiguousarray(np.concatenate([qk, v], axis=1))


def lay_rowmajor(W):
    return np.ascontiguousarray(lay_blocks(W, [np.arange(W.shape[1])]).reshape(128, -1))


def lay_pw1(W):
    blocks = []
    for j in range(16):
        for ag in range(2):
            blocks.append(np.arange(ag * 2048 + j * 128, ag * 2048 + (j + 1) * 128))
    return np.ascontiguousarray(lay_blocks(W, blocks).reshape(128, -1))


def lay_up(W):
    blocks = []
    for j in range(JF):
        for gv in range(2):
            blocks.append(np.arange(gv * DFF + j * 128, gv * DFF + (j + 1) * 128))
    return np.ascontiguousarray(lay_blocks(W, blocks).reshape(128, -1))


def lay_dn(W):
    return np.ascontiguousarray(lay_blocks(W, [np.arange(m * 128, (m + 1) * 128) for m in range(16)]).reshape(128, -1))


class Cfg:
    def __init__(self, NT=16, layers=None):
        self.NT = NT
        self.NTOK = NT * T
        self.NKC = self.NTOK // 128
        if layers is None:
            layers = []
            for i in range(4):
                layers.append(("attn" if i % 2 == 0 else "conv", i // 2))
                layers.append(("ffn", i))
        self.layers = layers
        self.n_attn = 1 + max([i for k, i in layers if k == "attn"], default=-1)
        self.n_conv = 1 + max([i for k, i in layers if k == "conv"], default=-1)
        self.n_ffn = 1 + max([i for k, i in layers if k == "ffn"], default=-1)
        self.conv_pool_every = 0
        self.debug = False
        self.attn_sum_pe = False
        self.ffn_norm_j = 12


def pv_layout(cfg):
    off = {}
    n = 0

    def add(name, w):
        nonlocal n
        off[name] = n
        n += w
    add("eps", 1)
    for a in range(cfg.n_attn):
        add("an%d" % a, 16)
        add("qg%d" % a, 1)
        add("kg%d" % a, 1)
    for c in range(cfg.n_conv):
        add("cn%d" % c, 16)
        add("bpw1%d" % c, 32)
        add("wdw%d" % c, CK * 16)
        add("bdw%d" % c, 16)
        add("lng%d" % c, 16)
        add("lnb%d" % c, 16)
        add("bpw2%d" % c, 16)
    for i in range(cfg.n_ffn):
        add("fn%d" % i, 16)
        add("fw%d" % i, 3 * 2 * JF)
        add("fb%d" % i, 2 * JF)
    return off, n


def build_program(cfg):
    NT, NTOK, NKC = cfg.NT, cfg.NTOK, cfg.NKC
    nc = bass.Bass("TRN2", target_bir_lowering=False)
    pvoff, npv = pv_layout(cfg)

    def din(name, shape, dt=F32):
        return nc.dram_tensor(name, list(shape), dt, kind="ExternalInput").ap()

    def dscr(name, shape, dt):
        return nc.dram_tensor(name, list(shape), dt).ap()

    xin = din("xin", [NTOK, D])
    yout = nc.dram_tensor("yout", [NTOK, D], F32, kind="ExternalOutput").ap()
    pv_d = din("pv", [128, npv])
    cm_d = din("cm", [128, 4 + 2 * NT])
    ropec_d = din("ropec", [128, NTOK])
    ropes_d = din("ropes", [128, NTOK])
    ident_d = din("ident", [128, 128])
    perm_d = din("perm", [128, 128])
    gn_d = din("gn", [1, max(1, cfg.n_attn) * 256])
    wsrc, wdst = {}, {}
    WL = {"wqkv": 49152, "wo": 32768, "wpw1": 65536, "wpw2": 32768, "wup": 2 * JF * 2048, "wdn": 16 * JF * 128}
    for a in range(cfg.n_attn):
        for nm in ("wqkv", "wo"):
            wsrc[nm, a] = din("%s%d" % (nm, a), [128, WL[nm]])
            wdst[nm, a] = dscr("b_%s%d" % (nm, a), [128, WL[nm]], BF16)
    for c in range(cfg.n_conv):
        for nm in ("wpw1", "wpw2"):
            wsrc[nm, c] = din("%s%d" % (nm, c), [128, WL[nm]])
            wdst[nm, c] = dscr("b_%s%d" % (nm, c), [128, WL[nm]], BF16)
    for i in range(cfg.n_ffn):
        for nm in ("wup", "wdn"):
            wsrc[nm, i] = din("%s%d" % (nm, i), [128, WL[nm]])
            wdst[nm, i] = dscr("b_%s%d" % (nm, i), [128, WL[nm]], BF16)
    XA = dscr("XA", [D, NTOK], F32)
    XB = dscr("XB", [D, NTOK], F32)
    G = dscr("Gs", [D, NTOK], F32)
    QS = dscr("QS", [NH * HD, NTOK], BF16)
    KS = dscr("KS", [NKV * HD, NTOK], BF16)
    VS = dscr("VS", [128, NKV * NKC * HD], BF16)
    OS = dscr("OS", [NH * HD, NTOK], BF16)

    if cfg.debug:
        dbg_xn = nc.dram_tensor("dbg_xn", [128, KC * 514], BF16, kind="ExternalOutput").ap()
        dbg_h = nc.dram_tensor("dbg_h", [128, JF * T], BF16, kind="ExternalOutput").ap()
        dbg_rs = nc.dram_tensor("dbg_rs", [128, 514], F32, kind="ExternalOutput").ap()
        dbg_wu = nc.dram_tensor("dbg_wu", [128, 2 * KC * 128], BF16, kind="ExternalOutput").ap()
        dbg_tt = nc.dram_tensor("dbg_tt", [128, 3 * T], F32, kind="ExternalOutput").ap()
        dbg_pm = nc.dram_tensor("dbg_pm", [128, 4 * T], F32, kind="ExternalOutput").ap()
    top = ExitStack()
    with top:
        S = Sched(nc, top)

        uid = [0]

        def sb(stack, name, shape, dt):
            uid[0] += 1
            return stack.enter_context(nc.sbuf_tensor("%s_u%d" % (name, uid[0]), list(shape), dt))

        pvt = sb(top, "pvt", [128, npv], F32)
        cmt = sb(top, "cmt", [128, 4 + 2 * NT], F32)
        ident = sb(top, "ident", [128, 128], F32)
        permf = sb(top, "permf", [128, 128], F32)
        permb = sb(top, "permb", [128, 128], BF16)
        ones_d = sb(top, "ones_d", [128, 128], BF16)
        ones_h = sb(top, "ones_h", [128, 128], BF16)
        ones_1 = sb(top, "ones_1", [128, 128], BF16)
        ones_f = sb(top, "ones_f", [128, 128], F32)
        tk_const = Tok()

        def pvc(name, c=0, n=1):
            o = pvoff[name] + c
            return pvt[:, o:o + n]

        eps_ap = pvc("eps")

        def psum_banks(stack):
            uid[0] += 1
            return [stack.enter_context(nc.psum_tensor("ps%d_u%d" % (i, uid[0]), [128, 512], F32)) for i in range(8)]

        with ExitStack() as ph:
            S.dma("sp", "c0", pvt[:], pv_d[:, :], w=[tk_const])
            S.dma("sp", "c1", cmt[:], cm_d[:, :], wadd=[tk_const])
            S.dma("sp", "c2", ident[:], ident_d[:, :], wadd=[tk_const])
            S.dma("sp", "c3", permf[:], perm_d[:, :], wadd=[tk_const])
            S.op("dve", lambda e: e.tensor_copy(out=permb[:], in_=permf[:]), r=[tk_const], wadd=[tk_const])
            S.op("dve", lambda e: e.memset(ones_d[:], 1.0 / D), wadd=[tk_const])
            S.op("dve", lambda e: e.memset(ones_h[:], 1.0 / HD), wadd=[tk_const])
            S.op("dve", lambda e: e.memset(ones_1[:], 1.0), wadd=[tk_const])
            S.op("dve", lambda e: e.memset(ones_f[:], 1.0), wadd=[tk_const])
            CH = 8192
            NB_ = 3
            stg = [sb(ph, "stg%d" % i, [128, CH], F32) for i in range(NB_)]
            obf = [sb(ph, "obf%d" % i, [128, CH], BF16) for i in range(NB_)]
            tks = [Tok() for _ in range(NB_)]
            tko = [Tok() for _ in range(NB_)]
            ci = 0

            def cast_region(src, dst, l0, l1, gname, kcn, cb):
                nonlocal ci
                per = kcn * cb
                pos = l0
                while pos < l1:
                    n = min(CH, l1 - pos)
                    b = ci % NB_
                    ci += 1
                    S.dma("sp", "stg%d" % b, stg[b][:, 0:n], src[:, pos:pos + n], w=[tks[b]])
                    if gname is None:
                        S.op("act", (lambda b=b, n=n: lambda e: e.copy(out=obf[b][:, 0:n], in_=stg[b][:, 0:n]))(),
                             r=[tks[b]], w=[tko[b]])
                    else:
                        rel = pos - l0
                        if n >= per:
                            assert n % per == 0 and rel % per == 0
                            reps, k0, nk = n // per, 0, kcn
                        else:
                            assert per % n == 0 and n % cb == 0
                            reps, k0, nk = 1, (rel % per) // cb, n // cb
                        go = pvoff[gname] + k0

                        def f(e, b=b, n=n, reps=reps, nk=nk, go=go):
                            i0 = stg[b][:, 0:n].rearrange("p (r k c) -> p r k c", r=reps, k=nk)
                            o0 = obf[b][:, 0:n].rearrange("p (r k c) -> p r k c", r=reps, k=nk)
                            g = pvt[:, go:go + nk].unsqueeze(1).unsqueeze(3).broadcast_to([128, reps, nk, cb])
                            return e.tensor_tensor(out=o0, in0=i0, in1=g, op=ALU.mult)
                        S.op("dve", f, r=[tks[b], tk_const], w=[tko[b]])
                    S.dma("pool", "obf%d" % b, dst[:, pos:pos + n], obf[b][:, 0:n], r=[tko[b]])
                    pos += n

            for a in range(cfg.n_attn):
                cast_region(wsrc["wqkv", a], wdst["wqkv", a], 0, 40960, "an%d" % a, 16, 128)
                cast_region(wsrc["wqkv", a], wdst["wqkv", a], 40960, 49152, "an%d" % a, 16, 512)
                cast_region(wsrc["wo", a], wdst["wo", a], 0, WL["wo"], None, 0, 0)
            for c in range(cfg.n_conv):
                cast_region(wsrc["wpw1", c], wdst["wpw1", c], 0, WL["wpw1"], "cn%d" % c, 16, 128)
                cast_region(wsrc["wpw2", c], wdst["wpw2", c], 0, WL["wpw2"], None, 0, 0)
            for i in range(cfg.n_ffn):
                cast_region(wsrc["wup", i], wdst["wup", i], 0, WL["wup"], "fn%d" % i, 16, 128)
                cast_region(wsrc["wdn", i], wdst["wdn", i], 0, WL["wdn"], None, 0, 0)
            S.flush()
            tk_const.w = []

        XT = lambda X: X.rearrange("(k p) t -> p k t", p=128)

        with ExitStack() as ph:
            ps = psum_banks(ph)
            ptk = [Tok(True) for _ in range(8)]
            xtok = [sb(ph, "xtok%d" % i, [128, 4, D], F32) for i in range(2)]
            ttok = [Tok() for _ in range(2)]
            xTt = [sb(ph, "xTt%d" % i, [128, KC, T], F32) for i in range(2)]
            tT = [[Tok() for _ in range(KC)] for _ in range(2)]
            for i in range(NT):
                b = i % 2
                S.dma("sp", "xtok%d" % b, xtok[b][:], xin[i * T:(i + 1) * T, :].rearrange("(s p) d -> p s d", p=128),
                      w=[ttok[b]])
                for k in range(KC):
                    pb = k % 4
                    for s in range(4):
                        S.op("pe", (lambda pb=pb, s=s, k=k, b=b: lambda e: e.transpose(
                            out=ps[pb][:, s * 128:(s + 1) * 128], in_=xtok[b][:, s, k * 128:(k + 1) * 128],
                            identity=ident[:]))(), r=[ttok[b], tk_const], w=[ptk[pb]])
                    if k % 2 == 0:
                        S.op("act", (lambda pb=pb, k=k, b=b: lambda e: e.copy(out=xTt[b][:, k, :], in_=ps[pb][:, :]))(),
                             r=[ptk[pb]], w=[tT[b][k]])
                    else:
                        S.op("dve", (lambda pb=pb, k=k, b=b: lambda e: e.tensor_copy(out=xTt[b][:, k, :], in_=ps[pb][:, :]))(),
                             r=[ptk[pb]], w=[tT[b][k]])
                S.dma("pool", "xTt%d" % b, XT(XA)[:, :, i * T:(i + 1) * T], xTt[b][:], r=tT[b])
            S.flush()

        def norm_tile(ph_bufs, src, i, W, xn_b, tk_xn, emask):
            xt, tk_xt, sq, tk_sq, rs, tk_rs, pst, tk_pst, psh, tk_psh, halo_bufs = ph_bufs
            X = XT(src)
            S.dma("sp", "xt", xt[:, :, 0:T], X[:, :, i * T:(i + 1) * T], w=[tk_xt])
            if W == 514:
                hl, hr, tk_hl, tk_hr, tk_xh = halo_bufs
                if i > 0:
                    S.dma("pool", "hl", hl[:], X[:, :, i * T - 16:i * T], w=[tk_hl])
                    S.op("dve", lambda e: e.tensor_copy(out=xt[:, :, 512:513], in_=hl[:, :, 15:16]), r=[tk_hl], w=[tk_xh])
                else:
                    S.op("dve", lambda e: e.memset(xt[:, :, 512:513], 0.0), w=[tk_xh])
                if i < NT - 1:
                    S.dma("pool", "hr", hr[:], X[:, :, (i + 1) * T:(i + 1) * T + 16], w=[tk_hr])
                    S.op("dve", lambda e: e.tensor_copy(out=xt[:, :, 513:514], in_=hr[:, :, 0:1]), r=[tk_hr], w=[tk_xh], tiny=True)
                else:
                    S.op("dve", lambda e: e.memset(xt[:, :, 513:514], 0.0), w=[tk_xh], tiny=True)
            for k in range(KC):
                b = k % 3
                S.op("act", (lambda k=k, b=b: lambda e: e.activation(out=sq[b][:, 0:W], in_=xt[:, k, 0:W], func=AF.Square))(),
                     r=[tk_xt] + ([halo_bufs[4]] if W == 514 else []), w=[tk_sq[b]])
                S.op("pe", (lambda k=k, b=b: lambda e: e.matmul(pst[:, :], ones_d[:], sq[b][:, 0:T], start=(k == 0), stop=(k == KC - 1)))(),
                     r=[tk_sq[b], tk_const], w=[tk_pst])
                if W == 514:
                    S.op("pe", (lambda k=k, b=b: lambda e: e.matmul(psh[:, :], ones_d[:], sq[b][:, 2:514], start=(k == 0), stop=(k == KC - 1)))(),
                         r=[tk_sq[b], tk_const], w=[tk_psh])
            S.op("act", lambda e: e.activation(out=rs[:, 0:T], in_=pst[:, :], func=AF.Sqrt, bias=eps_ap, scale=1.0),
                 r=[tk_const], w=[tk_rs, tk_pst])
            if W == 514:
                S.op("act", lambda e: e.activation(out=rs[:, 2:514], in_=psh[:, :], func=AF.Sqrt, bias=eps_ap, scale=1.0),
                     r=[tk_psh, tk_const], wadd=[tk_rs])
            S.op("dve", lambda e: e.reciprocal(out=rs[:, 0:T], in_=rs[:, 0:T]), r=[tk_rs], w=[tk_rs], tiny=True)
            if W == 514:
                S.op("dve", lambda e: e.reciprocal(out=rs[:, 512:514], in_=rs[:, 512:514]), r=[tk_rs], w=[tk_rs], tiny=True)
            if W == 514:
                S.op("dve", lambda e: e.tensor_tensor(out=rs[:, 512:514], in0=rs[:, 512:514], in1=cmt[:, 4 + 2 * i:6 + 2 * i], op=ALU.mult),
                     r=[tk_const, tk_rs], w=[tk_rs], tiny=True)
            for q in range(4):
                S.op("dve", (lambda q=q: lambda e: e.tensor_tensor(
                    out=xn_b[:, 4 * q:4 * q + 4, 0:W], in0=xt[:, 4 * q:4 * q + 4, 0:W],
                    in1=rs[:, 0:W].unsqueeze(1).broadcast_to([128, 4, W]), op=ALU.mult))(),
                     r=[tk_xt, tk_rs] + ([halo_bufs[4]] if W == 514 else []), w=[tk_xn[q]])

        def norm_bufs(ph, W, pst, tk_pst, psh, tk_psh):
            xt = sb(ph, "xt", [128, KC, W], F32)
            sq = [sb(ph, "sq%d" % i, [128, W], BF16) for i in range(3)]
            rs = sb(ph, "rs", [128, W], F32)
            hb = None
            if W == 514:
                hb = (sb(ph, "hl", [128, KC, 16], F32), sb(ph, "hr", [128, KC, 16], F32), Tok(), Tok(), Tok())
            return (xt, Tok(), sq, [Tok() for _ in range(3)], rs, Tok(), pst, tk_pst, psh, tk_psh, hb)

        def phase_ffn(li, src, dst):
            with ExitStack() as ph:
                ps = psum_banks(ph)
                ptk = [Tok(True) for _ in range(8)]
                nb = norm_bufs(ph, 514, ps[6], ptk[6], ps[7], ptk[7])
                xn = [sb(ph, "xn%d" % i, [128, KC, 514], BF16) for i in range(2)]
                tk_xn = [[Tok() for _ in range(5)] for _ in range(2)]
                hbuf = sb(ph, "hbuf", [128, JF, T], BF16)
                tk_h = [Tok() for _ in range(JF)]
                wu = [sb(ph, "wu%d" % i, [128, 2, KC, 128], BF16) for i in range(3)]
                tk_wu = [Tok() for _ in range(3)]
                wd = [sb(ph, "wd%d" % i, [128, JF, 128], BF16) for i in range(2)]
                tk_wd = [Tok() for _ in range(2)]
                tt = [[sb(ph, "tt%d%d" % (g, i), [128, T], F32) for i in range(2)] for g in range(2)]
                tk_tt = [[Tok() for _ in range(2)] for _ in range(2)]
                sg = [sb(ph, "sg%d" % i, [128, T], F32) for i in range(2)]
                tk_sg = [Tok() for _ in range(2)]
                xr = [sb(ph, "xr%d" % i, [128, T], F32) for i in range(2)]
                tk_xr = [Tok() for _ in range(2)]
                yo = [sb(ph, "yo%d" % i, [128, T], F32) for i in range(2)]
                tk_yo = [Tok() for _ in range(2)]
                wup = wdst["wup", li].rearrange("p (j g k c) -> p j g k c", j=JF, g=2, k=KC)
                wdn = wdst["wdn", li].rearrange("p (m j c) -> p m j c", m=16, j=JF)
                fw, fb = pvoff["fw%d" % li], pvoff["fb%d" % li]
                norm_tile(nb, src, 0, 514, xn[0], tk_xn[0], True)
                if cfg.debug:
                    S.dma("pool", "dbg3", dbg_rs, nb[4][:], r=[nb[5]] + tk_xn[0])
                jn = 0
                for i in range(NT):
                    cur = i % 2
                    xc = xn[cur]
                    for j in range(JF):
                        b = jn % 3
                        jp = jn % 2
                        jn += 1
                        S.dma("sp", "wu%d" % b, wu[b][:], wup[:, j], w=[tk_wu[b]])
                        for gv in range(2):
                            pm = ps[gv * 2 + jp]
                            hoff = gv * 2
                            hb_ = ps[4 + jp]
                            tkh = ptk[4 + jp]
                            for k in range(KC):
                                S.op("pe", (lambda pm=pm, b=b, gv=gv, k=k, xc=xc: lambda e: e.matmul(
                                    pm[:, :], wu[b][:, gv, k, :], xc[:, k, 0:T], start=(k == 0), stop=(k == KC - 1)))(),
                                    r=[tk_wu[b], tk_xn[cur][k // 4]], w=[ptk[gv * 2 + jp]])
                                S.op("pe", (lambda hoff=hoff, hb_=hb_, b=b, gv=gv, k=k, xc=xc: lambda e: e.matmul(
                                    hb_[:, hoff:hoff + 2], wu[b][:, gv, k, :], xc[:, k, 512:514], start=(k == 0), stop=(k == KC - 1)))(),
                                    r=[tk_wu[b], tk_xn[cur][k // 4]], w=[tkh])
                        if cfg.debug and i == 0 and j in (0, 20):
                            dj = 0 if j == 0 else 1
                            if j == 0:
                                dpm = sb(ph, "dpm", [128, 4 * T], F32)
                                tk_dpm = Tok()
                            S.op("act", (lambda jp=jp, dj=dj: lambda e: e.copy(out=dpm[:, dj * 2 * T:dj * 2 * T + T], in_=ps[jp][:, :]))(),
                                 r=[ptk[jp]], w=[tk_dpm])
                            S.op("act", (lambda jp=jp, dj=dj: lambda e: e.copy(out=dpm[:, dj * 2 * T + T:dj * 2 * T + 2 * T], in_=ps[2 + jp][:, :]))(),
                                 r=[ptk[2 + jp]], w=[tk_dpm])
                            if j == 20:
                                S.dma("pool", "dbg8", dbg_pm, dpm[:], r=[tk_dpm])
                        for gv in range(2):
                            pm = ps[gv * 2 + jp]
                            hoff = gv * 2
                            hb_ = ps[4 + jp]
                            tkh = ptk[4 + jp]
                            col = gv * JF + j
                            t_ = tt[gv][jp]
                            w0 = pvt[:, fw + col:fw + col + 1]
                            w1 = pvt[:, fw + 2 * JF + col:fw + 2 * JF + col + 1]
                            w2 = pvt[:, fw + 4 * JF + col:fw + 4 * JF + col + 1]
                            bb = pvt[:, fb + col:fb + col + 1]
                            tk = tk_tt[gv][jp]
                            S.op("act", (lambda t_=t_, pm=pm, w1=w1, bb=bb: lambda e: e.activation(
                                out=t_[:, :], in_=pm[:, :], func=AF.Identity, bias=bb, scale=w1))(),
                                r=[ptk[gv * 2 + jp], tk_const], w=[tk])
                            S.op("dve", (lambda t_=t_, pm=pm, w0=w0: lambda e: e.scalar_tensor_tensor(
                                out=t_[:, 1:T], in0=pm[:, 0:T - 1], scalar=w0, in1=t_[:, 1:T], op0=ALU.mult, op1=ALU.add))(),
                                r=[ptk[gv * 2 + jp]], w=[tk])
                            S.op("dve", (lambda t_=t_, pm=pm, w2=w2: lambda e: e.scalar_tensor_tensor(
                                out=t_[:, 0:T - 1], in0=pm[:, 1:T], scalar=w2, in1=t_[:, 0:T - 1], op0=ALU.mult, op1=ALU.add))(),
                                r=[ptk[gv * 2 + jp]], w=[tk])
                            S.op("dve", (lambda t_=t_, hoff=hoff, hb_=hb_, w0=w0: lambda e: e.scalar_tensor_tensor(
                                out=t_[:, 0:1], in0=hb_[:, hoff:hoff + 1], scalar=w0, in1=t_[:, 0:1], op0=ALU.mult, op1=ALU.add))(),
                                w=[tk, tkh])
                            S.op("dve", (lambda t_=t_, hoff=hoff, hb_=hb_, w2=w2: lambda e: e.scalar_tensor_tensor(
                                out=t_[:, T - 1:T], in0=hb_[:, hoff + 1:hoff + 2], scalar=w2, in1=t_[:, T - 1:T], op0=ALU.mult, op1=ALU.add))(),
                                w=[tk, tkh], tiny=True)
                        S.op("act", (lambda jp=jp: lambda e: e.activation(out=sg[jp][:, :], in_=tt[0][jp][:, :], func=AF.Silu))(),
                             r=[tk_tt[0][jp]], w=[tk_sg[jp]])
                        S.op("dve", (lambda jp=jp, j=j: lambda e: e.tensor_tensor(
                            out=hbuf[:, j, :], in0=sg[jp][:, :], in1=tt[1][jp][:, :], op=ALU.mult))(),
                            r=[tk_sg[jp], tk_tt[1][jp]], w=[tk_h[j]])
                        if cfg.debug and i == 0 and j == 0:
                            S.dma("pool", "dbg4", dbg_wu.rearrange("p (g k c) -> p g k c", g=2, k=KC), wu[b][:], r=[tk_wu[b]])
                            S.dma("pool", "dbg5", dbg_tt[:, 0:T], tt[0][jp][:], r=[tk_tt[0][jp], tk_h[j]])
                            S.dma("pool", "dbg6", dbg_tt[:, T:2 * T], tt[1][jp][:], r=[tk_tt[1][jp], tk_h[j]])
                            S.dma("pool", "dbg7", dbg_tt[:, 2 * T:3 * T], sg[jp][:], r=[tk_sg[jp], tk_h[j]])
                        if j == cfg.ffn_norm_j and i + 1 < NT:
                            norm_tile(nb, src, i + 1, 514, xn[1 - cur], tk_xn[1 - cur], True)
                    if cfg.debug and i == 0:
                        S.dma("pool", "dbg1", dbg_xn.rearrange("p (k t) -> p k t", k=KC), xn[0][:], r=tk_xn[0])
                        S.dma("pool", "dbg2", dbg_h.rearrange("p (j t) -> p j t", j=JF), hbuf[:], r=tk_h)
                    for m in range(16):
                        b = m % 2
                        S.dma("sp", "wd%d" % b, wd[b][:], wdn[:, m], w=[tk_wd[b]])
                        S.dma("pool", "xr%d" % b, xr[b][:], src[m * 128:(m + 1) * 128, i * T:(i + 1) * T], w=[tk_xr[b]])
                        for jc in range(JF):
                            S.op("pe", (lambda b=b, jc=jc: lambda e: e.matmul(
                                ps[6 + b][:, :], wd[b][:, jc, :], hbuf[:, jc, :], start=(jc == 0), stop=(jc == JF - 1)))(),
                                r=[tk_wd[b], tk_h[jc]], w=[ptk[6 + b]])
                        S.op("dve", (lambda b=b: lambda e: e.tensor_tensor(out=yo[b][:, :], in0=ps[6 + b][:, :], in1=xr[b][:, :], op=ALU.add))(),
                             r=[ptk[6 + b], tk_xr[b]], w=[tk_yo[b]])
                        S.dma("pool", "yo%d" % b, dst[m * 128:(m + 1) * 128, i * T:(i + 1) * T], yo[b][:], r=[tk_yo[b]])
                S.flush()

        def phase_conv(li, src, dst):
            with ExitStack() as ph:
                ps = psum_banks(ph)
                ptk = [Tok(True) for _ in range(8)]
                nb = norm_bufs(ph, T, ps[5], ptk[5], None, None)
                xn = [sb(ph, "xn%d" % i, [128, KC, T], BF16) for i in range(2)]
                tk_xn = [[Tok() for _ in range(5)] for _ in range(2)]
                wp = [sb(ph, "wp%d" % i, [128, 2, KC, 128], BF16) for i in range(3)]
                tk_wp = [Tok() for _ in range(3)]
                sig = [sb(ph, "sig%d" % i, [128, T], F32) for i in range(2)]
                tk_sig = [Tok() for _ in range(2)]
                gl = [sb(ph, "gl%d" % i, [128, T], F32) for i in range(2)]
                tk_gl = [Tok() for _ in range(2)]
                wsrc_ = wdst["wpw1", li].rearrange("p (j g k c) -> p j g k c", j=16, g=2, k=KC)
                bo = pvoff["bpw1%d" % li]
                norm_tile(nb, src, 0, T, xn[0], tk_xn[0], False)
                jn = 0
                for i in range(NT):
                    cur = i % 2
                    xc = xn[cur]
                    for j in range(16):
                        b = jn % 3
                        jp = jn % 2
                        jn += 1
                        S.dma("sp", "wp%d" % b, wp[b][:], wsrc_[:, j], w=[tk_wp[b]])
                        for ag in range(2):
                            for k in range(KC):
                                S.op("pe", (lambda ag=ag, jp=jp, b=b, k=k, xc=xc: lambda e: e.matmul(
                                    ps[ag * 2 + jp][:, :], wp[b][:, ag, k, :], xc[:, k, :], start=(k == 0), stop=(k == KC - 1)))(),
                                    r=[tk_wp[b], tk_xn[cur][k // 4]], w=[ptk[ag * 2 + jp]])
                        S.op("act", (lambda jp=jp, j=j: lambda e: e.activation(
                            out=sig[jp][:, :], in_=ps[2 + jp][:, :], func=AF.Sigmoid, bias=pvt[:, bo + 16 + j:bo + 17 + j], scale=1.0))(),
                            r=[ptk[2 + jp], tk_const], w=[tk_sig[jp]])
                        S.op("dve", (lambda jp=jp, j=j: lambda e: e.scalar_tensor_tensor(
                            out=gl[jp][:, :], in0=ps[jp][:, :], scalar=pvt[:, bo + j:bo + j + 1], in1=sig[jp][:, :], op0=ALU.add, op1=ALU.mult))(),
                            r=[ptk[jp], tk_sig[jp]], w=[tk_gl[jp]])
                        S.dma("pool", "gl%d" % jp, G[j * 128:(j + 1) * 128, i * T:(i + 1) * T], gl[jp][:], r=[tk_gl[jp]])
                        if j == 6 and i + 1 < NT:
                            norm_tile(nb, src, i + 1, T, xn[1 - cur], tk_xn[1 - cur], False)
                S.flush()
            with ExitStack() as ph:
                ps = psum_banks(ph)
                ptk = [Tok(True) for _ in range(8)]
                WG = T + CK - 1
                HW = CK // 2
                ge = [sb(ph, "ge%d" % i, [128, WG], F32) for i in range(4)]
                tk_ge = [Tok() for _ in range(4)]
                cb = sb(ph, "cb", [128, KC, T], F32)
                tk_c = [Tok() for _ in range(KC)]
                cbf = [sb(ph, "cbf%d" % i, [128, T], BF16) for i in range(3)]
                tk_cbf = [Tok() for _ in range(3)]
                csq = [sb(ph, "csq%d" % i, [128, T], BF16) for i in range(3)]
                tk_csq = [Tok() for _ in range(3)]
                mu = sb(ph, "mu", [128, T], F32)
                rstd = sb(ph, "rstd", [128, T], F32)
                tk_st = Tok()
                tmp = [sb(ph, "tmp%d" % i, [128, T], F32) for i in range(2)]
                tk_tmp = [Tok() for _ in range(2)]
                sbf = sb(ph, "sbf", [128, KC, T], BF16)
                tk_s = [Tok() for _ in range(KC)]
                w2 = sb(ph, "w2", [128, KC, D], BF16)
                tk_w2 = Tok()
                xr = [sb(ph, "xr%d" % i, [128, T], F32) for i in range(2)]
                tk_xr = [Tok() for _ in range(2)]
                yo = [sb(ph, "yo%d" % i, [128, T], F32) for i in range(2)]
                tk_yo = [Tok() for _ in range(2)]
                S.dma("sp", "w2", w2[:], wdst["wpw2", li].rearrange("p (k n) -> p k n", k=KC), w=[tk_w2])
                wo_, bdo = pvoff["wdw%d" % li], pvoff["bdw%d" % li]
                lg, lb, b2 = pvoff["lng%d" % li], pvoff["lnb%d" % li], pvoff["bpw2%d" % li]
                gn = 0
                for i in range(NT):
                    lo = i * T - HW
                    hi = (i + 1) * T + HW
                    clo, chi = max(lo, 0), min(hi, NTOK)
                    for k in range(KC):
                        b = gn % 4
                        gn += 1
                        g_ = ge[b]
                        if clo > lo:
                            S.op("dve", (lambda g_=g_: lambda e: e.memset(g_[:, 0:HW], 0.0))(), w=[tk_ge[b]])
                        if chi < hi:
                            S.op("dve", (lambda g_=g_: lambda e: e.memset(g_[:, WG - HW:WG], 0.0))(), w=[tk_ge[b]])
                        S.dma("sp", "ge%d" % b, g_[:, clo - lo:chi - lo], G[k * 128:(k + 1) * 128, clo:chi], w=[tk_ge[b]])
                        S.op("dve", (lambda g_=g_, i=i: lambda e: e.tensor_scalar(
                            out=g_[:, 0:HW], in0=g_[:, 0:HW], scalar1=cmt[:, 4 + 2 * i:5 + 2 * i], scalar2=None, op0=ALU.mult))(),
                            r=[tk_const], w=[tk_ge[b]])
                        S.op("dve", (lambda g_=g_, i=i: lambda e: e.tensor_scalar(
                            out=g_[:, WG - HW:WG], in0=g_[:, WG - HW:WG], scalar1=cmt[:, 5 + 2 * i:6 + 2 * i], scalar2=None, op0=ALU.mult))(),
                            r=[tk_const], w=[tk_ge[b]], tiny=True)
                        S.op("act", (lambda g_=g_, k=k: lambda e: e.activation(
                            out=cb[:, k, :], in_=g_[:, HW:HW + T], func=AF.Identity,
                            bias=pvt[:, bdo + k:bdo + k + 1], scale=pvt[:, wo_ + HW * 16 + k:wo_ + HW * 16 + k + 1]))(),
                            r=[tk_ge[b], tk_const], w=[tk_c[k]])
                        ceng = "pool" if (cfg.conv_pool_every and k % cfg.conv_pool_every == 0) else "dve"
                        for tap in range(CK):
                            if tap == HW:
                                continue
                            S.op(ceng, (lambda g_=g_, k=k, tap=tap: lambda e: e.scalar_tensor_tensor(
                                out=cb[:, k, :], in0=g_[:, tap:tap + T], scalar=pvt[:, wo_ + tap * 16 + k:wo_ + tap * 16 + k + 1],
                                in1=cb[:, k, :], op0=ALU.mult, op1=ALU.add))(),
                                r=[tk_ge[b]], w=[tk_c[k]])
                        bb = k % 3
                        S.op("act", (lambda k=k, bb=bb: lambda e: e.copy(out=cbf[bb][:, :], in_=cb[:, k, :]))(),
                             r=[tk_c[k]], w=[tk_cbf[bb]])
                        S.op("act", (lambda k=k, bb=bb: lambda e: e.activation(out=csq[bb][:, :], in_=cb[:, k, :], func=AF.Square))(),
                             r=[tk_c[k]], w=[tk_csq[bb]])
                        S.op("pe", (lambda k=k, bb=bb: lambda e: e.matmul(ps[0][:, :], ones_d[:], cbf[bb][:, :], start=(k == 0), stop=(k == KC - 1)))(),
                             r=[tk_cbf[bb], tk_const], w=[ptk[0]])
                        S.op("pe", (lambda k=k, bb=bb: lambda e: e.matmul(ps[1][:, :], ones_d[:], csq[bb][:, :], start=(k == 0), stop=(k == KC - 1)))(),
                             r=[tk_csq[bb], tk_const], w=[ptk[1]])
                    S.op("dve", lambda e: e.tensor_copy(out=mu[:, :], in_=ps[0][:, :]), r=[ptk[0]], w=[tk_st])
                    S.op("dve", lambda e: e.tensor_tensor(out=rstd[:, :], in0=mu[:, :], in1=mu[:, :], op=ALU.mult), r=[tk_st], w=[tk_st])
                    S.op("dve", lambda e: e.tensor_tensor(out=rstd[:, :], in0=ps[1][:, :], in1=rstd[:, :], op=ALU.subtract), r=[ptk[1]], w=[tk_st])
                    S.op("dve", lambda e: e.tensor_scalar(out=rstd[:, :], in0=rstd[:, :], scalar1=0.0, scalar2=None, op0=ALU.max), w=[tk_st])
                    S.op("act", lambda e: e.activation(out=rstd[:, :], in_=rstd[:, :], func=AF.Sqrt, bias=eps_ap, scale=1.0),
                         r=[tk_const], w=[tk_st])
                    S.op("dve", lambda e: e.reciprocal(out=rstd[:, :], in_=rstd[:, :]), w=[tk_st], tiny=True)
                    for k in range(KC):
                        tb = k % 2
                        S.op("dve", (lambda k=k, tb=tb: lambda e: e.tensor_tensor(out=tmp[tb][:, :], in0=cb[:, k, :], in1=mu[:, :], op=ALU.subtract))(),
                             r=[tk_c[k], tk_st], w=[tk_tmp[tb]])
                        S.op("dve", (lambda k=k, tb=tb: lambda e: e.tensor_tensor(out=tmp[tb][:, :], in0=tmp[tb][:, :], in1=rstd[:, :], op=ALU.mult))(),
                             r=[tk_st], w=[tk_tmp[tb]])
                        S.op("act", (lambda k=k, tb=tb: lambda e: e.activation(
                            out=sbf[:, k, :], in_=tmp[tb][:, :], func=AF.Silu, bias=pvt[:, lb + k:lb + k + 1], scale=pvt[:, lg + k:lg + k + 1]))(),
                            r=[tk_tmp[tb], tk_const], w=[tk_s[k]])
                    for m in range(16):
                        b = m % 2
                        S.dma("pool", "xr%d" % b, xr[b][:], src[m * 128:(m + 1) * 128, i * T:(i + 1) * T], w=[tk_xr[b]])
                        for k in range(KC):
                            S.op("pe", (lambda b=b, k=k, m=m: lambda e: e.matmul(
                                ps[2 + b][:, :], w2[:, k, m * 128:(m + 1) * 128], sbf[:, k, :], start=(k == 0), stop=(k == KC - 1)))(),
                                r=[tk_w2, tk_s[k]], w=[ptk[2 + b]])
                        S.op("dve", (lambda b=b, m=m: lambda e: e.scalar_tensor_tensor(
                            out=yo[b][:, :], in0=ps[2 + b][:, :], scalar=pvt[:, b2 + m:b2 + m + 1], in1=xr[b][:, :], op0=ALU.add, op1=ALU.add))(),
                            r=[ptk[2 + b], tk_xr[b], tk_const], w=[tk_yo[b]])
                        S.dma("pool", "yo%d" % b, dst[m * 128:(m + 1) * 128, i * T:(i + 1) * T], yo[b][:], r=[tk_yo[b]])
                S.flush()

        def phase_attn(li, src, dst):
            with ExitStack() as ph:
                ps = psum_banks(ph)
                ptk = [Tok(True) for _ in range(8)]
                nb = norm_bufs(ph, T, ps[7], ptk[7], None, None)
                xn = [sb(ph, "xn%d" % i, [128, KC, T], BF16) for i in range(2)]
                tk_xn = [[Tok() for _ in range(5)] for _ in range(2)]
                wq = [sb(ph, "wq%d" % i, [128, KC, 128], BF16) for i in range(3)]
                tk_wq = [Tok() for _ in range(3)]
                wv = sb(ph, "wv", [128, KC, 512], BF16)
                tk_wv = Tok()
                rc = [sb(ph, "rc%d" % i, [128, T], F32) for i in range(2)]
                rsn = [sb(ph, "rsn%d" % i, [128, T], F32) for i in range(2)]
                tk_rt = [Tok() for _ in range(2)]
                hsq = [sb(ph, "hsq%d" % i, [128, T], BF16) for i in range(2)]
                tk_hsq = [Tok() for _ in range(2)]
                hrs = [sb(ph, "hrs%d" % i, [128, T], F32) for i in range(2)]
                tk_hrs = [Tok() for _ in range(2)]
                qn = [sb(ph, "qn%d" % i, [128, T], BF16) for i in range(2)]
                tk_qn = [Tok() for _ in range(2)]
                t1 = [sb(ph, "t1%d" % i, [128, T], F32) for i in range(2)]
                tk_t1 = [Tok() for _ in range(2)]
                t2 = [sb(ph, "t2%d" % i, [128, T], F32) for i in range(2)]
                tk_t2 = [Tok() for _ in range(2)]
                qr = [sb(ph, "qr%d" % i, [128, T], BF16) for i in range(3)]
                tk_qr = [Tok() for _ in range(3)]
                vsb = [sb(ph, "vsb%d" % i, [128, 512], BF16) for i in range(2)]
                tk_vsb = [Tok() for _ in range(2)]
                wsrc_ = wdst["wqkv", li]
                wqk = wsrc_[:, 0:40960].rearrange("p (n k c) -> p n k c", n=20, k=KC)
                S.dma("sp", "wv", wv[:], wsrc_[:, 40960:49152].rearrange("p (k c) -> p k c", k=KC), w=[tk_wv])
                VSv = VS.rearrange("p (g c d) -> p g c d", g=NKV, c=NKC)
                norm_tile(nb, src, 0, T, xn[0], tk_xn[0], False)
                cn = 0
                for i in range(NT):
                    cur = i % 2
                    xc = xn[cur]
                    rb = i % 2
                    S.dma("pool", "rc%d" % rb, rc[rb][:], ropec_d[:, i * T:(i + 1) * T], w=[tk_rt[rb]])
                    S.dma("pool", "rsn%d" % rb, rsn[rb][:], ropes_d[:, i * T:(i + 1) * T], wadd=[tk_rt[rb]])
                    for c in range(20):
                        b = cn % 3
                        cp = cn % 2
                        q3 = cn % 3
                        cn += 1
                        S.dma("sp", "wq%d" % b, wq[b][:], wqk[:, c], w=[tk_wq[b]])
                        for k in range(KC):
                            S.op("pe", (lambda cp=cp, b=b, k=k, xc=xc: lambda e: e.matmul(
                                ps[cp][:, :], wq[b][:, k, :], xc[:, k, :], start=(k == 0), stop=(k == KC - 1)))(),
                                r=[tk_wq[b], tk_xn[cur][k // 4]], w=[ptk[cp]])
                        S.op("act", (lambda cp=cp: lambda e: e.activation(out=hsq[cp][:, :], in_=ps[cp][:, :], func=AF.Square))(),
                             r=[ptk[cp]], w=[tk_hsq[cp]])
                        S.op("pe", (lambda cp=cp: lambda e: e.matmul(ps[2 + cp][:, :], ones_h[:], hsq[cp][:, :], start=True, stop=True))(),
                             r=[tk_hsq[cp], tk_const], w=[ptk[2 + cp]])
                        S.op("act", (lambda cp=cp: lambda e: e.activation(out=hrs[cp][:, :], in_=ps[2 + cp][:, :], func=AF.Sqrt, bias=eps_ap, scale=1.0))(),
                             r=[ptk[2 + cp], tk_const], w=[tk_hrs[cp]])
                        S.op("dve", (lambda cp=cp: lambda e: e.reciprocal(out=hrs[cp][:, :], in_=hrs[cp][:, :]))(), w=[tk_hrs[cp]], tiny=True)
                        gcol = pvc("qg%d" % li) if c < 16 else pvc("kg%d" % li)
                        S.op("dve", (lambda cp=cp, gcol=gcol: lambda e: e.scalar_tensor_tensor(
                            out=qn[cp][:, :], in0=ps[cp][:, :], scalar=gcol, in1=hrs[cp][:, :], op0=ALU.mult, op1=ALU.mult))(),
                            r=[ptk[cp], tk_hrs[cp], tk_const], w=[tk_qn[cp]])
                        S.op("pe", (lambda cp=cp: lambda e: e.matmul(ps[4 + cp][:, :], permb[:], qn[cp][:, :], start=True, stop=True))(),
                             r=[tk_qn[cp], tk_const], w=[ptk[4 + cp]])
                        S.op("dve", (lambda cp=cp, rb=rb: lambda e: e.tensor_tensor(out=t1[cp][:, :], in0=qn[cp][:, :], in1=rc[rb][:, :], op=ALU.mult))(),
                             r=[tk_qn[cp], tk_rt[rb]], w=[tk_t1[cp]])
                        S.op("dve", (lambda cp=cp, rb=rb: lambda e: e.tensor_tensor(out=t2[cp][:, :], in0=ps[4 + cp][:, :], in1=rsn[rb][:, :], op=ALU.mult))(),
                             r=[ptk[4 + cp], tk_rt[rb]], w=[tk_t2[cp]])
                        S.op("dve", (lambda cp=cp, q3=q3: lambda e: e.tensor_tensor(out=qr[q3][:, :], in0=t1[cp][:, :], in1=t2[cp][:, :], op=ALU.add))(),
                             r=[tk_t1[cp], tk_t2[cp]], w=[tk_qr[q3]])
                        if c < 16:
                            S.dma("pool", "qr%d" % q3, QS[c * 128:(c + 1) * 128, i * T:(i + 1) * T], qr[q3][:], r=[tk_qr[q3]])
                        else:
                            S.dma("pool", "qr%d" % q3, KS[(c - 16) * 128:(c - 15) * 128, i * T:(i + 1) * T], qr[q3][:], r=[tk_qr[q3]])
                        if c == 8 and i + 1 < NT:
                            norm_tile(nb, src, i + 1, T, xn[1 - cur], tk_xn[1 - cur], False)
                    for s in range(4):
                        vb = s % 2
                        for k in range(KC):
                            S.op("pe", (lambda s=s, k=k, xc=xc: lambda e: e.matmul(
                                ps[6][:, :], xc[:, k, s * 128:(s + 1) * 128], wv[:, k, :], start=(k == 0), stop=(k == KC - 1)))(),
                                r=[tk_wv, tk_xn[cur][k // 4]], w=[ptk[6]])
                        S.op("act", (lambda vb=vb: lambda e: e.copy(out=vsb[vb][:, :], in_=ps[6][:, :]))(), r=[ptk[6]], w=[tk_vsb[vb]])
                        S.dma("pool", "vsb%d" % vb, VSv[:, :, 4 * i + s, :], vsb[vb][:, :].rearrange("p (g d) -> p g d", g=NKV),
                              r=[tk_vsb[vb]])
                S.flush()
            with ExitStack() as ph:
                ps = psum_banks(ph)
                ptk = [Tok(True) for _ in range(8)]
                kt = [sb(ph, "kt%d" % i, [128, NTOK], BF16) for i in range(2)]
                vt = [sb(ph, "vt%d" % i, [128, NKC, 128], BF16) for i in range(2)]
                tk_kv = [Tok() for _ in range(2)]
                qt = [sb(ph, "qt%d" % i, [128, T], BF16) for i in range(3)]
                tk_qt = [Tok() for _ in range(3)]
                pt = [sb(ph, "pt%d" % i, [128, T], BF16) for i in range(4)]
                tk_pt = [Tok() for _ in range(4)]
                rcp = [sb(ph, "rcp%d" % i, [128, T], F32) for i in range(2)]
                tk_rcp = [Tok() for _ in range(2)]
                ot = [sb(ph, "ot%d" % i, [128, T], BF16) for i in range(2)]
                tk_ot = [Tok() for _ in range(2)]
                accd = [sb(ph, "accd%d" % i, [128, T], F32) for i in range(2)]
                accp = [sb(ph, "accp%d" % i, [128, T], F32) for i in range(2)]
                tk_accd = [Tok() for _ in range(2)]
                tk_accp = [Tok() for _ in range(2)]
                gnb = sb(ph, "gnb", [128, 256], F32)
                bias = sb(ph, "bias", [128, 8], F32)
                tk_b = Tok()
                S.dma("sp", "gnb", gnb[:], gn_d[0:1, li * 256:(li + 1) * 256].partition_broadcast(128), w=[tk_b])
                S.op("dve", lambda e: e.reduce_max(out=bias[:, 4:5], in_=gnb[:, 0:128], axis=AX.X, apply_absolute_value=True), r=[tk_b], wadd=[tk_b], tiny=True)
                S.op("dve", lambda e: e.reduce_max(out=bias[:, 5:6], in_=gnb[:, 128:256], axis=AX.X, apply_absolute_value=True), wadd=[tk_b], tiny=True)
                S.op("dve", lambda e: e.tensor_tensor(out=bias[:, 6:7], in0=bias[:, 4:5], in1=bias[:, 5:6], op=ALU.mult), r=[tk_b], wadd=[tk_b], tiny=True)
                S.op("dve", lambda e: e.tensor_scalar(out=bias[:, 7:8], in0=bias[:, 6:7], scalar1=-float(np.sqrt(HD)), scalar2=None, op0=ALU.mult),
                     r=[tk_b], wadd=[tk_b], tiny=True)
                S.op("dve", lambda e: e.tensor_scalar(out=bias[:, 0:4], in0=cmt[:, 0:4], scalar1=bias[:, 7:8], scalar2=None, op0=ALU.add),
                     r=[tk_b, tk_const], wadd=[tk_b], tiny=True)
                scale = float(HD ** -0.5)
                items = []
                for g in range(NKV):
                    for qg in range(NT):
                        for hh in range(4):
                            for c in range(NKC):
                                items.append((g, qg, hh, c))
                LA = 2
                n_it = len(items)
                hn = 0
                for n in range(n_it + LA):
                    if n < n_it:
                        g, qg, hh, c = items[n]
                        gb = g % 2
                        hidx = (g * NT + qg) * 4 + hh
                        if qg == 0 and hh == 0 and c == 0:
                            S.dma("sp", "kt%d" % gb, kt[gb][:], KS[g * 128:(g + 1) * 128, :], w=[tk_kv[gb]])
                            S.dma("sp", "vt%d" % gb, vt[gb][:], VS.rearrange("p (g c d) -> p g c d", g=NKV, c=NKC)[:, g], wadd=[tk_kv[gb]])
                        if c == 0:
                            heads = [(g, qg, hh, hidx)] if n == 0 else []
                            if n + NKC < n_it:
                                g2, qg2, hh2, _ = items[n + NKC]
                                heads.append((g2, qg2, hh2, hidx + 1))
                            for (g2, qg2, hh2, hx) in heads:
                                qb2 = hx % 3
                                h2 = 4 * g2 + hh2
                                S.dma("sp", "qt%d" % qb2, qt[qb2][:], QS[h2 * 128:(h2 + 1) * 128, qg2 * T:(qg2 + 1) * T], w=[tk_qt[qb2]])
                        qb = hidx % 3
                        sbk = n % 3
                        S.op("pe", (lambda sbk=sbk, gb=gb, c=c, qb=qb: lambda e: e.matmul(
                            ps[sbk][:, :], kt[gb][:, c * 128:(c + 1) * 128], qt[qb][:, :], start=True, stop=True))(),
                            r=[tk_kv[gb], tk_qt[qb]], w=[ptk[sbk]])
                    m = n - LA
                    if m >= 0:
                        g, qg, hh, c = items[m]
                        gb = g % 2
                        hidx = (g * NT + qg) * 4 + hh
                        hp = hidx % 2
                        sbk = m % 3
                        pb = m % 4
                        bi = (c // (NKC // 2)) * 2 + (qg // (NT // 2))
                        S.op("act", (lambda sbk=sbk, pb=pb, bi=bi: lambda e: e.activation(
                            out=pt[pb][:, :], in_=ps[sbk][:, :], func=AF.Exp, bias=bias[:, bi:bi + 1], scale=scale))(),
                            r=[ptk[sbk], tk_b], w=[tk_pt[pb]])
                        S.op("pe", (lambda hp=hp, gb=gb, c=c, pb=pb: lambda e: e.matmul(
                            ps[3 + hp][:, :], vt[gb][:, c, :], pt[pb][:, :], start=(c == 0), stop=(c == NKC - 1)))(),
                            r=[tk_kv[gb], tk_pt[pb]], w=[ptk[3 + hp]])
                        if cfg.attn_sum_pe:
                            S.op("pe", (lambda hp=hp, pb=pb, c=c: lambda e: e.matmul(
                                ps[5 + hp][:, :], ones_1[:], pt[pb][:, :], start=(c == 0), stop=(c == NKC - 1)))(),
                                r=[tk_pt[pb], tk_const], w=[ptk[5 + hp]])
                        else:
                            if c % 8 == 7:
                                S.op("pe", (lambda hp=hp, pb=pb, c=c: lambda e: e.matmul(
                                    ps[5 + hp][:, :], ones_1[:], pt[pb][:, :], start=(c == 7), stop=False))(),
                                    r=[tk_pt[pb], tk_const], w=[ptk[5 + hp]])
                            else:
                                acc = accd[hp]
                                tka = tk_accd[hp]
                                if c == 0:
                                    S.op("dve", (lambda acc=acc, pb=pb: lambda e: e.tensor_copy(out=acc[:, :], in_=pt[pb][:, :]))(),
                                         r=[tk_pt[pb]], w=[tka])
                                else:
                                    S.op("dve", (lambda acc=acc, pb=pb: lambda e: e.tensor_tensor(out=acc[:, :], in0=acc[:, :], in1=pt[pb][:, :], op=ALU.add))(),
                                         r=[tk_pt[pb], tka], w=[tka])
                            if c == NKC - 1:
                                S.op("pe", (lambda hp=hp: lambda e: e.matmul(ps[5 + hp][:, :], ones_f[:], accd[hp][:, :], start=False, stop=True))(),
                                     r=[tk_accd[hp], tk_const], w=[ptk[5 + hp]])
                        if c == NKC - 1:
                            h = 4 * g + hh
                            S.op("dve", (lambda hp=hp: lambda e: e.reciprocal(out=rcp[hp][:, :], in_=ps[5 + hp][:, :]))(),
                                 r=[ptk[5 + hp]], w=[tk_rcp[hp]])
                            S.op("dve", (lambda hp=hp: lambda e: e.tensor_tensor(out=ot[hp][:, :], in0=ps[3 + hp][:, :], in1=rcp[hp][:, :], op=ALU.mult))(),
                                 r=[ptk[3 + hp], tk_rcp[hp]], w=[tk_ot[hp]])
                            S.dma("pool", "ot%d" % hp, OS[h * 128:(h + 1) * 128, qg * T:(qg + 1) * T], ot[hp][:], r=[tk_ot[hp]])
                S.flush()
            with ExitStack() as ph:
                ps = psum_banks(ph)
                ptk = [Tok(True) for _ in range(8)]
                wo = sb(ph, "wo", [128, KC, D], BF16)
                tk_wo = Tok()
                oin = [sb(ph, "oin%d" % i, [128, KC, T], BF16) for i in range(2)]
                tk_oin = [Tok() for _ in range(2)]
                xr = [sb(ph, "xr%d" % i, [128, T], F32) for i in range(3)]
                tk_xr = [Tok() for _ in range(3)]
                yo = [sb(ph, "yo%d" % i, [128, T], F32) for i in range(3)]
                tk_yo = [Tok() for _ in range(3)]
                S.dma("sp", "wo", wo[:], wdst["wo", li].rearrange("p (k n) -> p k n", k=KC), w=[tk_wo])
                mn = 0
                for i in range(NT):
                    ob = i % 2
                    S.dma("sp", "oin%d" % ob, oin[ob][:], OS.rearrange("(h p) t -> p h t", p=128)[:, :, i * T:(i + 1) * T], w=[tk_oin[ob]])
                    for m in range(16):
                        b = mn % 3
                        pb = mn % 4
                        mn += 1
                        S.dma("pool", "xr%d" % b, xr[b][:], src[m * 128:(m + 1) * 128, i * T:(i + 1) * T], w=[tk_xr[b]])
                        for k in range(KC):
                            S.op("pe", (lambda pb=pb, k=k, m=m, ob=ob: lambda e: e.matmul(
                                ps[pb][:, :], wo[:, k, m * 128:(m + 1) * 128], oin[ob][:, k, :], start=(k == 0), stop=(k == KC - 1)))(),
                                r=[tk_wo, tk_oin[ob]], w=[ptk[pb]])
                        S.op("dve", (lambda pb=pb, b=b: lambda e: e.tensor_tensor(out=yo[b][:, :], in0=ps[pb][:, :], in1=xr[b][:, :], op=ALU.add))(),
                             r=[ptk[pb], tk_xr[b]], w=[tk_yo[b]])
                        S.dma("pool", "yo%d" % b, dst[m * 128:(m + 1) * 128, i * T:(i + 1) * T], yo[b][:], r=[tk_yo[b]])
                S.flush()

        cur_, nxt_ = XA, XB
        for kind, li in cfg.layers:
            if kind == "attn":
                phase_attn(li, cur_, nxt_)
            elif kind == "conv":
                phase_conv(li, cur_, nxt_)
            else:
                phase_ffn(li, cur_, nxt_)
            cur_, nxt_ = nxt_, cur_

        with ExitStack() as ph:
            ps = psum_banks(ph)
            ptk = [Tok(True) for _ in range(8)]
            xTt = [sb(ph, "xTt%d" % i, [128, KC, T], F32) for i in range(2)]
            tT = [Tok() for _ in range(2)]
            ytok = [sb(ph, "ytok%d" % i, [128, D], F32) for i in range(2)]
            tky = [[Tok() for _ in range(4)] for _ in range(2)]
            yn = 0
            for i in range(NT):
                b = i % 2
                S.dma("sp", "xTt%d" % b, xTt[b][:], XT(cur_)[:, :, i * T:(i + 1) * T], w=[tT[b]])
                for s in range(4):
                    yb = yn % 2
                    yn += 1
                    for kq in range(4):
                        pb = (s * 4 + kq) % 4
                        for q in range(4):
                            k = kq * 4 + q
                            S.op("pe", (lambda pb=pb, q=q, k=k, s=s, b=b: lambda e: e.transpose(
                                out=ps[pb][:, q * 128:(q + 1) * 128], in_=xTt[b][:, k, s * 128:(s + 1) * 128], identity=ident[:]))(),
                                r=[tT[b], tk_const], w=[ptk[pb]])
                        if kq % 2 == 0:
                            S.op("act", (lambda pb=pb, kq=kq, yb=yb: lambda e: e.copy(out=ytok[yb][:, kq * 512:(kq + 1) * 512], in_=ps[pb][:, :]))(),
                                 r=[ptk[pb]], w=[tky[yb][kq]])
                        else:
                            S.op("dve", (lambda pb=pb, kq=kq, yb=yb: lambda e: e.tensor_copy(out=ytok[yb][:, kq * 512:(kq + 1) * 512], in_=ps[pb][:, :]))(),
                                 r=[ptk[pb]], w=[tky[yb][kq]])
                    S.dma("pool", "ytok%d" % yb, yout[i * T + s * 128:i * T + (s + 1) * 128, :], ytok[yb][:], r=tky[yb])
            S.flush(final=True)
    return nc


def rope_tables(pos):
    pos = np.asarray(pos)
    row = (pos // 64).astype(np.float32)
    col = (pos % 64).astype(np.float32)
    freqs = (np.float32(10000.0) ** (-np.arange(32, dtype=np.float32) * np.float32(2.0) / np.float32(64.0))).astype(np.float32)
    ar = (row[None, :] * freqs[:, None]).astype(np.float32)
    ac = (col[None, :] * freqs[:, None]).astype(np.float32)
    C = np.concatenate([np.cos(ar), np.cos(ar), np.cos(ac), np.cos(ac)], axis=0).astype(np.float32)
    Sn = np.concatenate([-np.sin(ar), np.sin(ar), -np.sin(ac), np.sin(ac)], axis=0).astype(np.float32)
    return np.ascontiguousarray(C), np.ascontiguousarray(Sn)


def perm_matrix():
    P = np.zeros((128, 128), np.float32)
    for d in range(128):
        blk, r = divmod(d, 64)
        src = blk * 64 + (r + 32) % 64
        P[src, d] = 1.0
    return P


def core_consts(cfg, nseq):
    NT, NTOK = cfg.NT, cfg.NTOK
    L = NTOK // nseq
    pos = np.arange(NTOK) % L
    C, Sn = rope_tables(pos)
    cm = np.zeros((128, 4 + 2 * NT), np.float32)
    for kh in range(2):
        for qh in range(2):
            cm[:, kh * 2 + qh] = 0.0 if (nseq == 1 or kh == qh) else NEG
    for i in range(NT):
        cm[:, 4 + 2 * i] = 0.0 if (i * T) % L == 0 else 1.0
        cm[:, 5 + 2 * i] = 0.0 if ((i + 1) * T) % L == 0 else 1.0
    return C, Sn, cm


def shared_inputs(cfg, p):
    pvoff, npv = pv_layout(cfg)
    pv = np.zeros((128, npv), np.float32)

    def put(name, arr):
        pv[:, pvoff[name]:pvoff[name] + arr.shape[1]] = arr
    put("eps", np.full((128, 1), EPS, np.float32))
    m = {}
    gn = np.zeros((1, max(1, cfg.n_attn) * 256), np.float32)
    for a in range(cfg.n_attn):
        put("an%d" % a, _pcols(p["attn_norm"][a]))
        put("qg%d" % a, _pcols(p["attn_q_norm"][a]))
        put("kg%d" % a, _pcols(p["attn_k_norm"][a]))
        gn[0, a * 256:a * 256 + 128] = p["attn_q_norm"][a]
        gn[0, a * 256 + 128:a * 256 + 256] = p["attn_k_norm"][a]
        m["wqkv%d" % a] = lay_qkv(np.asarray(p["attn_w_qkv"][a], np.float32))
        m["wo%d" % a] = lay_rowmajor(np.asarray(p["attn_w_o"][a], np.float32))
    for c in range(cfg.n_conv):
        put("cn%d" % c, _pcols(p["conv_norm"][c]))
        put("bpw1%d" % c, _pcols(p["conv_b_pw1"][c]))
        wdw = np.asarray(p["conv_w_dw"][c], np.float32)
        put("wdw%d" % c, np.concatenate([_pcols(wdw[t]) for t in range(CK)], axis=1))
        put("bdw%d" % c, _pcols(p["conv_b_dw"][c]))
        put("lng%d" % c, _pcols(p["conv_ln_g"][c]))
        put("lnb%d" % c, _pcols(p["conv_ln_b"][c]))
        put("bpw2%d" % c, _pcols(p["conv_b_pw2"][c]))
        m["wpw1%d" % c] = lay_pw1(np.asarray(p["conv_w_pw1"][c], np.float32))
        m["wpw2%d" % c] = lay_rowmajor(np.asarray(p["conv_w_pw2"][c], np.float32))
    for i in range(cfg.n_ffn):
        put("fn%d" % i, _pcols(p["ffn_norm"][i]))
        fw = np.asarray(p["ffn_w_dw"][i], np.float32)
        put("fw%d" % i, np.concatenate([_pcols(fw[t]) for t in range(3)], axis=1))
        put("fb%d" % i, _pcols(p["ffn_b_dw"][i]))
        m["wup%d" % i] = lay_up(np.asarray(p["ffn_w_up"][i], np.float32))
        m["wdn%d" % i] = lay_dn(np.asarray(p["ffn_w_down"][i], np.float32))
    m["pv"] = pv
    m["gn"] = gn
    m["ident"] = np.eye(128, dtype=np.float32)
    m["perm"] = perm_matrix()
    return m


def kernel(**inputs):
    cfg = Cfg()
    p = {k: np.asarray(v) for k, v in inputs.items()}
    xp = np.asarray(p["x_prompt"], np.float32)
    xs = np.asarray(p["x_sample"], np.float32)
    shared = shared_inputs(cfg, p)
    in_maps = []
    for core in range(8):
        if core < 4:
            x = xp[2 * core:2 * core + 2].reshape(cfg.NTOK, D)
            C, Sn, cm = core_consts(cfg, 2)
        else:
            x = xs[core - 4].reshape(cfg.NTOK, D)
            C, Sn, cm = core_consts(cfg, 1)
        mm = dict(shared)
        mm["xin"] = np.ascontiguousarray(x)
        mm["ropec"] = C
        mm["ropes"] = Sn
        mm["cm"] = cm
        in_maps.append(mm)
    nc = build_program(cfg)
    res = run_bass_kernel_spmd(nc, in_maps, core_ids=list(range(8)))
    outs = [np.asarray(r["yout"], np.float32) for r in res.results]
    y_prompt = np.stack([o.reshape(2, 4096, D) for o in outs[:4]], axis=0).reshape(8, 4096, D)
    y_sample = np.stack([o.reshape(1, 8192, D) for o in outs[4:]], axis=0).reshape(4, 8192, D)
    return (y_prompt, y_sample)
```

```python
import numpy as np
from contextlib import ExitStack
import concourse.bass as bass
import concourse.mybir as mybir
from concourse.bass_utils import run_bass_kernel_spmd

F32 = mybir.dt.float32
BF16 = mybir.dt.bfloat16
AF = mybir.ActivationFunctionType
ALU = mybir.AluOpType
AX = mybir.AxisListType

D = 2048
KC = 16
T = 512
NH, NKV, HD = 16, 4, 128
DFF = 5632
JF = DFF // 128
CK = 31
EPS = 1e-6
NEG = -30000.0


class Op:
    __slots__ = ("eng", "fn", "deps", "ddeps", "signal", "idx", "dent", "dval", "bar", "tiny")


class Tok:
    __slots__ = ("w", "r", "x")

    def __init__(self, x=False):
        self.w = []
        self.r = {}
        self.x = x


ENGS = ("pe", "act", "dve", "pool", "sp")


class Sched:
    def __init__(self, nc, stack):
        self.nc = nc
        self.h = {"pe": nc.tensor, "act": nc.scalar, "dve": nc.vector, "pool": nc.gpsimd, "sp": nc.sync}
        self.sem = {e: stack.enter_context(nc.semaphore("s_" + e)) for e in ENGS}
        self.cnt = {e: 0 for e in ENGS}
        self.ops = {e: [] for e in ENGS}
        self.dfree = [[stack.enter_context(nc.semaphore("d%d" % i)), 0] for i in range(72)]
        self.dkeys = {}
        self.dused = []
        self.waited = {e: {} for e in ENGS}
        self.bar = None
        self.bar_pending = set()
        self.nops = 0

    def _add(self, o, p, raw):
        if p is o:
            return
        if p.dent is not None:
            o.ddeps.append((p.dent, p.dent[1]))
            return
        if p.eng == o.eng and o.dent is None:
            if not (raw and p.tiny and p.eng != "pe"):
                return
        p.signal = True
        o.deps.append(p)

    def _track(self, o, r, w, wadd):
        if any(t.x for t in r):
            w = list(w) + [t for t in r if t.x and t not in w]
            r = [t for t in r if not t.x]
        for t in r:
            for p in t.w:
                self._add(o, p, True)
        for t in w:
            for p in t.w:
                self._add(o, p, False)
            for p in t.r.values():
                self._add(o, p, False)
        for t in wadd:
            for p in t.r.values():
                self._add(o, p, False)
        key = o.eng if o.dent is None else id(o.dent)
        for t in r:
            t.r[key] = o
        for t in w:
            t.w = [o]
            t.r = {}
        for t in wadd:
            t.w.append(o)

    def _new(self, eng):
        o = Op()
        o.eng = eng
        o.deps = []
        o.ddeps = []
        o.signal = False
        o.dent = None
        o.tiny = False
        o.bar = None
        if eng in self.bar_pending:
            o.bar = self.bar
            self.bar_pending.discard(eng)
        self.ops[eng].append(o)
        self.nops += 1
        return o

    def op(self, eng, fn, r=(), w=(), wadd=(), tiny=False):
        o = self._new(eng)
        o.fn = fn
        o.tiny = tiny
        self._track(o, r, w, wadd)
        return o

    def dma(self, q, key, out, in_, r=(), w=(), wadd=(), slow=False):
        o = self._new(q)
        ent = self.dkeys.get(key)
        if ent is None:
            ent = self.dfree.pop()
            self.dkeys[key] = ent
            self.dused.append(ent)
        o.dent = ent
        self._track(o, r, w, wadd)
        ent[1] += 16
        o.dval = ent[1]
        if slow:
            o.fn = lambda e: e.dma_start(out=out, in_=in_, allow_slow_non_contiguous=True)
        else:
            o.fn = lambda e: e.dma_start(out=out, in_=in_)
        return o

    def flush(self, final=False):
        nc = self.nc
        last = []
        for e in ENGS:
            for o in reversed(self.ops[e]):
                if o.dent is None:
                    o.signal = True
                    last.append(o)
                    break
        for e in ENGS:
            c = self.cnt[e]
            for o in self.ops[e]:
                if o.dent is None and o.signal:
                    c += 1
                    o.idx = c
            self.cnt[e] = c
        bar_d = [(ent, ent[1]) for ent in self.dused]
        if final:
            o = self._new("sp")
            o.fn = None
            o.bar = (last, bar_d)
        with nc.Block() as block:
            for e in ENGS:
                if self.ops[e]:
                    getattr(block, {"pe": "tensor", "act": "scalar", "dve": "vector", "pool": "gpsimd",
                                    "sp": "sync"}[e])(self._emitter(e))
        for e in ENGS:
            self.ops[e] = []
        self.bar = (last, bar_d)
        self.bar_pending = set(ENGS)
        self.dfree.extend(self.dused)
        self.dused = []
        self.dkeys = {}

    def _emitter(self, e):
        ops = self.ops[e]
        sem = self.sem
        waited = self.waited[e]
        mysem = sem[e]

        import os
        dbg = os.environ.get("SCHED_DBG")
        log = []

        def wait(h, s, v):
            k = id(s)
            if waited.get(k, 0) >= v:
                return
            waited[k] = v
            if dbg:
                log.append("   %s wait %s >= %d" % (e, getattr(s, "name", str(s)), v))
            h.wait_ge(s, v)

        def run(h):
            for o in ops:
                if o.bar is not None:
                    for p in o.bar[0]:
                        if p.eng != e:
                            wait(h, sem[p.eng], p.idx)
                    for ent, v in o.bar[1]:
                        wait(h, ent[0], v)
                for p in o.deps:
                    wait(h, sem[p.eng], p.idx)
                for ent, v in o.ddeps:
                    wait(h, ent[0], v)
                if o.fn is None:
                    continue
                ins = o.fn(h)
                if dbg and len(log) < 400:
                    log.append("%s op#%d %s%s" % (e, len(log), "DMA->%s=%d" % (getattr(o.dent[0], "name", "?"), o.dval) if o.dent is not None else "", " sig%d" % o.idx if (o.dent is None and o.signal) else ""))
                    if len(log) >= 400:
                        print("\n".join(log))
                if o.dent is not None:
                    ins.then_inc(o.dent[0], 16)
                elif o.signal:
                    ins.then_inc(mysem, 1)
        return run


def _pcols(v):
    v = np.asarray(v, np.float32).reshape(-1, 128)
    return np.ascontiguousarray(v.T)


class PV:
    def __init__(self):
        self.off = {}
        self.cols = []
        self.n = 0

    def add(self, name, arr2d):
        self.off[name] = self.n
        self.cols.append(np.asarray(arr2d, np.float32))
        self.n += arr2d.shape[1]

    def pack(self):
        return np.ascontiguousarray(np.concatenate(self.cols, axis=1))


def lay_blocks(W, cols_of_block):
    K = W.shape[0]
    kcn = K // 128
    out = []
    for cols in cols_of_block:
        blk = W[:, cols].reshape(kcn, 128, len(cols)).transpose(1, 0, 2)
        out.append(blk)
    return np.stack(out, axis=1)


def lay_qkv(W):
    qk = lay_blocks(W, [np.arange(c * 128, (c + 1) * 128) for c in range(20)]).reshape(128, -1)
    v = lay_blocks(W, [np.arange(2560, 3072)]).reshape(128, -1)
    return np.ascontiguousarray(np.concatenate([qk, v], axis=1))


def lay_rowmajor(W):
    return np.ascontiguousarray(lay_blocks(W, [np.arange(W.shape[1])]).reshape(128, -1))


def lay_pw1(W):
    blocks = []
    for j in range(16):
        for ag in range(2):
            blocks.append(np.arange(ag * 2048 + j * 128, ag * 2048 + (j + 1) * 128))
    return np.ascontiguousarray(lay_blocks(W, blocks).reshape(128, -1))


def lay_up(W):
    blocks = []
    for j in range(JF):
        for gv in range(2):
            blocks.append(np.arange(gv * DFF + j * 128, gv * DFF + (j + 1) * 128))
    return np.ascontiguousarray(lay_blocks(W, blocks).reshape(128, -1))


def lay_dn(W):
    return np.ascontiguousarray(lay_blocks(W, [np.arange(m * 128, (m + 1) * 128) for m in range(16)]).reshape(128, -1))


class Cfg:
    def __init__(self, NT=16, layers=None):
        self.NT = NT
        self.NTOK = NT * T
        self.NKC = self.NTOK // 128
        if layers is None:
            layers = []
            for i in range(4):
                layers.append(("attn" if i % 2 == 0 else "conv", i // 2))
                layers.append(("ffn", i))
        self.layers = layers
        self.n_attn = 1 + max([i for k, i in layers if k == "attn"], default=-1)
        self.n_conv = 1 + max([i for k, i in layers if k == "conv"], default=-1)
        self.n_ffn = 1 + max([i for k, i in layers if k == "ffn"], default=-1)
        self.conv_pool_every = 0
        self.debug = False
        self.attn_sum_pe = False
        self.conv_pe = True
        self.ffn_norm_j = 12


def pv_layout(cfg):
    off = {}
    n = 0

    def add(name, w):
        nonlocal n
        off[name] = n
        n += w
    add("eps", 1)
    for a in range(cfg.n_attn):
        add("an%d" % a, 16)
        add("qg%d" % a, 1)
        add("kg%d" % a, 1)
    for c in range(cfg.n_conv):
        add("cn%d" % c, 16)
        add("bpw1%d" % c, 32)
        add("wdw%d" % c, CK * 16)
        add("bdw%d" % c, 16)
        add("lng%d" % c, 16)
        add("lnb%d" % c, 16)
        add("bpw2%d" % c, 16)
    for i in range(cfg.n_ffn):
        add("fn%d" % i, 16)
        add("fw%d" % i, 3 * 2 * JF)
        add("fb%d" % i, 2 * JF)
    return off, n


def build_program(cfg):
    NT, NTOK, NKC = cfg.NT, cfg.NTOK, cfg.NKC
    nc = bass.Bass("TRN2", target_bir_lowering=False)
    pvoff, npv = pv_layout(cfg)

    def din(name, shape, dt=F32):
        return nc.dram_tensor(name, list(shape), dt, kind="ExternalInput").ap()

    def dscr(name, shape, dt):
        return nc.dram_tensor(name, list(shape), dt).ap()

    xin = din("xin", [NTOK, D])
    yout = nc.dram_tensor("yout", [NTOK, D], F32, kind="ExternalOutput").ap()
    pv_d = din("pv", [128, npv])
    cm_d = din("cm", [128, 4 + 2 * NT])
    ropec_d = din("ropec", [128, NTOK])
    ropes_d = din("ropes", [128, NTOK])
    ident_d = din("ident", [128, 128])
    perm_d = din("perm", [128, 128])
    gn_d = din("gn", [1, max(1, cfg.n_attn) * 256])
    wsrc, wdst = {}, {}
    WL = {"wqkv": 49152, "wo": 32768, "wpw1": 65536, "wpw2": 32768, "wup": 2 * JF * 2048, "wdn": 16 * JF * 128}
    for a in range(cfg.n_attn):
        for nm in ("wqkv", "wo"):
            wsrc[nm, a] = din("%s%d" % (nm, a), [128, WL[nm]])
            wdst[nm, a] = dscr("b_%s%d" % (nm, a), [128, WL[nm]], BF16)
    for c in range(cfg.n_conv):
        for nm in ("wpw1", "wpw2"):
            wsrc[nm, c] = din("%s%d" % (nm, c), [128, WL[nm]])
            wdst[nm, c] = dscr("b_%s%d" % (nm, c), [128, WL[nm]], BF16)
    for i in range(cfg.n_ffn):
        for nm in ("wup", "wdn"):
            wsrc[nm, i] = din("%s%d" % (nm, i), [128, WL[nm]])
            wdst[nm, i] = dscr("b_%s%d" % (nm, i), [128, WL[nm]], BF16)
    XA = dscr("XA", [D, NTOK], F32)
    XB = dscr("XB", [D, NTOK], F32)
    G = dscr("Gs", [D, NTOK], F32)
    QS = dscr("QS", [NH * HD, NTOK], BF16)
    KS = dscr("KS", [NKV * HD, NTOK], BF16)
    VS = dscr("VS", [128, NKV * NKC * HD], BF16)
    OS = dscr("OS", [NH * HD, NTOK], BF16)

    if cfg.debug:
        dbg_xn = nc.dram_tensor("dbg_xn", [128, KC * 514], BF16, kind="ExternalOutput").ap()
        dbg_h = nc.dram_tensor("dbg_h", [128, JF * T], BF16, kind="ExternalOutput").ap()
        dbg_rs = nc.dram_tensor("dbg_rs", [128, 514], F32, kind="ExternalOutput").ap()
        dbg_wu = nc.dram_tensor("dbg_wu", [128, 2 * KC * 128], BF16, kind="ExternalOutput").ap()
        dbg_tt = nc.dram_tensor("dbg_tt", [128, 3 * T], F32, kind="ExternalOutput").ap()
        dbg_pm = nc.dram_tensor("dbg_pm", [128, 4 * T], F32, kind="ExternalOutput").ap()
    top = ExitStack()
    with top:
        S = Sched(nc, top)

        uid = [0]

        def sb(stack, name, shape, dt):
            uid[0] += 1
            return stack.enter_context(nc.sbuf_tensor("%s_u%d" % (name, uid[0]), list(shape), dt))

        pvt = sb(top, "pvt", [128, npv], F32)
        cmt = sb(top, "cmt", [128, 4 + 2 * NT], F32)
        ident = sb(top, "ident", [128, 128], F32)
        permf = sb(top, "permf", [128, 128], F32)
        permb = sb(top, "permb", [128, 128], BF16)
        ones_d = sb(top, "ones_d", [128, 128], BF16)
        ones_h = sb(top, "ones_h", [128, 128], BF16)
        ones_1 = sb(top, "ones_1", [128, 128], BF16)
        ones_f = sb(top, "ones_f", [128, 128], F32)
        identb = sb(top, "identb", [128, 128], BF16)
        tk_const = Tok()

        def pvc(name, c=0, n=1):
            o = pvoff[name] + c
            return pvt[:, o:o + n]

        eps_ap = pvc("eps")

        def psum_banks(stack):
            uid[0] += 1
            return [stack.enter_context(nc.psum_tensor("ps%d_u%d" % (i, uid[0]), [128, 512], F32)) for i in range(8)]

        with ExitStack() as ph:
            S.dma("sp", "c0", pvt[:], pv_d[:, :], w=[tk_const])
            S.dma("sp", "c1", cmt[:], cm_d[:, :], wadd=[tk_const])
            S.dma("sp", "c2", ident[:], ident_d[:, :], wadd=[tk_const])
            S.dma("sp", "c3", permf[:], perm_d[:, :], wadd=[tk_const])
            S.op("dve", lambda e: e.tensor_copy(out=permb[:], in_=permf[:]), r=[tk_const], wadd=[tk_const])
            S.op("dve", lambda e: e.memset(ones_d[:], 1.0 / D), wadd=[tk_const])
            S.op("dve", lambda e: e.memset(ones_h[:], 1.0 / HD), wadd=[tk_const])
            S.op("dve", lambda e: e.memset(ones_1[:], 1.0), wadd=[tk_const])
            S.op("dve", lambda e: e.memset(ones_f[:], 1.0), wadd=[tk_const])
            S.op("dve", lambda e: e.tensor_copy(out=identb[:], in_=ident[:]), r=[tk_const], wadd=[tk_const])
            CH = 8192
            NB_ = 3
            stg = [sb(ph, "stg%d" % i, [128, CH], F32) for i in range(NB_)]
            obf = [sb(ph, "obf%d" % i, [128, CH], BF16) for i in range(NB_)]
            tks = [Tok() for _ in range(NB_)]
            tko = [Tok() for _ in range(NB_)]
            ci = 0

            def cast_region(src, dst, l0, l1, gname, kcn, cb):
                nonlocal ci
                per = kcn * cb
                pos = l0
                while pos < l1:
                    n = min(CH, l1 - pos)
                    b = ci % NB_
                    ci += 1
                    S.dma("sp", "stg%d" % b, stg[b][:, 0:n], src[:, pos:pos + n], w=[tks[b]])
                    if gname is None:
                        S.op("act", (lambda b=b, n=n: lambda e: e.copy(out=obf[b][:, 0:n], in_=stg[b][:, 0:n]))(),
                             r=[tks[b]], w=[tko[b]])
                    else:
                        rel = pos - l0
                        if n >= per:
                            assert n % per == 0 and rel % per == 0
                            reps, k0, nk = n // per, 0, kcn
                        else:
                            assert per % n == 0 and n % cb == 0
                            reps, k0, nk = 1, (rel % per) // cb, n // cb
                        go = pvoff[gname] + k0

                        def f(e, b=b, n=n, reps=reps, nk=nk, go=go):
                            i0 = stg[b][:, 0:n].rearrange("p (r k c) -> p r k c", r=reps, k=nk)
                            o0 = obf[b][:, 0:n].rearrange("p (r k c) -> p r k c", r=reps, k=nk)
                            g = pvt[:, go:go + nk].unsqueeze(1).unsqueeze(3).broadcast_to([128, reps, nk, cb])
                            return e.tensor_tensor(out=o0, in0=i0, in1=g, op=ALU.mult)
                        S.op("dve", f, r=[tks[b], tk_const], w=[tko[b]])
                    S.dma("pool", "obf%d" % b, dst[:, pos:pos + n], obf[b][:, 0:n], r=[tko[b]])
                    pos += n

            for a in range(cfg.n_attn):
                cast_region(wsrc["wqkv", a], wdst["wqkv", a], 0, 40960, "an%d" % a, 16, 128)
                cast_region(wsrc["wqkv", a], wdst["wqkv", a], 40960, 49152, "an%d" % a, 16, 512)
                cast_region(wsrc["wo", a], wdst["wo", a], 0, WL["wo"], None, 0, 0)
            for c in range(cfg.n_conv):
                cast_region(wsrc["wpw1", c], wdst["wpw1", c], 0, WL["wpw1"], "cn%d" % c, 16, 128)
                cast_region(wsrc["wpw2", c], wdst["wpw2", c], 0, WL["wpw2"], None, 0, 0)
            for i in range(cfg.n_ffn):
                cast_region(wsrc["wup", i], wdst["wup", i], 0, WL["wup"], "fn%d" % i, 16, 128)
                cast_region(wsrc["wdn", i], wdst["wdn", i], 0, WL["wdn"], None, 0, 0)
            S.flush()
            tk_const.w = []

        XT = lambda X: X.rearrange("(k p) t -> p k t", p=128)

        with ExitStack() as ph:
            ps = psum_banks(ph)
            ptk = [Tok(True) for _ in range(8)]
            xtok = [sb(ph, "xtok%d" % i, [128, 4, D], F32) for i in range(2)]
            ttok = [Tok() for _ in range(2)]
            xTt = [sb(ph, "xTt%d" % i, [128, KC, T], F32) for i in range(2)]
            tT = [[Tok() for _ in range(KC)] for _ in range(2)]
            for i in range(NT):
                b = i % 2
                S.dma("sp", "xtok%d" % b, xtok[b][:], xin[i * T:(i + 1) * T, :].rearrange("(s p) d -> p s d", p=128),
                      w=[ttok[b]])
                for k in range(KC):
                    pb = k % 4
                    for s in range(4):
                        S.op("pe", (lambda pb=pb, s=s, k=k, b=b: lambda e: e.transpose(
                            out=ps[pb][:, s * 128:(s + 1) * 128], in_=xtok[b][:, s, k * 128:(k + 1) * 128],
                            identity=ident[:]))(), r=[ttok[b], tk_const], w=[ptk[pb]])
                    if k % 2 == 0:
                        S.op("act", (lambda pb=pb, k=k, b=b: lambda e: e.copy(out=xTt[b][:, k, :], in_=ps[pb][:, :]))(),
                             r=[ptk[pb]], w=[tT[b][k]])
                    else:
                        S.op("dve", (lambda pb=pb, k=k, b=b: lambda e: e.tensor_copy(out=xTt[b][:, k, :], in_=ps[pb][:, :]))(),
                             r=[ptk[pb]], w=[tT[b][k]])
                S.dma("pool", "xTt%d" % b, XT(XA)[:, :, i * T:(i + 1) * T], xTt[b][:], r=tT[b])
            S.flush()

        def norm_tile(ph_bufs, src, i, W, xn_b, tk_xn, emask):
            xt, tk_xt, sq, tk_sq, rs, tk_rs, pst, tk_pst, psh, tk_psh, halo_bufs = ph_bufs
            X = XT(src)
            S.dma("sp", "xt", xt[:, :, 0:T], X[:, :, i * T:(i + 1) * T], w=[tk_xt])
            if W == 514:
                hl, hr, tk_hl, tk_hr, tk_xh = halo_bufs
                if i > 0:
                    S.dma("pool", "hl", hl[:], X[:, :, i * T - 16:i * T], w=[tk_hl])
                    S.op("dve", lambda e: e.tensor_copy(out=xt[:, :, 512:513], in_=hl[:, :, 15:16]), r=[tk_hl], w=[tk_xh])
                else:
                    S.op("dve", lambda e: e.memset(xt[:, :, 512:513], 0.0), w=[tk_xh])
                if i < NT - 1:
                    S.dma("pool", "hr", hr[:], X[:, :, (i + 1) * T:(i + 1) * T + 16], w=[tk_hr])
                    S.op("dve", lambda e: e.tensor_copy(out=xt[:, :, 513:514], in_=hr[:, :, 0:1]), r=[tk_hr], w=[tk_xh], tiny=True)
                else:
                    S.op("dve", lambda e: e.memset(xt[:, :, 513:514], 0.0), w=[tk_xh], tiny=True)
            for k in range(KC):
                b = k % 3
                S.op("act", (lambda k=k, b=b: lambda e: e.activation(out=sq[b][:, 0:W], in_=xt[:, k, 0:W], func=AF.Square))(),
                     r=[tk_xt] + ([halo_bufs[4]] if W == 514 else []), w=[tk_sq[b]])
                S.op("pe", (lambda k=k, b=b: lambda e: e.matmul(pst[:, :], ones_d[:], sq[b][:, 0:T], start=(k == 0), stop=(k == KC - 1)))(),
                     r=[tk_sq[b], tk_const], w=[tk_pst])
                if W == 514:
                    S.op("pe", (lambda k=k, b=b: lambda e: e.matmul(psh[:, :], ones_d[:], sq[b][:, 2:514], start=(k == 0), stop=(k == KC - 1)))(),
                         r=[tk_sq[b], tk_const], w=[tk_psh])
            S.op("act", lambda e: e.activation(out=rs[:, 0:T], in_=pst[:, :], func=AF.Sqrt, bias=eps_ap, scale=1.0),
                 r=[tk_const], w=[tk_rs, tk_pst])
            if W == 514:
                S.op("act", lambda e: e.activation(out=rs[:, 2:514], in_=psh[:, :], func=AF.Sqrt, bias=eps_ap, scale=1.0),
                     r=[tk_psh, tk_const], wadd=[tk_rs])
            S.op("dve", lambda e: e.reciprocal(out=rs[:, 0:T], in_=rs[:, 0:T]), r=[tk_rs], w=[tk_rs], tiny=True)
            if W == 514:
                S.op("dve", lambda e: e.reciprocal(out=rs[:, 512:514], in_=rs[:, 512:514]), r=[tk_rs], w=[tk_rs], tiny=True)
            if W == 514:
                S.op("dve", lambda e: e.tensor_tensor(out=rs[:, 512:514], in0=rs[:, 512:514], in1=cmt[:, 4 + 2 * i:6 + 2 * i], op=ALU.mult),
                     r=[tk_const, tk_rs], w=[tk_rs], tiny=True)
            for q in range(4):
                S.op("dve", (lambda q=q: lambda e: e.tensor_tensor(
                    out=xn_b[:, 4 * q:4 * q + 4, 0:W], in0=xt[:, 4 * q:4 * q + 4, 0:W],
                    in1=rs[:, 0:W].unsqueeze(1).broadcast_to([128, 4, W]), op=ALU.mult))(),
                     r=[tk_xt, tk_rs] + ([halo_bufs[4]] if W == 514 else []), w=[tk_xn[q]])

        def norm_bufs(ph, W, pst, tk_pst, psh, tk_psh):
            xt = sb(ph, "xt", [128, KC, W], F32)
            sq = [sb(ph, "sq%d" % i, [128, W], BF16) for i in range(3)]
            rs = sb(ph, "rs", [128, W], F32)
            hb = None
            if W == 514:
                hb = (sb(ph, "hl", [128, KC, 16], F32), sb(ph, "hr", [128, KC, 16], F32), Tok(), Tok(), Tok())
            return (xt, Tok(), sq, [Tok() for _ in range(3)], rs, Tok(), pst, tk_pst, psh, tk_psh, hb)

        def phase_ffn(li, src, dst):
            with ExitStack() as ph:
                ps = psum_banks(ph)
                ptk = [Tok(True) for _ in range(8)]
                nb = norm_bufs(ph, T, ps[6], ptk[6], None, None)
                xn = [sb(ph, "xn%d" % i, [128, KC, T], BF16) for i in range(2)]
                tk_xn = [[Tok() for _ in range(5)] for _ in range(2)]
                hbuf = sb(ph, "hbuf", [128, JF, T], BF16)
                tk_h = [Tok() for _ in range(JF)]
                wu = [sb(ph, "wu%d" % i, [128, 2, KC, 128], BF16) for i in range(3)]
                tk_wu = [Tok() for _ in range(3)]
                wd = [sb(ph, "wd%d" % i, [128, JF, 128], BF16) for i in range(2)]
                tk_wd = [Tok() for _ in range(2)]
                tt = [[sb(ph, "tt%d%d" % (g, i), [128, T], F32) for i in range(2)] for g in range(2)]
                tk_tt = [[Tok() for _ in range(2)] for _ in range(2)]
                xr = [sb(ph, "xr%d" % i, [128, T], F32) for i in range(2)]
                tk_xr = [Tok() for _ in range(2)]
                wup = wdst["wup", li].rearrange("p (j g k c) -> p j g k c", j=JF, g=2, k=KC)
                wdn = wdst["wdn", li].rearrange("p (m j c) -> p m j c", m=16, j=JF)
                fw, fb = pvoff["fw%d" % li], pvoff["fb%d" % li]
                X = XT(src)
                NHC = 2 * NT
                xth = sb(ph, "xth", [128, KC, NHC], F32)
                tk_xth = Tok()
                sqh = sb(ph, "sqh", [128, KC, NHC], BF16)
                tk_sqh = Tok()
                rsh = sb(ph, "rsh", [128, NHC], F32)
                tk_rsh = Tok()
                xh = sb(ph, "xh", [128, KC, NHC], BF16)
                tk_xh = Tok()
                uh = sb(ph, "uh", [128, 2 * JF, NHC], F32)
                tk_uh = Tok()
                h16 = [sb(ph, "h16_%d" % i, [128, KC, 16], F32) for i in range(2)]
                tk_h16 = [Tok() for _ in range(2)]
                hn = 0
                for i in range(NT):
                    for side in range(2):
                        col = 2 * i + side
                        if (side == 0 and i == 0) or (side == 1 and i == NT - 1):
                            S.op("dve", (lambda col=col: lambda e: e.memset(xth[:, :, col:col + 1], 0.0))(), w=[tk_xth], tiny=True)
                            continue
                        b = hn % 2
                        hn += 1
                        if side == 0:
                            S.dma("pool", "h16_%d" % b, h16[b][:], X[:, :, i * T - 16:i * T], w=[tk_h16[b]])
                            sc_ = 15
                        else:
                            S.dma("pool", "h16_%d" % b, h16[b][:], X[:, :, (i + 1) * T:(i + 1) * T + 16], w=[tk_h16[b]])
                            sc_ = 0
                        S.op("dve", (lambda col=col, b=b, sc_=sc_: lambda e: e.tensor_copy(out=xth[:, :, col:col + 1], in_=h16[b][:, :, sc_:sc_ + 1]))(),
                             r=[tk_h16[b]], w=[tk_xth], tiny=True)
                S.op("act", lambda e: e.activation(out=sqh[:], in_=xth[:], func=AF.Square), r=[tk_xth], w=[tk_sqh])
                for k in range(KC):
                    S.op("pe", (lambda k=k: lambda e: e.matmul(ps[7][:, 0:NHC], ones_d[:], sqh[:, k, :], start=(k == 0), stop=(k == KC - 1)))(),
                         r=[tk_sqh, tk_const], w=[ptk[7]])
                S.op("act", lambda e: e.activation(out=rsh[:, :], in_=ps[7][:, 0:NHC], func=AF.Sqrt, bias=eps_ap, scale=1.0),
                     r=[ptk[7], tk_const], w=[tk_rsh], tiny=True)
                S.op("dve", lambda e: e.reciprocal(out=rsh[:, :], in_=rsh[:, :]), r=[tk_rsh], w=[tk_rsh], tiny=True)
                S.op("dve", lambda e: e.tensor_tensor(out=rsh[:, :], in0=rsh[:, :], in1=cmt[:, 4:4 + NHC], op=ALU.mult),
                     r=[tk_rsh, tk_const], w=[tk_rsh], tiny=True)
                S.op("dve", lambda e: e.tensor_tensor(out=xh[:], in0=xth[:], in1=rsh[:, :].unsqueeze(1).broadcast_to([128, KC, NHC]), op=ALU.mult),
                     r=[tk_xth, tk_rsh], w=[tk_xh], tiny=True)
                GPB = 512 // NHC
                jn = 0
                npair = 2 * JF
                for j in range(JF):
                    b = jn % 3
                    jn += 1
                    S.dma("sp", "wu%d" % b, wu[b][:], wup[:, j], w=[tk_wu[b]])
                    for gv in range(2):
                        idx = j * 2 + gv
                        bk = (idx // GPB) % 2
                        pos = idx % GPB
                        for k in range(KC):
                            S.op("pe", (lambda bk=bk, pos=pos, b=b, gv=gv, k=k: lambda e: e.matmul(
                                ps[4 + bk][:, pos * NHC:(pos + 1) * NHC], wu[b][:, gv, k, :], xh[:, k, :], start=(k == 0), stop=(k == KC - 1)))(),
                                r=[tk_wu[b], tk_xh], w=[ptk[4 + bk]])
                        if pos == GPB - 1 or idx == npair - 1:
                            base = (idx // GPB) * GPB
                            cnt = idx - base + 1
                            S.op("act", (lambda bk=bk, base=base, cnt=cnt: lambda e: e.copy(
                                out=uh[:, base:base + cnt, :], in_=ps[4 + bk][:, 0:cnt * NHC].rearrange("p (g h) -> p g h", h=NHC)))(),
                                r=[ptk[4 + bk]], w=[tk_uh])
                norm_tile(nb, src, 0, T, xn[0], tk_xn[0], False)
                for i in range(NT):
                    cur = i % 2
                    xc = xn[cur]
                    for j in range(JF):
                        b = jn % 3
                        jp = jn % 2
                        jn += 1
                        S.dma("sp", "wu%d" % b, wu[b][:], wup[:, j], w=[tk_wu[b]])
                        for gv in range(2):
                            pm = ps[gv * 2 + jp]
                            for k in range(KC):
                                S.op("pe", (lambda pm=pm, b=b, gv=gv, k=k, xc=xc: lambda e: e.matmul(
                                    pm[:, :], wu[b][:, gv, k, :], xc[:, k, :], start=(k == 0), stop=(k == KC - 1)))(),
                                    r=[tk_wu[b], tk_xn[cur][k // 4]], w=[ptk[gv * 2 + jp]])
                        for gv in range(2):
                            pm = ps[gv * 2 + jp]
                            idx = j * 2 + gv
                            col = gv * JF + j
                            t_ = tt[gv][jp]
                            w0 = pvt[:, fw + col:fw + col + 1]
                            w1 = pvt[:, fw + 2 * JF + col:fw + 2 * JF + col + 1]
                            w2 = pvt[:, fw + 4 * JF + col:fw + 4 * JF + col + 1]
                            bb = pvt[:, fb + col:fb + col + 1]
                            tk = tk_tt[gv][jp]
                            S.op("act", (lambda t_=t_, pm=pm, w1=w1, bb=bb: lambda e: e.activation(
                                out=t_[:, :], in_=pm[:, :], func=AF.Identity, bias=bb, scale=w1))(),
                                r=[ptk[gv * 2 + jp], tk_const], w=[tk])
                            S.op("dve", (lambda t_=t_, pm=pm, w0=w0: lambda e: e.scalar_tensor_tensor(
                                out=t_[:, 1:T], in0=pm[:, 0:T - 1], scalar=w0, in1=t_[:, 1:T], op0=ALU.mult, op1=ALU.add))(),
                                r=[ptk[gv * 2 + jp], tk], w=[tk])
                            S.op("dve", (lambda t_=t_, pm=pm, w2=w2: lambda e: e.scalar_tensor_tensor(
                                out=t_[:, 0:T - 1], in0=pm[:, 1:T], scalar=w2, in1=t_[:, 0:T - 1], op0=ALU.mult, op1=ALU.add))(),
                                r=[ptk[gv * 2 + jp], tk], w=[tk])
                            S.op("dve", (lambda t_=t_, idx=idx, i=i, w0=w0: lambda e: e.scalar_tensor_tensor(
                                out=t_[:, 0:1], in0=uh[:, idx, 2 * i:2 * i + 1], scalar=w0, in1=t_[:, 0:1], op0=ALU.mult, op1=ALU.add))(),
                                r=[tk_uh, tk], w=[tk])
                            S.op("dve", (lambda t_=t_, idx=idx, i=i, w2=w2: lambda e: e.scalar_tensor_tensor(
                                out=t_[:, T - 1:T], in0=uh[:, idx, 2 * i + 1:2 * i + 2], scalar=w2, in1=t_[:, T - 1:T], op0=ALU.mult, op1=ALU.add))(),
                                r=[tk_uh, tk], w=[tk], tiny=True)
                        S.op("act", (lambda jp=jp: lambda e: e.activation(out=tt[0][jp][:, :], in_=tt[0][jp][:, :], func=AF.Silu))(),
                             r=[tk_tt[0][jp]], w=[tk_tt[0][jp]])
                        S.op("dve", (lambda jp=jp, j=j: lambda e: e.tensor_tensor(
                            out=hbuf[:, j, :], in0=tt[0][jp][:, :], in1=tt[1][jp][:, :], op=ALU.mult))(),
                            r=[tk_tt[0][jp], tk_tt[1][jp]], w=[tk_h[j]])
                        if j == cfg.ffn_norm_j and i + 1 < NT:
                            norm_tile(nb, src, i + 1, T, xn[1 - cur], tk_xn[1 - cur], False)
                    for m in range(16):
                        b = m % 2
                        S.dma("sp", "wd%d" % b, wd[b][:], wdn[:, m], w=[tk_wd[b]])
                        S.dma("pool", "xr%d" % b, xr[b][:], src[m * 128:(m + 1) * 128, i * T:(i + 1) * T], w=[tk_xr[b]])
                        for jc in range(JF):
                            S.op("pe", (lambda b=b, jc=jc: lambda e: e.matmul(
                                ps[6 + b][:, :], wd[b][:, jc, :], hbuf[:, jc, :], start=(jc == 0), stop=(jc == JF - 1)))(),
                                r=[tk_wd[b], tk_h[jc]], w=[ptk[6 + b]])
                        S.op("dve", (lambda b=b: lambda e: e.tensor_tensor(out=xr[b][:, :], in0=ps[6 + b][:, :], in1=xr[b][:, :], op=ALU.add))(),
                             r=[ptk[6 + b], tk_xr[b]], w=[tk_xr[b]])
                        S.dma("pool", "xr%d" % b, dst[m * 128:(m + 1) * 128, i * T:(i + 1) * T], xr[b][:], r=[tk_xr[b]])
                S.flush()

        def phase_conv(li, src, dst):
            with ExitStack() as ph:
                ps = psum_banks(ph)
                ptk = [Tok(True) for _ in range(8)]
                nb = norm_bufs(ph, T, ps[5], ptk[5], None, None)
                xn = [sb(ph, "xn%d" % i, [128, KC, T], BF16) for i in range(2)]
                tk_xn = [[Tok() for _ in range(5)] for _ in range(2)]
                wp = [sb(ph, "wp%d" % i, [128, 2, KC, 128], BF16) for i in range(3)]
                tk_wp = [Tok() for _ in range(3)]
                sig = [sb(ph, "sig%d" % i, [128, T], F32) for i in range(2)]
                tk_sig = [Tok() for _ in range(2)]
                gl = [sb(ph, "gl%d" % i, [128, T], F32) for i in range(2)]
                tk_gl = [Tok() for _ in range(2)]
                wsrc_ = wdst["wpw1", li].rearrange("p (j g k c) -> p j g k c", j=16, g=2, k=KC)
                bo = pvoff["bpw1%d" % li]
                norm_tile(nb, src, 0, T, xn[0], tk_xn[0], False)
                jn = 0
                for i in range(NT):
                    cur = i % 2
                    xc = xn[cur]
                    for j in range(16):
                        b = jn % 3
                        jp = jn % 2
                        jn += 1
                        S.dma("sp", "wp%d" % b, wp[b][:], wsrc_[:, j], w=[tk_wp[b]])
                        for ag in range(2):
                            for k in range(KC):
                                S.op("pe", (lambda ag=ag, jp=jp, b=b, k=k, xc=xc: lambda e: e.matmul(
                                    ps[ag * 2 + jp][:, :], wp[b][:, ag, k, :], xc[:, k, :], start=(k == 0), stop=(k == KC - 1)))(),
                                    r=[tk_wp[b], tk_xn[cur][k // 4]], w=[ptk[ag * 2 + jp]])
                        S.op("act", (lambda jp=jp, j=j: lambda e: e.activation(
                            out=sig[jp][:, :], in_=ps[2 + jp][:, :], func=AF.Sigmoid, bias=pvt[:, bo + 16 + j:bo + 17 + j], scale=1.0))(),
                            r=[ptk[2 + jp], tk_const], w=[tk_sig[jp]])
                        S.op("dve", (lambda jp=jp, j=j: lambda e: e.scalar_tensor_tensor(
                            out=gl[jp][:, :], in0=ps[jp][:, :], scalar=pvt[:, bo + j:bo + j + 1], in1=sig[jp][:, :], op0=ALU.add, op1=ALU.mult))(),
                            r=[ptk[jp], tk_sig[jp]], w=[tk_gl[jp]])
                        S.dma("pool", "gl%d" % jp, G[j * 128:(j + 1) * 128, i * T:(i + 1) * T], gl[jp][:], r=[tk_gl[jp]])
                        if j == 6 and i + 1 < NT:
                            norm_tile(nb, src, i + 1, T, xn[1 - cur], tk_xn[1 - cur], False)
                S.flush()
            with ExitStack() as ph:
                ps = psum_banks(ph)
                ptk = [Tok(True) for _ in range(8)]
                WG = T + CK - 1
                HW = CK // 2
                ge = [sb(ph, "ge%d" % i, [128, WG], F32) for i in range(4)]
                tk_ge = [Tok() for _ in range(4)]
                cb = sb(ph, "cb", [128, KC, T], F32)
                tk_c = [Tok() for _ in range(KC)]
                PE_TAPS = [t for t in range(0, CK, 2)] if cfg.conv_pe else []
                gb = [sb(ph, "gb%d" % i, [128, WG], BF16) for i in range(2)]
                tk_gb = [Tok() for _ in range(2)]
                dg = [sb(ph, "dg%d" % i, [128, max(1, len(PE_TAPS)), 128], BF16) for i in range(2)]
                tk_dg = [Tok() for _ in range(2)]
                cbf = [sb(ph, "cbf%d" % i, [128, T], BF16) for i in range(3)]
                tk_cbf = [Tok() for _ in range(3)]
                csq = [sb(ph, "csq%d" % i, [128, T], BF16) for i in range(3)]
                tk_csq = [Tok() for _ in range(3)]
                mu = sb(ph, "mu", [128, T], F32)
                rstd = sb(ph, "rstd", [128, T], F32)
                tk_st = Tok()
                tmp = [sb(ph, "tmp%d" % i, [128, T], F32) for i in range(2)]
                tk_tmp = [Tok() for _ in range(2)]
                sbf = sb(ph, "sbf", [128, KC, T], BF16)
                tk_s = [Tok() for _ in range(KC)]
                w2 = sb(ph, "w2", [128, KC, D], BF16)
                tk_w2 = Tok()
                xr = [sb(ph, "xr%d" % i, [128, T], F32) for i in range(2)]
                tk_xr = [Tok() for _ in range(2)]
                yo = [sb(ph, "yo%d" % i, [128, T], F32) for i in range(2)]
                tk_yo = [Tok() for _ in range(2)]
                S.dma("sp", "w2", w2[:], wdst["wpw2", li].rearrange("p (k n) -> p k n", k=KC), w=[tk_w2])
                wo_, bdo = pvoff["wdw%d" % li], pvoff["bdw%d" % li]
                lg, lb, b2 = pvoff["lng%d" % li], pvoff["lnb%d" % li], pvoff["bpw2%d" % li]
                gn = 0
                for i in range(NT):
                    lo = i * T - HW
                    hi = (i + 1) * T + HW
                    clo, chi = max(lo, 0), min(hi, NTOK)
                    for k in range(KC):
                        b = gn % 4
                        gn += 1
                        g_ = ge[b]
                        if clo > lo:
                            S.op("dve", (lambda g_=g_: lambda e: e.memset(g_[:, 0:HW], 0.0))(), w=[tk_ge[b]])
                        if chi < hi:
                            S.op("dve", (lambda g_=g_: lambda e: e.memset(g_[:, WG - HW:WG], 0.0))(), w=[tk_ge[b]])
                        S.dma("sp", "ge%d" % b, g_[:, clo - lo:chi - lo], G[k * 128:(k + 1) * 128, clo:chi], w=[tk_ge[b]])
                        S.op("dve", (lambda g_=g_, i=i: lambda e: e.tensor_scalar(
                            out=g_[:, 0:HW], in0=g_[:, 0:HW], scalar1=cmt[:, 4 + 2 * i:5 + 2 * i], scalar2=None, op0=ALU.mult))(),
                            r=[tk_const], w=[tk_ge[b]])
                        S.op("dve", (lambda g_=g_, i=i: lambda e: e.tensor_scalar(
                            out=g_[:, WG - HW:WG], in0=g_[:, WG - HW:WG], scalar1=cmt[:, 5 + 2 * i:6 + 2 * i], scalar2=None, op0=ALU.mult))(),
                            r=[tk_const], w=[tk_ge[b]], tiny=True)
                        S.op("act", (lambda g_=g_, k=k: lambda e: e.activation(
                            out=cb[:, k, :], in_=g_[:, HW:HW + T], func=AF.Identity,
                            bias=pvt[:, bdo + k:bdo + k + 1], scale=pvt[:, wo_ + HW * 16 + k:wo_ + HW * 16 + k + 1]))(),
                            r=[tk_ge[b], tk_const], w=[tk_c[k]])
                        ceng = "dve"
                        if PE_TAPS:
                            g2 = (gn - 1) % 2
                            S.op("act", (lambda g_=g_, g2=g2: lambda e: e.copy(out=gb[g2][:, :], in_=g_[:, :]))(),
                                 r=[tk_ge[b]], w=[tk_gb[g2]])
                            for ti, tap in enumerate(PE_TAPS):
                                S.op("act", (lambda g2=g2, ti=ti, tap=tap, k=k: lambda e: e.activation(
                                    out=dg[g2][:, ti, :], in_=identb[:], func=AF.Copy,
                                    scale=pvt[:, wo_ + tap * 16 + k:wo_ + tap * 16 + k + 1]))(),
                                    r=[tk_const], w=[tk_dg[g2]])
                            for ti, tap in enumerate(PE_TAPS):
                                S.op("pe", (lambda g2=g2, ti=ti, tap=tap: lambda e: e.matmul(
                                    ps[4 + g2][:, :], dg[g2][:, ti, :], gb[g2][:, tap:tap + T], start=(ti == 0), stop=(ti == len(PE_TAPS) - 1)))(),
                                    r=[tk_dg[g2], tk_gb[g2]], w=[ptk[4 + g2]])
                        for tap in range(CK):
                            if tap == HW or tap in PE_TAPS:
                                continue
                            S.op(ceng, (lambda g_=g_, k=k, tap=tap: lambda e: e.scalar_tensor_tensor(
                                out=cb[:, k, :], in0=g_[:, tap:tap + T], scalar=pvt[:, wo_ + tap * 16 + k:wo_ + tap * 16 + k + 1],
                                in1=cb[:, k, :], op0=ALU.mult, op1=ALU.add))(),
                                r=[tk_ge[b]], w=[tk_c[k]])
                        if PE_TAPS:
                            S.op("dve", (lambda k=k, g2=g2: lambda e: e.tensor_tensor(out=cb[:, k, :], in0=ps[4 + g2][:, :], in1=cb[:, k, :], op=ALU.add))(),
                                 r=[ptk[4 + g2], tk_c[k]], w=[tk_c[k]])
                        bb = k % 3
                        S.op("act", (lambda k=k, bb=bb: lambda e: e.copy(out=cbf[bb][:, :], in_=cb[:, k, :]))(),
                             r=[tk_c[k]], w=[tk_cbf[bb]])
                        S.op("act", (lambda k=k, bb=bb: lambda e: e.activation(out=csq[bb][:, :], in_=cb[:, k, :], func=AF.Square))(),
                             r=[tk_c[k]], w=[tk_csq[bb]])
                        S.op("pe", (lambda k=k, bb=bb: lambda e: e.matmul(ps[0][:, :], ones_d[:], cbf[bb][:, :], start=(k == 0), stop=(k == KC - 1)))(),
                             r=[tk_cbf[bb], tk_const], w=[ptk[0]])
                        S.op("pe", (lambda k=k, bb=bb: lambda e: e.matmul(ps[1][:, :], ones_d[:], csq[bb][:, :], start=(k == 0), stop=(k == KC - 1)))(),
                             r=[tk_csq[bb], tk_const], w=[ptk[1]])
                    S.op("dve", lambda e: e.tensor_copy(out=mu[:, :], in_=ps[0][:, :]), r=[ptk[0]], w=[tk_st])
                    S.op("dve", lambda e: e.tensor_tensor(out=rstd[:, :], in0=mu[:, :], in1=mu[:, :], op=ALU.mult), r=[tk_st], w=[tk_st])
                    S.op("dve", lambda e: e.tensor_tensor(out=rstd[:, :], in0=ps[1][:, :], in1=rstd[:, :], op=ALU.subtract), r=[ptk[1]], w=[tk_st])
                    S.op("dve", lambda e: e.tensor_scalar(out=rstd[:, :], in0=rstd[:, :], scalar1=0.0, scalar2=None, op0=ALU.max), w=[tk_st])
                    S.op("act", lambda e: e.activation(out=rstd[:, :], in_=rstd[:, :], func=AF.Sqrt, bias=eps_ap, scale=1.0),
                         r=[tk_const], w=[tk_st])
                    S.op("dve", lambda e: e.reciprocal(out=rstd[:, :], in_=rstd[:, :]), w=[tk_st], tiny=True)
                    for k in range(KC):
                        tb = k % 2
                        S.op("dve", (lambda k=k, tb=tb: lambda e: e.tensor_tensor(out=tmp[tb][:, :], in0=cb[:, k, :], in1=mu[:, :], op=ALU.subtract))(),
                             r=[tk_c[k], tk_st], w=[tk_tmp[tb]])
                        S.op("dve", (lambda k=k, tb=tb: lambda e: e.tensor_tensor(out=tmp[tb][:, :], in0=tmp[tb][:, :], in1=rstd[:, :], op=ALU.mult))(),
                             r=[tk_st], w=[tk_tmp[tb]])
                        S.op("act", (lambda k=k, tb=tb: lambda e: e.activation(
                            out=sbf[:, k, :], in_=tmp[tb][:, :], func=AF.Silu, bias=pvt[:, lb + k:lb + k + 1], scale=pvt[:, lg + k:lg + k + 1]))(),
                            r=[tk_tmp[tb], tk_const], w=[tk_s[k]])
                    for m in range(16):
                        b = m % 2
                        S.dma("pool", "xr%d" % b, xr[b][:], src[m * 128:(m + 1) * 128, i * T:(i + 1) * T], w=[tk_xr[b]])
                        for k in range(KC):
                            S.op("pe", (lambda b=b, k=k, m=m: lambda e: e.matmul(
                                ps[2 + b][:, :], w2[:, k, m * 128:(m + 1) * 128], sbf[:, k, :], start=(k == 0), stop=(k == KC - 1)))(),
                                r=[tk_w2, tk_s[k]], w=[ptk[2 + b]])
                        S.op("dve", (lambda b=b, m=m: lambda e: e.scalar_tensor_tensor(
                            out=yo[b][:, :], in0=ps[2 + b][:, :], scalar=pvt[:, b2 + m:b2 + m + 1], in1=xr[b][:, :], op0=ALU.add, op1=ALU.add))(),
                            r=[ptk[2 + b], tk_xr[b], tk_const], w=[tk_yo[b]])
                        S.dma("pool", "yo%d" % b, dst[m * 128:(m + 1) * 128, i * T:(i + 1) * T], yo[b][:], r=[tk_yo[b]])
                S.flush()

        def phase_attn(li, src, dst):
            with ExitStack() as ph:
                ps = psum_banks(ph)
                ptk = [Tok(True) for _ in range(8)]
                nb = norm_bufs(ph, T, ps[7], ptk[7], None, None)
                xn = [sb(ph, "xn%d" % i, [128, KC, T], BF16) for i in range(2)]
                tk_xn = [[Tok() for _ in range(5)] for _ in range(2)]
                wq = [sb(ph, "wq%d" % i, [128, KC, 128], BF16) for i in range(3)]
                tk_wq = [Tok() for _ in range(3)]
                wv = sb(ph, "wv", [128, KC, 512], BF16)
                tk_wv = Tok()
                rc = [sb(ph, "rc%d" % i, [128, T], F32) for i in range(2)]
                rsn = [sb(ph, "rsn%d" % i, [128, T], F32) for i in range(2)]
                tk_rt = [Tok() for _ in range(2)]
                hsq = [sb(ph, "hsq%d" % i, [128, T], BF16) for i in range(2)]
                tk_hsq = [Tok() for _ in range(2)]
                hrs = [sb(ph, "hrs%d" % i, [128, T], F32) for i in range(2)]
                tk_hrs = [Tok() for _ in range(2)]
                qn = [sb(ph, "qn%d" % i, [128, T], BF16) for i in range(2)]
                tk_qn = [Tok() for _ in range(2)]
                t1 = [sb(ph, "t1%d" % i, [128, T], F32) for i in range(2)]
                tk_t1 = [Tok() for _ in range(2)]
                t2 = [sb(ph, "t2%d" % i, [128, T], F32) for i in range(2)]
                tk_t2 = [Tok() for _ in range(2)]
                qr = [sb(ph, "qr%d" % i, [128, T], BF16) for i in range(3)]
                tk_qr = [Tok() for _ in range(3)]
                vsb = [sb(ph, "vsb%d" % i, [128, 512], BF16) for i in range(2)]
                tk_vsb = [Tok() for _ in range(2)]
                wsrc_ = wdst["wqkv", li]
                wqk = wsrc_[:, 0:40960].rearrange("p (n k c) -> p n k c", n=20, k=KC)
                S.dma("sp", "wv", wv[:], wsrc_[:, 40960:49152].rearrange("p (k c) -> p k c", k=KC), w=[tk_wv])
                VSv = VS.rearrange("p (g c d) -> p g c d", g=NKV, c=NKC)
                norm_tile(nb, src, 0, T, xn[0], tk_xn[0], False)
                cn = 0
                for i in range(NT):
                    cur = i % 2
                    xc = xn[cur]
                    rb = i % 2
                    S.dma("pool", "rc%d" % rb, rc[rb][:], ropec_d[:, i * T:(i + 1) * T], w=[tk_rt[rb]])
                    S.dma("pool", "rsn%d" % rb, rsn[rb][:], ropes_d[:, i * T:(i + 1) * T], wadd=[tk_rt[rb]])
                    for c in range(20):
                        b = cn % 3
                        cp = cn % 2
                        q3 = cn % 3
                        cn += 1
                        S.dma("sp", "wq%d" % b, wq[b][:], wqk[:, c], w=[tk_wq[b]])
                        for k in range(KC):
                            S.op("pe", (lambda cp=cp, b=b, k=k, xc=xc: lambda e: e.matmul(
                                ps[cp][:, :], wq[b][:, k, :], xc[:, k, :], start=(k == 0), stop=(k == KC - 1)))(),
                                r=[tk_wq[b], tk_xn[cur][k // 4]], w=[ptk[cp]])
                        S.op("act", (lambda cp=cp: lambda e: e.activation(out=hsq[cp][:, :], in_=ps[cp][:, :], func=AF.Square))(),
                             r=[ptk[cp]], w=[tk_hsq[cp]])
                        S.op("pe", (lambda cp=cp: lambda e: e.matmul(ps[2 + cp][:, :], ones_h[:], hsq[cp][:, :], start=True, stop=True))(),
                             r=[tk_hsq[cp], tk_const], w=[ptk[2 + cp]])
                        S.op("act", (lambda cp=cp: lambda e: e.activation(out=hrs[cp][:, :], in_=ps[2 + cp][:, :], func=AF.Sqrt, bias=eps_ap, scale=1.0))(),
                             r=[ptk[2 + cp], tk_const], w=[tk_hrs[cp]])
                        S.op("dve", (lambda cp=cp: lambda e: e.reciprocal(out=hrs[cp][:, :], in_=hrs[cp][:, :]))(), w=[tk_hrs[cp]], tiny=True)
                        gcol = pvc("qg%d" % li) if c < 16 else pvc("kg%d" % li)
                        S.op("dve", (lambda cp=cp, gcol=gcol: lambda e: e.scalar_tensor_tensor(
                            out=qn[cp][:, :], in0=ps[cp][:, :], scalar=gcol, in1=hrs[cp][:, :], op0=ALU.mult, op1=ALU.mult))(),
                            r=[ptk[cp], tk_hrs[cp], tk_const], w=[tk_qn[cp]])
                        S.op("pe", (lambda cp=cp: lambda e: e.matmul(ps[4 + cp][:, :], permb[:], qn[cp][:, :], start=True, stop=True))(),
                             r=[tk_qn[cp], tk_const], w=[ptk[4 + cp]])
                        S.op("dve", (lambda cp=cp, rb=rb: lambda e: e.tensor_tensor(out=t1[cp][:, :], in0=qn[cp][:, :], in1=rc[rb][:, :], op=ALU.mult))(),
                             r=[tk_qn[cp], tk_rt[rb]], w=[tk_t1[cp]])
                        S.op("dve", (lambda cp=cp, rb=rb: lambda e: e.tensor_tensor(out=t2[cp][:, :], in0=ps[4 + cp][:, :], in1=rsn[rb][:, :], op=ALU.mult))(),
                             r=[ptk[4 + cp], tk_rt[rb]], w=[tk_t2[cp]])
                        S.op("dve", (lambda cp=cp, q3=q3: lambda e: e.tensor_tensor(out=qr[q3][:, :], in0=t1[cp][:, :], in1=t2[cp][:, :], op=ALU.add))(),
                             r=[tk_t1[cp], tk_t2[cp]], w=[tk_qr[q3]])
                        if c < 16:
                            S.dma("pool", "qr%d" % q3, QS[c * 128:(c + 1) * 128, i * T:(i + 1) * T], qr[q3][:], r=[tk_qr[q3]])
                        else:
                            S.dma("pool", "qr%d" % q3, KS[(c - 16) * 128:(c - 15) * 128, i * T:(i + 1) * T], qr[q3][:], r=[tk_qr[q3]])
                        if c == 8 and i + 1 < NT:
                            norm_tile(nb, src, i + 1, T, xn[1 - cur], tk_xn[1 - cur], False)
                    for s in range(4):
                        vb = s % 2
                        for k in range(KC):
                            S.op("pe", (lambda s=s, k=k, xc=xc: lambda e: e.matmul(
                                ps[6][:, :], xc[:, k, s * 128:(s + 1) * 128], wv[:, k, :], start=(k == 0), stop=(k == KC - 1)))(),
                                r=[tk_wv, tk_xn[cur][k // 4]], w=[ptk[6]])
                        S.op("act", (lambda vb=vb: lambda e: e.copy(out=vsb[vb][:, :], in_=ps[6][:, :]))(), r=[ptk[6]], w=[tk_vsb[vb]])
                        S.dma("pool", "vsb%d" % vb, VSv[:, :, 4 * i + s, :], vsb[vb][:, :].rearrange("p (g d) -> p g d", g=NKV),
                              r=[tk_vsb[vb]])
                S.flush()
            with ExitStack() as ph:
                ps = psum_banks(ph)
                ptk = [Tok(True) for _ in range(8)]
                kt = [sb(ph, "kt%d" % i, [128, NTOK], BF16) for i in range(2)]
                vt = [sb(ph, "vt%d" % i, [128, NKC, 128], BF16) for i in range(2)]
                tk_kv = [Tok() for _ in range(2)]
                qt = [sb(ph, "qt%d" % i, [128, T], BF16) for i in range(3)]
                tk_qt = [Tok() for _ in range(3)]
                pt = [sb(ph, "pt%d" % i, [128, T], BF16) for i in range(4)]
                tk_pt = [Tok() for _ in range(4)]
                rcp = [sb(ph, "rcp%d" % i, [128, T], F32) for i in range(2)]
                tk_rcp = [Tok() for _ in range(2)]
                ot = [sb(ph, "ot%d" % i, [128, T], BF16) for i in range(2)]
                tk_ot = [Tok() for _ in range(2)]
                accd = [sb(ph, "accd%d" % i, [128, T], F32) for i in range(2)]
                accp = [sb(ph, "accp%d" % i, [128, T], F32) for i in range(2)]
                tk_accd = [Tok() for _ in range(2)]
                tk_accp = [Tok() for _ in range(2)]
                gnb = sb(ph, "gnb", [128, 256], F32)
                bias = sb(ph, "bias", [128, 8], F32)
                tk_b = Tok()
                S.dma("sp", "gnb", gnb[:], gn_d[0:1, li * 256:(li + 1) * 256].partition_broadcast(128), w=[tk_b])
                S.op("dve", lambda e: e.reduce_max(out=bias[:, 4:5], in_=gnb[:, 0:128], axis=AX.X, apply_absolute_value=True), r=[tk_b], wadd=[tk_b], tiny=True)
                S.op("dve", lambda e: e.reduce_max(out=bias[:, 5:6], in_=gnb[:, 128:256], axis=AX.X, apply_absolute_value=True), wadd=[tk_b], tiny=True)
                S.op("dve", lambda e: e.tensor_tensor(out=bias[:, 6:7], in0=bias[:, 4:5], in1=bias[:, 5:6], op=ALU.mult), r=[tk_b], wadd=[tk_b], tiny=True)
                S.op("dve", lambda e: e.tensor_scalar(out=bias[:, 7:8], in0=bias[:, 6:7], scalar1=-float(np.sqrt(HD)), scalar2=None, op0=ALU.mult),
                     r=[tk_b], wadd=[tk_b], tiny=True)
                S.op("dve", lambda e: e.tensor_scalar(out=bias[:, 0:4], in0=cmt[:, 0:4], scalar1=bias[:, 7:8], scalar2=None, op0=ALU.add),
                     r=[tk_b, tk_const], wadd=[tk_b], tiny=True)
                scale = float(HD ** -0.5)
                items = []
                for g in range(NKV):
                    for qg in range(NT):
                        for hh in range(4):
                            for c in range(NKC):
                                items.append((g, qg, hh, c))
                LA = 2
                n_it = len(items)
                hn = 0
                for n in range(n_it + LA):
                    if n < n_it:
                        g, qg, hh, c = items[n]
                        gb = g % 2
                        hidx = (g * NT + qg) * 4 + hh
                        if qg == 0 and hh == 0 and c == 0:
                            S.dma("sp", "kt%d" % gb, kt[gb][:], KS[g * 128:(g + 1) * 128, :], w=[tk_kv[gb]])
                            S.dma("sp", "vt%d" % gb, vt[gb][:], VS.rearrange("p (g c d) -> p g c d", g=NKV, c=NKC)[:, g], wadd=[tk_kv[gb]])
                        if c == 0:
                            heads = [(g, qg, hh, hidx)] if n == 0 else []
                            if n + NKC < n_it:
                                g2, qg2, hh2, _ = items[n + NKC]
                                heads.append((g2, qg2, hh2, hidx + 1))
                            for (g2, qg2, hh2, hx) in heads:
                                qb2 = hx % 3
                                h2 = 4 * g2 + hh2
                                S.dma("sp", "qt%d" % qb2, qt[qb2][:], QS[h2 * 128:(h2 + 1) * 128, qg2 * T:(qg2 + 1) * T], w=[tk_qt[qb2]])
                        qb = hidx % 3
                        sbk = n % 3
                        S.op("pe", (lambda sbk=sbk, gb=gb, c=c, qb=qb: lambda e: e.matmul(
                            ps[sbk][:, :], kt[gb][:, c * 128:(c + 1) * 128], qt[qb][:, :], start=True, stop=True))(),
                            r=[tk_kv[gb], tk_qt[qb]], w=[ptk[sbk]])
                    m = n - LA
                    if m >= 0:
                        g, qg, hh, c = items[m]
                        gb = g % 2
                        hidx = (g * NT + qg) * 4 + hh
                        hp = hidx % 2
                        sbk = m % 3
                        pb = m % 4
                        bi = (c // (NKC // 2)) * 2 + (qg // (NT // 2))
                        S.op("act", (lambda sbk=sbk, pb=pb, bi=bi: lambda e: e.activation(
                            out=pt[pb][:, :], in_=ps[sbk][:, :], func=AF.Exp, bias=bias[:, bi:bi + 1], scale=scale))(),
                            r=[ptk[sbk], tk_b], w=[tk_pt[pb]])
                        S.op("pe", (lambda hp=hp, gb=gb, c=c, pb=pb: lambda e: e.matmul(
                            ps[3 + hp][:, :], vt[gb][:, c, :], pt[pb][:, :], start=(c == 0), stop=(c == NKC - 1)))(),
                            r=[tk_kv[gb], tk_pt[pb]], w=[ptk[3 + hp]])
                        if cfg.attn_sum_pe:
                            S.op("pe", (lambda hp=hp, pb=pb, c=c: lambda e: e.matmul(
                                ps[5 + hp][:, :], ones_1[:], pt[pb][:, :], start=(c == 0), stop=(c == NKC - 1)))(),
                                r=[tk_pt[pb], tk_const], w=[ptk[5 + hp]])
                        else:
                            if c % 8 == 7:
                                S.op("pe", (lambda hp=hp, pb=pb, c=c: lambda e: e.matmul(
                                    ps[5 + hp][:, :], ones_1[:], pt[pb][:, :], start=(c == 7), stop=False))(),
                                    r=[tk_pt[pb], tk_const], w=[ptk[5 + hp]])
                            else:
                                acc = accd[hp]
                                tka = tk_accd[hp]
                                if c == 0:
                                    S.op("dve", (lambda acc=acc, pb=pb: lambda e: e.tensor_copy(out=acc[:, :], in_=pt[pb][:, :]))(),
                                         r=[tk_pt[pb]], w=[tka])
                                else:
                                    S.op("dve", (lambda acc=acc, pb=pb: lambda e: e.tensor_tensor(out=acc[:, :], in0=acc[:, :], in1=pt[pb][:, :], op=ALU.add))(),
                                         r=[tk_pt[pb], tka], w=[tka])
                            if c == NKC - 1:
                                S.op("pe", (lambda hp=hp: lambda e: e.matmul(ps[5 + hp][:, :], ones_f[:], accd[hp][:, :], start=False, stop=True))(),
                                     r=[tk_accd[hp], tk_const], w=[ptk[5 + hp]])
                        if c == NKC - 1:
                            h = 4 * g + hh
                            S.op("dve", (lambda hp=hp: lambda e: e.reciprocal(out=rcp[hp][:, :], in_=ps[5 + hp][:, :]))(),
                                 r=[ptk[5 + hp]], w=[tk_rcp[hp]])
                            S.op("dve", (lambda hp=hp: lambda e: e.tensor_tensor(out=ot[hp][:, :], in0=ps[3 + hp][:, :], in1=rcp[hp][:, :], op=ALU.mult))(),
                                 r=[ptk[3 + hp], tk_rcp[hp]], w=[tk_ot[hp]])
                            S.dma("pool", "ot%d" % hp, OS[h * 128:(h + 1) * 128, qg * T:(qg + 1) * T], ot[hp][:], r=[tk_ot[hp]])
                S.flush()
            with ExitStack() as ph:
                ps = psum_banks(ph)
                ptk = [Tok(True) for _ in range(8)]
                wo = sb(ph, "wo", [128, KC, D], BF16)
                tk_wo = Tok()
                oin = [sb(ph, "oin%d" % i, [128, KC, T], BF16) for i in range(2)]
                tk_oin = [Tok() for _ in range(2)]
                xr = [sb(ph, "xr%d" % i, [128, T], F32) for i in range(3)]
                tk_xr = [Tok() for _ in range(3)]
                yo = [sb(ph, "yo%d" % i, [128, T], F32) for i in range(3)]
                tk_yo = [Tok() for _ in range(3)]
                S.dma("sp", "wo", wo[:], wdst["wo", li].rearrange("p (k n) -> p k n", k=KC), w=[tk_wo])
                mn = 0
                for i in range(NT):
                    ob = i % 2
                    S.dma("sp", "oin%d" % ob, oin[ob][:], OS.rearrange("(h p) t -> p h t", p=128)[:, :, i * T:(i + 1) * T], w=[tk_oin[ob]])
                    for m in range(16):
                        b = mn % 3
                        pb = mn % 4
                        mn += 1
                        S.dma("pool", "xr%d" % b, xr[b][:], src[m * 128:(m + 1) * 128, i * T:(i + 1) * T], w=[tk_xr[b]])
                        for k in range(KC):
                            S.op("pe", (lambda pb=pb, k=k, m=m, ob=ob: lambda e: e.matmul(
                                ps[pb][:, :], wo[:, k, m * 128:(m + 1) * 128], oin[ob][:, k, :], start=(k == 0), stop=(k == KC - 1)))(),
                                r=[tk_wo, tk_oin[ob]], w=[ptk[pb]])
                        S.op("dve", (lambda pb=pb, b=b: lambda e: e.tensor_tensor(out=yo[b][:, :], in0=ps[pb][:, :], in1=xr[b][:, :], op=ALU.add))(),
                             r=[ptk[pb], tk_xr[b]], w=[tk_yo[b]])
                        S.dma("pool", "yo%d" % b, dst[m * 128:(m + 1) * 128, i * T:(i + 1) * T], yo[b][:], r=[tk_yo[b]])
                S.flush()

        cur_, nxt_ = XA, XB
        for kind, li in cfg.layers:
            if kind == "attn":
                phase_attn(li, cur_, nxt_)
            elif kind == "conv":
                phase_conv(li, cur_, nxt_)
            else:
                phase_ffn(li, cur_, nxt_)
            cur_, nxt_ = nxt_, cur_

        with ExitStack() as ph:
            ps = psum_banks(ph)
            ptk = [Tok(True) for _ in range(8)]
            xTt = [sb(ph, "xTt%d" % i, [128, KC, T], F32) for i in range(2)]
            tT = [Tok() for _ in range(2)]
            ytok = [sb(ph, "ytok%d" % i, [128, D], F32) for i in range(2)]
            tky = [[Tok() for _ in range(4)] for _ in range(2)]
            yn = 0
            for i in range(NT):
                b = i % 2
                S.dma("sp", "xTt%d" % b, xTt[b][:], XT(cur_)[:, :, i * T:(i + 1) * T], w=[tT[b]])
                for s in range(4):
                    yb = yn % 2
                    yn += 1
                    for kq in range(4):
                        pb = (s * 4 + kq) % 4
                        for q in range(4):
                            k = kq * 4 + q
                            S.op("pe", (lambda pb=pb, q=q, k=k, s=s, b=b: lambda e: e.transpose(
                                out=ps[pb][:, q * 128:(q + 1) * 128], in_=xTt[b][:, k, s * 128:(s + 1) * 128], identity=ident[:]))(),
                                r=[tT[b], tk_const], w=[ptk[pb]])
                        if kq % 2 == 0:
                            S.op("act", (lambda pb=pb, kq=kq, yb=yb: lambda e: e.copy(out=ytok[yb][:, kq * 512:(kq + 1) * 512], in_=ps[pb][:, :]))(),
                                 r=[ptk[pb]], w=[tky[yb][kq]])
                        else:
                            S.op("dve", (lambda pb=pb, kq=kq, yb=yb: lambda e: e.tensor_copy(out=ytok[yb][:, kq * 512:(kq + 1) * 512], in_=ps[pb][:, :]))(),
                                 r=[ptk[pb]], w=[tky[yb][kq]])
                    S.dma("pool", "ytok%d" % yb, yout[i * T + s * 128:i * T + (s + 1) * 128, :], ytok[yb][:], r=tky[yb])
            S.flush(final=True)
    return nc


def rope_tables(pos):
    pos = np.asarray(pos)
    row = (pos // 64).astype(np.float32)
    col = (pos % 64).astype(np.float32)
    freqs = (np.float32(10000.0) ** (-np.arange(32, dtype=np.float32) * np.float32(2.0) / np.float32(64.0))).astype(np.float32)
    ar = (row[None, :] * freqs[:, None]).astype(np.float32)
    ac = (col[None, :] * freqs[:, None]).astype(np.float32)
    C = np.concatenate([np.cos(ar), np.cos(ar), np.cos(ac), np.cos(ac)], axis=0).astype(np.float32)
    Sn = np.concatenate([-np.sin(ar), np.sin(ar), -np.sin(ac), np.sin(ac)], axis=0).astype(np.float32)
    return np.ascontiguousarray(C), np.ascontiguousarray(Sn)


def perm_matrix():
    P = np.zeros((128, 128), np.float32)
    for d in range(128):
        blk, r = divmod(d, 64)
        src = blk * 64 + (r + 32) % 64
        P[src, d] = 1.0
    return P


def core_consts(cfg, nseq):
    NT, NTOK = cfg.NT, cfg.NTOK
    L = NTOK // nseq
    pos = np.arange(NTOK) % L
    C, Sn = rope_tables(pos)
    cm = np.zeros((128, 4 + 2 * NT), np.float32)
    for kh in range(2):
        for qh in range(2):
            cm[:, kh * 2 + qh] = 0.0 if (nseq == 1 or kh == qh) else NEG
    for i in range(NT):
        cm[:, 4 + 2 * i] = 0.0 if (i * T) % L == 0 else 1.0
        cm[:, 5 + 2 * i] = 0.0 if ((i + 1) * T) % L == 0 else 1.0
    return C, Sn, cm


def shared_inputs(cfg, p):
    pvoff, npv = pv_layout(cfg)
    pv = np.zeros((128, npv), np.float32)

    def put(name, arr):
        pv[:, pvoff[name]:pvoff[name] + arr.shape[1]] = arr
    put("eps", np.full((128, 1), EPS, np.float32))
    m = {}
    gn = np.zeros((1, max(1, cfg.n_attn) * 256), np.float32)
    for a in range(cfg.n_attn):
        put("an%d" % a, _pcols(p["attn_norm"][a]))
        put("qg%d" % a, _pcols(p["attn_q_norm"][a]))
        put("kg%d" % a, _pcols(p["attn_k_norm"][a]))
        gn[0, a * 256:a * 256 + 128] = p["attn_q_norm"][a]
        gn[0, a * 256 + 128:a * 256 + 256] = p["attn_k_norm"][a]
        m["wqkv%d" % a] = lay_qkv(np.asarray(p["attn_w_qkv"][a], np.float32))
        m["wo%d" % a] = lay_rowmajor(np.asarray(p["attn_w_o"][a], np.float32))
    for c in range(cfg.n_conv):
        put("cn%d" % c, _pcols(p["conv_norm"][c]))
        put("bpw1%d" % c, _pcols(p["conv_b_pw1"][c]))
        wdw = np.asarray(p["conv_w_dw"][c], np.float32)
        put("wdw%d" % c, np.concatenate([_pcols(wdw[t]) for t in range(CK)], axis=1))
        put("bdw%d" % c, _pcols(p["conv_b_dw"][c]))
        put("lng%d" % c, _pcols(p["conv_ln_g"][c]))
        put("lnb%d" % c, _pcols(p["conv_ln_b"][c]))
        put("bpw2%d" % c, _pcols(p["conv_b_pw2"][c]))
        m["wpw1%d" % c] = lay_pw1(np.asarray(p["conv_w_pw1"][c], np.float32))
        m["wpw2%d" % c] = lay_rowmajor(np.asarray(p["conv_w_pw2"][c], np.float32))
    for i in range(cfg.n_ffn):
        put("fn%d" % i, _pcols(p["ffn_norm"][i]))
        fw = np.asarray(p["ffn_w_dw"][i], np.float32)
        put("fw%d" % i, np.concatenate([_pcols(fw[t]) for t in range(3)], axis=1))
        put("fb%d" % i, _pcols(p["ffn_b_dw"][i]))
        m["wup%d" % i] = lay_up(np.asarray(p["ffn_w_up"][i], np.float32))
        m["wdn%d" % i] = lay_dn(np.asarray(p["ffn_w_down"][i], np.float32))
    m["pv"] = pv
    m["gn"] = gn
    m["ident"] = np.eye(128, dtype=np.float32)
    m["perm"] = perm_matrix()
    return m


def kernel(**inputs):
    cfg = Cfg()
    p = {k: np.asarray(v) for k, v in inputs.items()}
    xp = np.asarray(p["x_prompt"], np.float32)
    xs = np.asarray(p["x_sample"], np.float32)
    shared = shared_inputs(cfg, p)
    in_maps = []
    for core in range(8):
        if core < 4:
            x = xp[2 * core:2 * core + 2].reshape(cfg.NTOK, D)
            C, Sn, cm = core_consts(cfg, 2)
        else:
            x = xs[core - 4].reshape(cfg.NTOK, D)
            C, Sn, cm = core_consts(cfg, 1)
        mm = dict(shared)
        mm["xin"] = np.ascontiguousarray(x)
        mm["ropec"] = C
        mm["ropes"] = Sn
        mm["cm"] = cm
        in_maps.append(mm)
    nc = build_program(cfg)
    res = run_bass_kernel_spmd(nc, in_maps, core_ids=list(range(8)))
    outs = [np.asarray(r["yout"], np.float32) for r in res.results]
    y_prompt = np.stack([o.reshape(2, 4096, D) for o in outs[:4]], axis=0).reshape(8, 4096, D)
    y_sample = np.stack([o.reshape(1, 8192, D) for o in outs[4:]], axis=0).reshape(4, 8192, D)
    return (y_prompt, y_sample)
```
